# Optimizing a Trainium2 kernel written in Bass

```python
import math
import jax
import jax.numpy as jnp
from jax import lax
import numpy as np

D_MODEL = 2048
BATCH = 8
SEQ = 4096
DEPTH = 4

GRID_W = 64
CTX_LEN = 256
N_MIXERS = 3
EXPAND = 2
D_INNER = EXPAND * D_MODEL
EPS = 1e-6
CONV_WIDTH = 31
MLSTM_HEADS = 8
MLSTM_HEAD_DIM = D_INNER // MLSTM_HEADS
QKV_BLOCK = 4
MLSTM_CONV_WIDTH = 3
MLSTM_CHUNK = 64
HYENA_SHORT_WIDTH = 3
HYENA_EMB_DIM = 33
HYENA_FILTER_WIDTH = 64
HYENA_FAST_DECAY = 0.3
HYENA_SLOW_DECAY = 1.5
HYENA_DECAY_TARGET = 1e-2
N_A = (DEPTH + 2) // 3
N_B = (DEPTH + 1) // 3
N_C = DEPTH // 3

kernel_name = 'hybrid_conformer_mlstm_hyena_dit'


def _rmsnorm(x, g):
    xf = x.astype(jnp.float32)
    y = xf * lax.rsqrt(jnp.mean(xf * xf, axis=-1, keepdims=True) + EPS)
    return (y * g.astype(jnp.float32)).astype(x.dtype)


def _ada_rmsnorm(x, g, shift, scale):
    return _rmsnorm(x, g) * (1 + scale) + shift


def _layernorm(x, g, b):
    xf = x.astype(jnp.float32)
    mu = jnp.mean(xf, axis=-1, keepdims=True)
    var = jnp.mean(jnp.square(xf - mu), axis=-1, keepdims=True)
    y = (xf - mu) * lax.rsqrt(var + EPS) * g.astype(jnp.float32) + b.astype(jnp.float32)
    return y.astype(x.dtype)


def _dwconv(x, w, b):
    width = w.shape[0]
    pad = width // 2
    y = lax.conv_general_dilated(
        x, w[:, None, :].astype(x.dtype), window_strides=(1,), padding=[(pad, pad)],
        dimension_numbers=('NWC', 'WIO', 'NWC'), feature_group_count=x.shape[-1])
    return y + b.astype(x.dtype)


def _conv_module(u, w_in, dw_w, dw_b, ln_g, ln_b, w_out, rows):
    a, g, z = jnp.split(u @ w_in, 3, axis=-1)
    y = a * jax.nn.sigmoid(g)
    if rows is None:
        y = _dwconv(y, dw_w, dw_b)
    else:
        bsz, n, e = y.shape
        y = _dwconv(y.reshape(bsz * rows, GRID_W, e), dw_w, dw_b).reshape(bsz, n, e)
    y = jax.nn.silu(_layernorm(y, ln_g, ln_b))
    return (y * jax.nn.silu(z)) @ w_out


def _headwise(x, w):
    g, bi, bo = w.shape
    xs = x.reshape(x.shape[:-1] + (g, bi))
    return jnp.einsum('btgi,gio->btgo', xs, w).reshape(x.shape[:-1] + (g * bo,))


def _reverse_segments(a, n_first):
    return jnp.concatenate([jnp.flip(a[:, :, :n_first], 2), jnp.flip(a[:, :, n_first:], 2)], axis=2)


def _mlstm_scan(q, k, v, li, lf):
    bsz, nh, t, dk = q.shape
    dv = v.shape[-1]
    L = MLSTM_CHUNK
    nc = t // L
    q = q * (dk ** -0.5)
    tril = jnp.tril(jnp.ones((L, L), dtype=bool))

    def chunks(a):
        return jnp.moveaxis(a.reshape((bsz, nh, nc, L) + a.shape[3:]), 2, 0)

    def step(carry, inp):
        C, n, m = carry
        qc, kc, vc, lic, lfc = inp
        b = jnp.cumsum(lfc, axis=-1)
        log_d = b[..., :, None] - b[..., None, :] + lic[..., None, :]
        log_d = jnp.where(tril, log_d, -jnp.inf)
        log_inter = b + m[..., None]
        m_t = jnp.maximum(log_inter, jnp.max(log_d, axis=-1))
        d = jnp.exp(log_d - m_t[..., None])
        a = jnp.exp(log_inter - m_t)
        s = jnp.einsum('bhtd,bhsd->bhts', qc, kc) * d
        num = a[..., None] * jnp.einsum('bhtd,bhde->bhte', qc, C) + jnp.einsum('bhts,bhse->bhte', s, vc)
        den = a * jnp.einsum('bhtd,bhd->bht', qc, n) + jnp.sum(s, axis=-1)
        h = num / jnp.maximum(jnp.abs(den), jnp.exp(-m_t))[..., None]
        b_last = b[..., -1]
        log_w = b_last[..., None] - b + lic
        m_new = jnp.maximum(b_last + m, jnp.max(log_w, axis=-1))
        w = jnp.exp(log_w - m_new[..., None])
        decay = jnp.exp(b_last + m - m_new)
        kw = kc * w[..., None]
        C_new = decay[..., None, None] * C + jnp.einsum('bhsd,bhse->bhde', kw, vc)
        n_new = decay[..., None] * n + jnp.sum(kw, axis=2)
        return (C_new, n_new, m_new), h

    init = (jnp.zeros((bsz, nh, dk, dv), jnp.float32), jnp.zeros((bsz, nh, dk), jnp.float32),
            jnp.full((bsz, nh), -1e30, jnp.float32))
    _, h = lax.scan(step, init, (chunks(q), chunks(k), chunks(v), chunks(li), chunks(lf)))
    return jnp.moveaxis(h, 0, 2).reshape(bsz, nh, t, dv)


def _head_layernorm(h, g):
    mu = jnp.mean(h, axis=-1, keepdims=True)
    var = jnp.mean(jnp.square(h - mu), axis=-1, keepdims=True)
    return (h - mu) * lax.rsqrt(var + EPS) * g.astype(jnp.float32).reshape(h.shape[1], 1, h.shape[3])


def _mlstm_module(u_ctx, u_lat, w_in, conv_w, conv_b, w_q, w_k, w_v, w_o, b_o,
                  w_gates, b_gates, mh_g, skip, w_out, ctx_out):
    n_ctx = u_ctx.shape[1]

    def streams(u):
        xm, z = jnp.split(u @ w_in, 2, axis=-1)
        xc = jax.nn.silu(_dwconv(xm, conv_w, conv_b))
        return xm, xc, z

    xm_c, xc_c, z_c = streams(u_ctx)
    xm_l, xc_l, z_l = streams(u_lat)
    xm = jnp.concatenate([xm_c, xm_l], axis=1)
    xc = jnp.concatenate([xc_c, xc_l], axis=1)
    z = jnp.concatenate([z_c, z_l], axis=1)
    q = _headwise(xc, w_q)
    k = _headwise(xc, w_k)
    v = _headwise(xm, w_v)
    o = jax.nn.sigmoid(_headwise(xc, w_o) + b_o)
    bsz, t, e = q.shape
    wg = w_gates.reshape(3, e, -1)
    gates = (q @ wg[0] + k @ wg[1] + v @ wg[2] + b_gates).astype(jnp.float32)
    gates = gates.reshape(bsz, t, 2, 2, MLSTM_HEADS).transpose(2, 3, 0, 4, 1)

    def heads(a):
        return a.reshape(bsz, t, MLSTM_HEADS, -1).transpose(0, 2, 1, 3).astype(jnp.float32)

    qh, kh, vh = heads(q), heads(k), heads(v)

    def rev(a):
        return _reverse_segments(a, n_ctx)

    h_fwd = _mlstm_scan(qh, kh, vh, gates[0, 0], jax.nn.log_sigmoid(gates[0, 1]))
    h_bwd = rev(_mlstm_scan(rev(qh), rev(kh), rev(vh), rev(gates[1, 0]),
                            rev(jax.nn.log_sigmoid(gates[1, 1]))))
    h = _head_layernorm(h_fwd + h_bwd, mh_g).transpose(0, 2, 1, 3).reshape(bsz, t, e).astype(u_lat.dtype)
    y = (o * h + skip * xc) * jax.nn.silu(z)
    if ctx_out:
        y = y @ w_out
        return y[:, :n_ctx], y[:, n_ctx:]
    return None, y[:, n_ctx:] @ w_out


def _hyena_filters(L, f_w1, f_b1, f_freq1, f_w2, f_b2, f_freq2, f_w3, f_b3):
    f32 = jnp.float32
    t = jnp.linspace(0.0, 1.0, L, dtype=f32)[:, None]
    bands = (HYENA_EMB_DIM - 1) // 2
    ang = 2.0 * math.pi * jnp.arange(L, dtype=f32)[:, None] / L
    fr = jnp.linspace(1e-4, bands - 1, bands, dtype=f32)[None, :]
    feat = jnp.concatenate([t, jnp.cos(fr * ang), -jnp.sin(fr * ang)], axis=-1)
    h = jnp.sin(f_freq1.astype(f32) * (feat @ f_w1.astype(f32) + f_b1.astype(f32)))
    h = jnp.sin(f_freq2.astype(f32) * (h @ f_w2.astype(f32) + f_b2.astype(f32)))
    h = h @ f_w3.astype(f32) + f_b3.astype(f32)
    e = h.shape[-1] // 2
    lo = math.log(HYENA_DECAY_TARGET) / HYENA_FAST_DECAY
    hi = math.log(HYENA_DECAY_TARGET) / HYENA_SLOW_DECAY
    deltas = jnp.abs(jnp.linspace(lo, hi, e, dtype=f32))
    h = h * jnp.exp(-t * jnp.tile(deltas, 2)[None, :])
    return h[:, :e], h[:, e:]


def _bidir_long_conv(u, hf, hb, bias):
    n = u.shape[1]
    uf = u.astype(jnp.float32)
    k = jnp.concatenate([hf, jnp.zeros_like(hf[:1]), jnp.flip(hb[1:], axis=0)], axis=0)
    spec = jnp.fft.rfft(uf, n=2 * n, axis=1) * jnp.fft.rfft(k, axis=0)[None]
    y = jnp.fft.irfft(spec, n=2 * n, axis=1)[:, :n]
    return (y + uf * bias.astype(jnp.float32)).astype(u.dtype)


def _hyena_module(u, w_in, conv_w, conv_b, f_w1, f_b1, f_freq1, f_w2, f_b2, f_freq2,
                  f_w3, f_b3, h_bias, w_out):
    e = w_out.shape[0]
    p = u @ w_in
    g = _dwconv(p[..., :3 * e], conv_w, conv_b)
    x0, x1, v = jnp.split(g, 3, axis=-1)
    z = p[..., 3 * e:]
    hf, hb = _hyena_filters(u.shape[1], f_w1, f_b1, f_freq1, f_w2, f_b2, f_freq2, f_w3, f_b3)
    y = x0 * _bidir_long_conv(x1 * v, hf, hb, h_bias)
    return (y * jax.nn.silu(z)) @ w_out


def setup_inputs(seed: int = 0) -> dict:
    key = jax.random.key(seed)
    ks = iter(jax.random.split(key, 64))
    D, E, H = D_MODEL, D_INNER, MLSTM_HEADS
    F = HYENA_FILTER_WIDTH

    def nrm(shape, std):
        return std * jax.random.normal(next(ks), shape, jnp.float32)

    def gain(shape):
        return 1.0 + nrm(shape, 0.02)

    gate_bias = jnp.concatenate([jnp.zeros((H,), jnp.float32), jnp.linspace(3.0, 6.0, H, dtype=jnp.float32),
                                 jnp.zeros((H,), jnp.float32), jnp.linspace(3.0, 6.0, H, dtype=jnp.float32)])
    return {
        'x': nrm((BATCH, SEQ, D), 1.0),
        'c': nrm((BATCH, D), 1.0),
        'ctx': nrm((BATCH, CTX_LEN, D), 1.0),
        'c_ctx': nrm((D,), 1.0),
        'norm_g': gain((DEPTH, D)),
        'ada_w': nrm((DEPTH, D, 3 * D), 0.5 * D ** -0.5),
        'ada_b': nrm((DEPTH, 3 * D), 0.02),
        'final_g': gain((D,)),
        'cv_w_in': nrm((N_A, D, 3 * E), D ** -0.5),
        'cv_dw_w': nrm((N_A, CONV_WIDTH, E), CONV_WIDTH ** -0.5),
        'cv_dw_b': nrm((N_A, E), 0.02),
        'cv_ln_g': gain((N_A, E)),
        'cv_ln_b': nrm((N_A, E), 0.02),
        'cv_w_out': nrm((N_A, E, D), E ** -0.5),
        'ml_w_in': nrm((N_B, D, 2 * E), D ** -0.5),
        'ml_conv_w': nrm((N_B, MLSTM_CONV_WIDTH, E), MLSTM_CONV_WIDTH ** -0.5),
        'ml_conv_b': nrm((N_B, E), 0.02),
        'ml_w_q': nrm((N_B, E // QKV_BLOCK, QKV_BLOCK, QKV_BLOCK), QKV_BLOCK ** -0.5),
        'ml_w_k': nrm((N_B, E // QKV_BLOCK, QKV_BLOCK, QKV_BLOCK), QKV_BLOCK ** -0.5),
        'ml_w_v': nrm((N_B, E // QKV_BLOCK, QKV_BLOCK, QKV_BLOCK), QKV_BLOCK ** -0.5),
        'ml_w_o': nrm((N_B, E // QKV_BLOCK, QKV_BLOCK, QKV_BLOCK), QKV_BLOCK ** -0.5),
        'ml_b_o': nrm((N_B, E), 0.02),
        'ml_w_gates': nrm((N_B, 3 * E, 4 * H), 0.5 * (3 * E) ** -0.5),
        'ml_b_gates': gate_bias[None, :] + nrm((N_B, 4 * H), 0.1),
        'ml_mh_g': gain((N_B, E)),
        'ml_skip': gain((N_B, E)),
        'ml_w_out': nrm((N_B, E, D), E ** -0.5),
        'hy_w_in': nrm((N_C, D, 4 * E), D ** -0.5),
        'hy_conv_w': nrm((N_C, HYENA_SHORT_WIDTH, 3 * E), HYENA_SHORT_WIDTH ** -0.5),
        'hy_conv_b': nrm((N_C, 3 * E), 0.02),
        'hy_f_w1': nrm((N_C, HYENA_EMB_DIM, F), HYENA_EMB_DIM ** -0.5),
        'hy_f_b1': nrm((N_C, F), 0.1),
        'hy_f_freq1': 1.0 + nrm((N_C, F), 0.1),
        'hy_f_w2': nrm((N_C, F, F), F ** -0.5),
        'hy_f_b2': nrm((N_C, F), 0.1),
        'hy_f_freq2': 1.0 + nrm((N_C, F), 0.1),
        'hy_f_w3': nrm((N_C, F, 2 * E), 0.1 * F ** -0.5),
        'hy_f_b3': nrm((N_C, 2 * E), 0.01),
        'hy_h_bias': nrm((N_C, E), 0.5),
        'hy_w_out': nrm((N_C, E, D), E ** -0.5),
    }


def reference(x, c, ctx, c_ctx, norm_g, ada_w, ada_b, final_g,
              cv_w_in, cv_dw_w, cv_dw_b, cv_ln_g, cv_ln_b, cv_w_out,
              ml_w_in, ml_conv_w, ml_conv_b, ml_w_q, ml_w_k, ml_w_v, ml_w_o, ml_b_o,
              ml_w_gates, ml_b_gates, ml_mh_g, ml_skip, ml_w_out,
              hy_w_in, hy_conv_w, hy_conv_b, hy_f_w1, hy_f_b1, hy_f_freq1, hy_f_w2, hy_f_b2,
              hy_f_freq2, hy_f_w3, hy_f_b3, hy_h_bias, hy_w_out):
    rows = x.shape[1] // GRID_W
    readers = [i for i in range(DEPTH) if i % N_MIXERS == 1]
    last_reader = readers[-1] if readers else -1
    silu_c = jax.nn.silu(c)
    silu_cc = jax.nn.silu(c_ctx)
    h_lat, h_ctx = x, ctx
    for i in range(DEPTH):
        kind, j = i % N_MIXERS, i // N_MIXERS
        ctx_in = i <= last_reader
        ctx_out = i < last_reader
        sh, sc, gt = jnp.split(silu_c @ ada_w[i] + ada_b[i], 3, axis=-1)
        u_lat = _ada_rmsnorm(h_lat, norm_g[i], sh[:, None], sc[:, None])
        if ctx_in:
            sh_c, sc_c, gt_c = jnp.split(silu_cc @ ada_w[i] + ada_b[i], 3, axis=-1)
            u_ctx = _ada_rmsnorm(h_ctx, norm_g[i], sh_c, sc_c)
        if kind == 0:
            y_lat = _conv_module(u_lat, cv_w_in[j], cv_dw_w[j], cv_dw_b[j], cv_ln_g[j], cv_ln_b[j],
                                 cv_w_out[j], rows)
            if ctx_out:
                y_ctx = _conv_module(u_ctx, cv_w_in[j], cv_dw_w[j], cv_dw_b[j], cv_ln_g[j], cv_ln_b[j],
                                     cv_w_out[j], None)
        elif kind == 1:
            y_ctx, y_lat = _mlstm_module(u_ctx, u_lat, ml_w_in[j], ml_conv_w[j], ml_conv_b[j], ml_w_q[j],
                                         ml_w_k[j], ml_w_v[j], ml_w_o[j], ml_b_o[j], ml_w_gates[j],
                                         ml_b_gates[j], ml_mh_g[j], ml_skip[j], ml_w_out[j], ctx_out)
        else:
            hy = (hy_w_in[j], hy_conv_w[j], hy_conv_b[j], hy_f_w1[j], hy_f_b1[j], hy_f_freq1[j],
                  hy_f_w2[j], hy_f_b2[j], hy_f_freq2[j], hy_f_w3[j], hy_f_b3[j], hy_h_bias[j], hy_w_out[j])
            y_lat = _hyena_module(u_lat, *hy)
            if ctx_out:
                y_ctx = _hyena_module(u_ctx, *hy)
        h_lat = h_lat + gt[:, None] * y_lat
        if ctx_out:
            h_ctx = h_ctx + gt_c * y_ctx
    return _rmsnorm(h_lat, final_g)
```

```python
import contextlib
import math
import numpy as np
import ml_dtypes
import concourse.bass as bass
import concourse.mybir as mybir
from concourse.bass import AP
from concourse.bass_utils import run_bass_kernel_spmd

F32 = mybir.dt.float32
BF16 = mybir.dt.bfloat16
I32 = mybir.dt.int32
AF = mybir.ActivationFunctionType
ALU = mybir.AluOpType
AX = mybir.AxisListType

D = 2048
E = 4096
SEQ = 4096
NCTX = 256
EPS = 1e-6
KC = D // 128
EC = E // 128
CONVW = 31
NH = 8
DH = 512
LCH = 64
UT_CTX0 = 1
UT_LAT0 = 259
UT_COLS = 4356


class Buf:
    __slots__ = ("w", "r")

    def __init__(self):
        self.w = None
        self.r = {}


class Sched:
    def __init__(self, nc, es):
        self.nc = nc
        self.engs = {"pe": nc.tensor, "act": nc.scalar, "dve": nc.vector, "pool": nc.gpsimd, "sp": nc.sync}
        self.sem = {}
        self.cnt = {}
        for e in ("pe", "act", "dve", "pool"):
            self.sem[e] = es.enter_context(nc.semaphore("prog_" + e))
            self.cnt[e] = 0
        self.dsem = {"sp": [], "pool": [], "act": []}
        for q, n in (("sp", 28), ("pool", 20), ("act", 8)):
            for i in range(n):
                self.dsem[q].append([es.enter_context(nc.semaphore(f"d_{q}{i}")), 0, None])
        self.dnext = {"sp": 0, "pool": 0, "act": 0}
        self.seen = {e: {} for e in self.engs}
        self.semobj = {}

    def _wait(self, eng, deps):
        best = {}
        for tok in deps:
            if tok is None:
                continue
            sid, val = tok
            if best.get(sid, 0) < val:
                best[sid] = val
        seen = self.seen[eng]
        for sid, val in best.items():
            if seen.get(sid, 0) >= val:
                continue
            self.engs[eng].wait_ge(self.semobj[sid], val)
            seen[sid] = val

    def _deps(self, reads, writes):
        deps = []
        for b in reads:
            deps.append(b.w)
        for b in writes:
            deps.append(b.w)
            for sid, val in b.r.items():
                deps.append((sid, val))
        return deps

    def _commit(self, tok, reads, writes):
        sid, val = tok
        for b in reads:
            if b.r.get(sid, 0) < val:
                b.r[sid] = val
        for b in writes:
            b.w = tok
            b.r = {}

    def op(self, eng, fn, reads=(), writes=()):
        deps = self._deps(reads, writes)
        if eng == "pe":
            sid_own = id(self.sem["pe"])
            deps = [t for t in deps if t is not None and t[0] != sid_own]
        self._wait(eng, deps)
        e = self.engs[eng]
        fns = fn if isinstance(fn, (list, tuple)) else [fn]
        inst = None
        for f in fns:
            inst = f(e)
        self.cnt[eng] += 1
        sem = self.sem[eng]
        inst.then_inc(sem, 1)
        self.semobj[id(sem)] = sem
        tok = (id(sem), self.cnt[eng])
        self._commit(tok, reads, writes)
        return tok

    def dma(self, q, out, in_, reads=(), writes=(), **kw):
        slot = self.dsem[q][self.dnext[q]]
        self.dnext[q] = (self.dnext[q] + 1) % len(self.dsem[q])
        deps = self._deps(reads, writes)
        deps.append(slot[2])
        self._wait(q, deps)
        sem = slot[0]
        slot[1] += 16
        self.engs[q].dma_start(out=out, in_=in_, **kw).then_inc(sem, 16)
        self.semobj[id(sem)] = sem
        tok = (id(sem), slot[1])
        slot[2] = tok
        self._commit(tok, reads, writes)
        return tok

    def barrier(self):
        toks = [(id(self.sem[e]), self.cnt[e]) for e in self.sem if self.cnt[e] > 0]
        for q in self.dsem:
            for slot in self.dsem[q]:
                if slot[2] is not None:
                    toks.append(slot[2])
        for e in self.engs:
            self._wait(e, toks)

    def drain(self, eng, bufs):
        deps = []
        for b in bufs:
            deps.append(b.w)
        self._wait(eng, deps)


class T:
    def __init__(self, t, nb=1):
        self.t = t
        self.b = [Buf() for _ in range(nb)]

    def __getitem__(self, k):
        return self.t[k]

    @property
    def B(self):
        return self.b


class V:
    def __init__(self, t, n):
        self.t = t.t[:, 0:n]
        self.b = t.b

    def __getitem__(self, k):
        return self.t[k]

    @property
    def B(self):
        return self.b


def bcast_free(ap2d, n):
    a = ap2d.ap
    return AP(ap2d.tensor, ap2d.offset, [list(a[0]), list(a[1]), [0, n]])


class Prog:
    def __init__(self, layers=(0, 1, 2, 3), final=True):
        self.layers = layers
        self.final = final
        self.nc = bass.Bass("TRN2", target_bir_lowering=False)
        self.es = contextlib.ExitStack()

    def sb(self, name, shape, dt=F32, nb=1, st=None):
        self._uid = getattr(self, "_uid", 0) + 1
        return T((st or self.es).enter_context(self.nc.sbuf_tensor(f"{name}_{self._uid}", shape, dt)), nb)

    def dram_in(self, name, shape, dt=F32):
        t = self.nc.dram_tensor(name, list(shape), dt, kind="ExternalInput")
        self.in_names.append(name)
        return T(t.ap(), 1)

    def dram_scratch(self, name, shape, dt, nb=1):
        return T(self.nc.dram_tensor(name, list(shape), dt, kind="Internal").ap(), nb)

    def build(self):
        nc = self.nc
        self.in_names = []
        with self.es:
            self.S = Sched(nc, self.es)
            self._declare()
            self._consts()
            for li in self.layers:
                kind, j = li % 3, li // 3
                if kind == 0:
                    self.layer_conv(li, j)
                elif kind == 1:
                    self.layer_mlstm(li, j)
                else:
                    self.layer_hyena(li, j)
            if self.final:
                self.final_norm()
            else:
                self.copy_out()
            S = self.S
            S.drain("sp", self.out.B)
            S.drain("pool", self.out.B)
        return nc

    def _declare(self):
        di = self.dram_in
        self.x = di("x", [SEQ, D])
        self.ctx = di("ctx", [NCTX, D])
        self.c_pk = di("c_pk", [128, KC])
        self.cc_pk = di("cc_pk", [128, KC])
        self.norm_g = di("norm_g", [4, D])
        self.ada_w = di("ada_w", [4, D, 3 * D])
        self.ada_b = di("ada_b", [4, 3 * D])
        self.final_g = di("final_g", [D])
        if 0 in self.layers or 3 in self.layers:
            self.cv_w_in = di("cv_w_in", [2, D, 3 * E])
            self.cv_dw = di("cv_dw", [2, 128, EC, CONVW])
            self.cv_dwb = di("cv_dwb", [2, 128, EC])
            self.cv_lng = di("cv_lng", [2, 128, EC])
            self.cv_lnb = di("cv_lnb", [2, 128, EC])
            self.cv_w_out = di("cv_w_out", [2, E, D])
        self.ident_d = di("ident", [128, 128])
        if 1 in self.layers:
            self.ml_w_in = di("ml_w_in", [D, 2 * E])
            self.ml_w_out = di("ml_w_out", [E, D])
            self.ml_bd = di("ml_bd", [4 * EC * 128, 128])
            self.ml_wg = di("ml_wg", [128, 4 * 96 * 8])
            self.ml_cw = di("ml_cw", [128, EC, 3])
            self.ml_vec = di("ml_vec", [128, 3, EC])
            self.ml_bg = di("ml_bg", [8, 4])
            self.ml_mhg = di("ml_mhg", [E])
            self.ml_mask = di("ml_mask", [2, 64, 64])
        if 2 in self.layers:
            self.hy_w_in = di("hy_w_in", [D, 4 * E])
            self.hy_w_out = di("hy_w_out", [E, D])
            self.hy_cw = di("hy_cw", [128, 96, 3])
            self.hy_cb = di("hy_cb", [128, 96])
            self.hy_featT = di("hy_featT", [33, SEQ])
            self.hy_fvec = di("hy_fvec", [64, 4])
            self.hy_w1 = di("hy_w1", [33, 64])
            self.hy_w2 = di("hy_w2", [64, 64])
            self.hy_w3 = di("hy_w3", [64, 2 * E])
            self.hy_b3 = di("hy_b3", [2 * E])
            self.hy_hb = di("hy_hb", [E])
            self.hy_delta = di("hy_delta", [E])
            self.hy_ntl = di("hy_ntl", [128, 32])
            self.hy_C = di("hy_C", [SEQ, SEQ], BF16)
            self.hy_S = di("hy_S", [SEQ, SEQ], BF16)
            self.hy_CT = di("hy_CT", [SEQ, SEQ], BF16)
            self.hy_ST = di("hy_ST", [SEQ, SEQ], BF16)
        self.out = T(self.nc.dram_tensor("out", [SEQ, D], F32, kind="ExternalOutput").ap(), SEQ // 128)
        self.hctx = self.dram_scratch("hctx", [NCTX, D], F32, NCTX // 128)
        self.UT = self.dram_scratch("UT", [D, UT_COLS], BF16, 1)
        self.wb = {}

    def _consts(self):
        S, nc = self.S, self.nc
        self.ident_f = self.sb("ident_f", [128, 128], F32)
        self.ident = self.sb("ident_b", [128, 128], BF16)
        self.ones_f = self.sb("ones_f", [128, 128], F32)
        S.dma("sp", self.ident_f[:], self.ident_d[:], reads=self.ident_d.B, writes=self.ident_f.B)
        S.op("dve", lambda e: e.tensor_copy(out=self.ident[:], in_=self.ident_f[:]), reads=self.ident_f.B, writes=self.ident.B)
        S.op("dve", lambda e: e.memset(self.ones_f[:], 1.0), writes=self.ones_f.B)
        self.ps = [T(self.es.enter_context(nc.psum_tensor(f"ps{i}", [128, 512], F32))) for i in range(7)]
        self.psb = T(self.es.enter_context(nc.psum_tensor("psb", [128, 1024], BF16)))
        self.psi = 0
        self.csb = {}
        for nm, src in (("lat", self.c_pk), ("ctx", self.cc_pk)):
            cf = self.sb("cf_" + nm, [128, KC], F32)
            cs = self.sb("cs_" + nm, [128, KC], F32)
            S.dma("sp", cf[:], src[:], reads=src.B, writes=cf.B)
            S.op("act", lambda e: e.activation(out=cs[:], in_=cf[:], func=AF.Silu), reads=cf.B, writes=cs.B)
            self.csb[nm] = cs
        self.zcol = self.sb("zcol", [128, KC, 2], BF16)
        S.op("dve", lambda e: e.memset(self.zcol[:], 0.0), writes=self.zcol.B)
        utv = self.UT[:].rearrange("(k p) t -> p k t", p=128)
        for c0 in (0, 257, 4355):
            n = 2 if c0 == 257 else 1
            S.dma("sp", utv[:, :, c0:c0 + n], self.zcol[:, :, 0:n], reads=self.zcol.B, writes=self.UT.B,
                  allow_slow_non_contiguous=True)

    def alloc_commons(self, st, mods=True):
        self.G1 = self.sb("G1", [128, D], st=st)
        self.SH = self.sb("SH", [128, D], st=st)
        self.GT = self.sb("GT", [128, D], st=st)
        self.wpool = [self.sb(f"wp{i}", [128, 16 * 512], BF16, st=st) for i in range(4)]
        self.wpi = 0
        self.bb = [self.sb(f"bb{i}", [128, 512], st=st) for i in range(2)]
        self.tmpA = [self.sb(f"tmpA{i}", [128, 512], st=st) for i in range(2)]
        self.ht = [self.sb(f"ht{i}", [128, D], st=st) for i in range(2)]
        self.nt = self.sb("ntmp", [128, D], st=st)
        self.ngb = self.nt
        self.ub = self.sb("ub", [128, D], BF16, st=st)
        self.ss = self.sb("ss", [128, 4], st=st)
        self.uTs = [self.sb(f"uTs{i}", [128, KC, 128], BF16, st=st) for i in range(2)]

    def next_ps(self):
        p = self.ps[self.psi]
        self.psi = (self.psi + 1) % len(self.ps)
        return p

    def next_w(self):
        w = self.wpool[self.wpi]
        self.wpi = (self.wpi + 1) % len(self.wpool)
        return w

    def convert_weight(self, key, src_ap, rows, cols):
        if key in self.wb:
            return self.wb[key]
        nblk = rows // 128
        dst = self.dram_scratch("wb_" + key, [rows, cols], BF16, nblk)
        srcbuf = Buf()
        for r in range(nblk):
            self.S.dma("pool", dst[r * 128:(r + 1) * 128, :], src_ap[r * 128:(r + 1) * 128, :],
                       reads=[srcbuf], writes=[dst.b[r]])
        self.wb[key] = dst
        return dst

    def ada_mod(self, li, stream):
        S = self.S
        cs = self.csb[stream]
        cb = self.uTs[0]
        S.op("dve", lambda e: e.tensor_copy(out=cb[:], in_=bcast_free(cs[:], 128)), reads=cs.B, writes=cb.B)
        wv = self.ada_w[li].rearrange("(k p) n -> p k n", p=128)
        S.dma("sp", self.ngb[:], self.norm_g[li].partition_broadcast(128), reads=self.norm_g.B, writes=self.ngb.B)
        for n in range(12):
            w = self.next_w()
            wt = w[:].rearrange("p (k n) -> p k n", k=16)
            S.dma("pool", wt, wv[:, :, n * 512:(n + 1) * 512], reads=self.ada_w.B, writes=w.B)
            bb = self.bb[n % 2]
            S.dma("sp", bb[:], self.ada_b[li, n * 512:(n + 1) * 512].partition_broadcast(128),
                  reads=self.ada_b.B, writes=bb.B)
            ps = self.next_ps()
            S.op("pe", [(lambda e, k=k: e.matmul(ps[:], lhsT=cb[:, k, :], rhs=wt[:, k, :], start=(k == 0), stop=(k == 15)))
                        for k in range(16)], reads=cb.B + w.B, writes=ps.B)
            cols = slice((n % 4) * 512, (n % 4 + 1) * 512)
            if n < 4:
                S.op("dve", lambda e: e.tensor_tensor(out=self.SH[:, cols], in0=ps[:], in1=bb[:], op=ALU.add),
                     reads=ps.B + bb.B, writes=self.SH.B)
            elif n < 8:
                tmp = self.tmpA[n % 2]
                S.op("dve", lambda e: e.tensor_tensor(out=tmp[:], in0=ps[:], in1=bb[:], op=ALU.add),
                     reads=ps.B + bb.B, writes=tmp.B)
                S.op("dve", lambda e: e.scalar_tensor_tensor(out=self.G1[:, cols], in0=tmp[:], scalar=1.0, in1=self.ngb[:, cols],
                                                             op0=ALU.add, op1=ALU.mult),
                     reads=tmp.B + self.ngb.B, writes=self.G1.B)
            else:
                S.op("dve", lambda e: e.tensor_tensor(out=self.GT[:, cols], in0=ps[:], in1=bb[:], op=ALU.add),
                     reads=ps.B + bb.B, writes=self.GT.B)

    def hsrc(self, li, stream):
        if stream == "ctx":
            return self.ctx if (0 not in self.layers or li == 0) else self.hctx
        return self.x if li == self.layers[0] else self.out

    def pass1(self, li, stream):
        S = self.S
        src = self.hsrc(li, stream)
        ntile = (NCTX if stream == "ctx" else SEQ) // 128
        col0 = UT_CTX0 if stream == "ctx" else UT_LAT0
        utv = self.UT[:].rearrange("(k p) t -> p k t", p=128)
        for j in range(ntile):
            ht = self.ht[j % 2]
            sbuf = src.b[j] if len(src.b) > 1 else src.b[0]
            S.dma("sp", ht[:], src[j * 128:(j + 1) * 128, :], reads=[sbuf], writes=ht.B)
            self.norm_u(ht, self.G1, self.SH)
            uTs = self.uTs[j % 2]
            for g in range(4):
                S.op("pe", [(lambda e, q=q: e.transpose(self.psb[:, q * 128:(q + 1) * 128],
                                                        self.ub[:, (g * 4 + q) * 128:(g * 4 + q + 1) * 128], self.ident[:]))
                            for q in range(4)], reads=self.ub.B + self.ident.B, writes=self.psb.B)
                S.op("act", lambda e: e.copy(out=uTs[:, g * 4:(g + 1) * 4, :],
                                             in_=self.psb[:, 0:512].rearrange("p (q t) -> p q t", q=4)),
                     reads=self.psb.B, writes=uTs.B)
            S.dma("pool", utv[:, :, col0 + j * 128: col0 + (j + 1) * 128], uTs[:], reads=uTs.B, writes=self.UT.B)

    def norm_u(self, ht, G1, SH):
        S = self.S
        nt, ss = self.nt, self.ss
        S.op("dve", lambda e: e.tensor_tensor(out=nt[:], in0=ht[:], in1=ht[:], op=ALU.mult), reads=ht.B, writes=nt.B)
        S.op("dve", lambda e: e.reduce_sum(out=ss[:, 0:1], in_=nt[:], axis=AX.X), reads=nt.B, writes=ss.B)
        S.op("dve", lambda e: e.tensor_scalar(out=ss[:, 1:2], in0=ss[:, 0:1], scalar1=1.0 / D, scalar2=EPS,
                                              op0=ALU.mult, op1=ALU.add), reads=ss.B, writes=ss.B)
        S.op("act", lambda e: e.activation(out=ss[:, 2:3], in_=ss[:, 1:2], func=AF.Sqrt), reads=ss.B, writes=ss.B)
        S.op("dve", lambda e: e.reciprocal(out=ss[:, 3:4], in_=ss[:, 2:3]), reads=ss.B, writes=ss.B)
        S.op("dve", lambda e: e.scalar_tensor_tensor(out=nt[:], in0=ht[:], scalar=ss[:, 3:4], in1=G1[:],
                                                     op0=ALU.mult, op1=ALU.mult), reads=ht.B + ss.B + G1.B, writes=nt.B)
        S.op("dve", lambda e: e.tensor_tensor(out=self.ub[:], in0=nt[:], in1=SH[:], op=ALU.add),
             reads=nt.B + SH.B, writes=self.ub.B)

    def out_proj(self, li, stream, vT, wout, t0, ntok, dst_only_lat=True):
        S = self.S
        src = self.hsrc(li, stream)
        dst = self.hctx if stream == "ctx" else self.out
        nsub = ntok // 128
        hts = []
        for j in range(nsub):
            ht = self.ht[j % 2]
            jj = t0 // 128 + j
            sbuf = src.b[jj] if len(src.b) > 1 else src.b[0]
            S.dma("sp", ht[:], src[t0 + j * 128: t0 + (j + 1) * 128, :], reads=[sbuf], writes=ht.B)
            hts.append(ht)
        wv = wout[:].rearrange("(c p) d -> p c d", p=128)
        for dch in range(4):
            ws = []
            for half in range(2):
                w = self.next_w()
                wt = w[:].rearrange("p (k n) -> p k n", k=16)
                S.dma("sp", wt, wv[:, half * 16:(half + 1) * 16, dch * 512:(dch + 1) * 512],
                      reads=wout.b[half * 16:(half + 1) * 16], writes=w.B)
                ws.append((w, wt))
            for j in range(nsub):
                ps = self.next_ps()
                S.op("pe", [(lambda e, c=c: e.matmul(ps[:], lhsT=vT[:, c, j * 128:(j + 1) * 128], rhs=ws[c // 16][1][:, c % 16, :],
                                                     start=(c == 0), stop=(c == EC - 1))) for c in range(EC)],
                     reads=vT.B + ws[0][0].B + ws[1][0].B, writes=ps.B)
                tmp = self.tmpA[j % 2]
                cols = slice(dch * 512, (dch + 1) * 512)
                S.op("dve", lambda e: e.tensor_tensor(out=tmp[:], in0=ps[:], in1=self.GT[:, cols], op=ALU.mult),
                     reads=ps.B + self.GT.B, writes=tmp.B)
                S.op("dve", lambda e: e.tensor_tensor(out=hts[j][:, cols], in0=hts[j][:, cols], in1=tmp[:], op=ALU.add),
                     reads=tmp.B + hts[j].B, writes=hts[j].B)
        for j in range(nsub):
            jj = t0 // 128 + j
            S.dma("pool", dst[t0 + j * 128: t0 + (j + 1) * 128, :], hts[j][:], reads=hts[j].B, writes=[dst.b[jj]])

    def layer_conv(self, li, j):
        S = self.S
        TT = 256
        w_in = self.convert_weight(f"cv_in{j}", self.cv_w_in[j], D, 3 * E)
        w_out = self.convert_weight(f"cv_out{j}", self.cv_w_out[j], E, D)
        with contextlib.ExitStack() as les:
            self.alloc_commons(les)
            self.cv_uT = self.sb("cv_uT", [128, KC, TT], BF16, st=les)
            self.cv_c = self.sb("cv_c", [128, EC, TT], F32, nb=EC, st=les)
            self.cv_z = self.sb("cv_z", [128, EC, TT], BF16, nb=EC, st=les)
            self.cv_y = [self.sb(f"cv_y{i}", [128, TT], F32, st=les) for i in range(2)]
            self.cv_sig = [self.sb(f"cv_sig{i}", [128, TT], F32, st=les) for i in range(2)]
            self.cv_sq = [self.sb(f"cv_sq{i}", [128, TT], F32, st=les) for i in range(2)]
            self.cv_dwt = self.sb("cv_dwt", [128, EC, CONVW], st=les)
            self.cv_vec = self.sb("cv_vec", [128, 3, EC], st=les)
            self.cv_st = self.sb("cv_st", [128, 4, TT], st=les)
            self._layer_conv_body(li, j, w_in, w_out, TT)
            S.barrier()

    def _layer_conv_body(self, li, j, w_in, w_out, TT):
        S = self.S
        S.dma("sp", self.cv_dwt[:], self.cv_dw[j], reads=self.cv_dw.B, writes=self.cv_dwt.B)
        for q, src in enumerate((self.cv_dwb, self.cv_lng, self.cv_lnb)):
            S.dma("sp", self.cv_vec[:, q, :], src[j], reads=src.B, writes=self.cv_vec.B)
        streams = ["ctx", "lat"] if li == 0 else ["lat"]
        for stream in streams:
            self.ada_mod(li, stream)
            self.pass1(li, stream)
            ntok = NCTX if stream == "ctx" else SEQ
            rowlen = NCTX if stream == "ctx" else 64
            col0 = UT_CTX0 if stream == "ctx" else UT_LAT0
            for tt in range(ntok // TT):
                self.conv_tile(li, stream, w_in, w_out, tt * TT, TT, rowlen, col0)

    def conv_tile(self, li, stream, w_in, w_out, t0, TT, rowlen, col0):
        S = self.S
        uT = self.cv_uT
        utv = self.UT[:].rearrange("(k p) t -> p k t", p=128)
        S.dma("sp", uT[:], utv[:, :, col0 + t0: col0 + t0 + TT], reads=self.UT.B, writes=uT.B)
        wv = w_in[:].rearrange("(k p) n -> p k n", p=128)
        nrow = TT // rowlen
        psS, psQ = self.next_ps(), self.next_ps()
        for g in range(EC // 4):
            wts = []
            for part in range(3):
                w = self.next_w()
                wt = w[:].rearrange("p (k n) -> p k n", k=16)
                S.dma("sp", wt, wv[:, :, part * E + g * 512: part * E + (g + 1) * 512], reads=w_in.B, writes=w.B)
                wts.append((w, wt))
            for q in range(4):
                ec = g * 4 + q
                pss = []
                for part in range(3):
                    ps = self.next_ps()
                    while ps is psS or ps is psQ:
                        ps = self.next_ps()
                    wt = wts[part][1]
                    S.op("pe", [(lambda e, k=k, ps=ps, wt=wt: e.matmul(ps[:, 0:TT], lhsT=wt[:, k, q * 128:(q + 1) * 128], rhs=uT[:, k, :],
                                                                       start=(k == 0), stop=(k == KC - 1))) for k in range(KC)],
                         reads=uT.B + wts[part][0].B, writes=ps.B)
                    pss.append(ps)
                sig, y, sq = self.cv_sig[ec % 2], self.cv_y[ec % 2], self.cv_sq[ec % 2]
                S.op("act", lambda e: e.activation(out=sig[:], in_=pss[1][:, 0:TT], func=AF.Sigmoid), reads=pss[1].B, writes=sig.B)
                S.op("dve", lambda e: e.tensor_tensor(out=y[:], in0=pss[0][:, 0:TT], in1=sig[:], op=ALU.mult),
                     reads=pss[0].B + sig.B, writes=y.B)
                S.op("act", lambda e: e.activation(out=self.cv_z[:, ec, :], in_=pss[2][:, 0:TT], func=AF.Silu),
                     reads=pss[2].B, writes=[self.cv_z.b[ec]])
                cb = self.cv_c.b[ec]
                cflat = self.cv_c[:, ec, :]
                S.op("dve", lambda e: e.tensor_scalar(out=cflat, in0=y[:], scalar1=self.cv_dwt[:, ec, 15:16],
                                                      scalar2=self.cv_vec[:, 0, ec:ec + 1], op0=ALU.mult, op1=ALU.add),
                     reads=y.B + self.cv_dwt.B + self.cv_vec.B, writes=[cb])
                c3 = cflat.rearrange("p (r t) -> p r t", r=nrow)
                y3 = y[:].rearrange("p (r t) -> p r t", r=nrow)
                for tap in range(CONVW):
                    s = tap - 15
                    if s == 0 or abs(s) >= rowlen:
                        continue
                    lo, hi = max(0, -s), min(rowlen, rowlen - s)
                    S.op("dve", lambda e, s=s, lo=lo, hi=hi, tap=tap: e.scalar_tensor_tensor(
                        out=c3[:, :, lo:hi], in0=y3[:, :, lo + s:hi + s], scalar=self.cv_dwt[:, ec, tap:tap + 1],
                        in1=c3[:, :, lo:hi], op0=ALU.mult, op1=ALU.add), reads=y.B + [cb], writes=[cb])
                S.op("act", lambda e: e.activation(out=sq[:], in_=cflat, func=AF.Square), reads=[cb], writes=sq.B)
                S.op("pe", lambda e: e.matmul(psS[:, 0:TT], lhsT=self.ones_f[:], rhs=cflat, start=(ec == 0), stop=(ec == EC - 1)),
                     reads=[cb] + self.ones_f.B, writes=psS.B)
                S.op("pe", lambda e: e.matmul(psQ[:, 0:TT], lhsT=self.ones_f[:], rhs=sq[:], start=(ec == 0), stop=(ec == EC - 1)),
                     reads=sq.B + self.ones_f.B, writes=psQ.B)
        st = self.cv_st
        S.op("act", lambda e: e.mul(out=st[:, 0, :], in_=psS[:, 0:TT], mul=1.0 / E), reads=psS.B, writes=st.B)
        S.op("act", lambda e: e.mul(out=st[:, 1, :], in_=psQ[:, 0:TT], mul=1.0 / E), reads=psQ.B, writes=st.B)
        S.op("dve", lambda e: e.tensor_tensor(out=st[:, 2, :], in0=st[:, 0, :], in1=st[:, 0, :], op=ALU.mult), reads=st.B, writes=st.B)
        S.op("dve", lambda e: e.tensor_tensor(out=st[:, 1, :], in0=st[:, 1, :], in1=st[:, 2, :], op=ALU.subtract), reads=st.B, writes=st.B)
        S.op("dve", lambda e: e.tensor_scalar(out=st[:, 1, :], in0=st[:, 1, :], scalar1=EPS, scalar2=None, op0=ALU.add), reads=st.B, writes=st.B)
        S.op("act", lambda e: e.activation(out=st[:, 2, :], in_=st[:, 1, :], func=AF.Sqrt), reads=st.B, writes=st.B)
        S.op("dve", lambda e: e.reciprocal(out=st[:, 3, :], in_=st[:, 2, :]), reads=st.B, writes=st.B)
        for ec in range(EC):
            cb = self.cv_c.b[ec]
            cflat = self.cv_c[:, ec, :]
            zb = self.cv_z.b[ec]
            S.op("dve", lambda e: e.tensor_tensor(out=cflat, in0=cflat, in1=st[:, 0, :], op=ALU.subtract), reads=[cb] + st.B, writes=[cb])
            S.op("dve", lambda e: e.tensor_tensor(out=cflat, in0=cflat, in1=st[:, 3, :], op=ALU.mult), reads=[cb] + st.B, writes=[cb])
            S.op("act", lambda e: e.activation(out=cflat, in_=cflat, func=AF.Silu, scale=self.cv_vec[:, 1, ec:ec + 1],
                                               bias=self.cv_vec[:, 2, ec:ec + 1]), reads=[cb] + self.cv_vec.B, writes=[cb])
            S.op("dve", lambda e: e.tensor_tensor(out=self.cv_z[:, ec, :], in0=self.cv_z[:, ec, :], in1=cflat, op=ALU.mult),
                 reads=[cb, zb], writes=[zb])
        self.out_proj(li, stream, self.cv_z, w_out, t0, TT)

    def final_norm(self):
        S = self.S
        src = self.x if len(self.layers) == 0 else self.out
        with contextlib.ExitStack() as les:
            self.alloc_commons(les)
            self._final_body()
            S.barrier()

    def _final_body(self):
        S = self.S
        src = self.x if len(self.layers) == 0 else self.out
        gb = self.G1
        S.dma("sp", gb[:], self.final_g[:].partition_broadcast(128), reads=self.final_g.B, writes=gb.B)
        for j in range(SEQ // 128):
            ht = self.ht[j % 2]
            sbuf = src.b[j] if len(src.b) > 1 else src.b[0]
            S.dma("sp", ht[:], src[j * 128:(j + 1) * 128, :], reads=[sbuf], writes=ht.B)
            nt, ss = self.nt, self.ss
            S.op("dve", lambda e: e.tensor_tensor(out=nt[:], in0=ht[:], in1=ht[:], op=ALU.mult), reads=ht.B, writes=nt.B)
            S.op("dve", lambda e: e.reduce_sum(out=ss[:, 0:1], in_=nt[:], axis=AX.X), reads=nt.B, writes=ss.B)
            S.op("dve", lambda e: e.tensor_scalar(out=ss[:, 1:2], in0=ss[:, 0:1], scalar1=1.0 / D, scalar2=EPS,
                                                  op0=ALU.mult, op1=ALU.add), reads=ss.B, writes=ss.B)
            S.op("act", lambda e: e.activation(out=ss[:, 2:3], in_=ss[:, 1:2], func=AF.Sqrt), reads=ss.B, writes=ss.B)
            S.op("dve", lambda e: e.reciprocal(out=ss[:, 3:4], in_=ss[:, 2:3]), reads=ss.B, writes=ss.B)
            S.op("dve", lambda e: e.scalar_tensor_tensor(out=ht[:], in0=ht[:], scalar=ss[:, 3:4], in1=gb[:],
                                                         op0=ALU.mult, op1=ALU.mult), reads=ht.B + ss.B + gb.B, writes=ht.B)
            S.dma("pool", self.out[j * 128:(j + 1) * 128, :], ht[:], reads=ht.B, writes=[self.out.b[j]])

    def copy_out(self):
        pass


    def layer_mlstm(self, li, j):
        S = self.S
        TS = NCTX + SEQ
        NCH = TS // LCH
        w_in = self.convert_weight("ml_in", self.ml_w_in[:], D, 2 * E)
        w_out = self.convert_weight("ml_out", self.ml_w_out[:], E, D)
        bdb = self.convert_weight("ml_bd", self.ml_bd[:], 4 * EC * 128, 128)
        ds = self.dram_scratch
        qTd, kTd = ds("ml_qT", [E, TS], BF16), ds("ml_kT", [E, TS], BF16)
        ktmd, vtmd = ds("ml_ktm", [TS, E], BF16), ds("ml_vtm", [TS, E], BF16)
        P1d, P2d = ds("ml_P1", [E, TS], BF16), ds("ml_P2", [E, TS], BF16)
        Hd, HNd = ds("ml_H", [TS, E], F32), ds("ml_HN", [TS, E], BF16)
        GPd = ds("ml_GP", [4, 8, TS], F32)
        GQd = ds("ml_GQ", [2, 5, 8, TS], F32)
        DQd = ds("ml_DQ", [2, 8, NCH], F32)
        S.barrier()
        with contextlib.ExitStack() as les:
            self.alloc_commons(les)
            for stream in ("ctx", "lat"):
                self.ada_mod(li, stream)
                self.pass1(li, stream)
            if getattr(self, "ml_stop", 9) > 0.5:
                self.ml_proj(les, w_in, bdb, qTd, kTd, ktmd, vtmd, P1d, P2d, GPd)
            S.barrier()
        stop = getattr(self, "ml_stop", 9)
        if stop <= 1:
            return
        self.ml_gates(GPd, GQd, DQd)
        S.barrier()
        if stop <= 2:
            return
        self.ml_scan(qTd, kTd, ktmd, vtmd, Hd, HNd, GQd, DQd)
        S.barrier()
        if stop <= 3:
            return
        with contextlib.ExitStack() as les:
            self.alloc_commons(les)
            self.ada_mod(li, "lat")
            sb = lambda n, sh, dt=F32: self.sb(n, sh, dt, st=les)
            vT, p1, p2 = sb("mo_vT", [128, EC, 256], BF16), sb("mo_p1", [128, EC, 256], BF16), sb("mo_p2", [128, EC, 256], BF16)
            hn = [sb(f"mo_hn{i}", [128, E], BF16) for i in range(2)]
            for tt in range(SEQ // 256):
                c0 = NCTX + tt * 256
                S.dma("sp", p1[:], P1d[:, c0:c0 + 256].rearrange("(c p) t -> p c t", p=128), reads=P1d.B, writes=p1.B)
                S.dma("sp", p2[:], P2d[:, c0:c0 + 256].rearrange("(c p) t -> p c t", p=128), reads=P2d.B, writes=p2.B)
                for sub in range(2):
                    y = hn[sub]
                    S.dma("sp", y[:], HNd[c0 + sub * 128:c0 + (sub + 1) * 128, :], reads=HNd.B, writes=y.B)
                    for g in range(EC // 4):
                        S.op("pe", [(lambda e, q=q: e.transpose(self.psb[:, q * 128:(q + 1) * 128],
                                                                y[:, (g * 4 + q) * 128:(g * 4 + q + 1) * 128], self.ident[:]))
                                    for q in range(4)], reads=y.B + self.ident.B, writes=self.psb.B)
                        S.op("act", lambda e: e.copy(out=vT[:, g * 4:(g + 1) * 4, sub * 128:(sub + 1) * 128],
                                                     in_=self.psb[:, 0:512].rearrange("p (q t) -> p q t", q=4)),
                             reads=self.psb.B, writes=vT.B)
                S.op("dve", lambda e: e.tensor_tensor(out=vT[:], in0=vT[:], in1=p1[:], op=ALU.mult), reads=vT.B + p1.B, writes=vT.B)
                S.op("dve", lambda e: e.tensor_tensor(out=vT[:], in0=vT[:], in1=p2[:], op=ALU.add), reads=vT.B + p2.B, writes=vT.B)
                self.out_proj(li, "lat", vT, w_out, tt * 256, 256)
            S.barrier()

    def ml_proj(self, les, w_in, bdb, qTd, kTd, ktmd, vtmd, P1d, P2d, GPd):
        S = self.S
        TT = 256
        sb = lambda n, sh, dt=F32: self.sb(n, sh, dt, st=les)
        uT = sb("mp_uT", [128, KC, TT + 2], BF16)
        ktm, vtm = sb("mp_ktm", [128, 2, E], BF16), sb("mp_vtm", [128, 2, E], BF16)
        wgf, wg = sb("mp_wgf", [128, 4 * 96 * 8]), sb("mp_wg", [128, 4, 96, 8], BF16)
        bd = [sb(f"mp_bd{i}", [128, 4, 128], BF16) for i in range(2)]
        cw, vec = sb("mp_cw", [128, EC, 3]), sb("mp_vec", [128, 3, EC])
        bg = sb("mp_bg", [8, 8])
        a_, xc, zs, og = sb("mp_a", [128, TT]), sb("mp_xc", [128, TT]), sb("mp_zs", [128, TT]), sb("mp_o", [128, TT])
        xmb, xcb = sb("mp_xmb", [128, TT], BF16), sb("mp_xcb", [128, TT], BF16)
        qb, kb, vb = sb("mp_qb", [128, TT], BF16), sb("mp_kb", [128, TT], BF16), sb("mp_vb", [128, TT], BF16)
        p1, p2 = sb("mp_p1", [128, TT], BF16), sb("mp_p2", [128, TT], BF16)
        gst = sb("mp_gst", [8, 4, TT])
        S.dma("sp", wgf[:], self.ml_wg[:], reads=self.ml_wg.B, writes=wgf.B)
        S.op("dve", lambda e: e.tensor_copy(out=wg[:].rearrange("p a b c -> p (a b c)"), in_=wgf[:]), reads=wgf.B, writes=wg.B)
        S.dma("sp", cw[:], self.ml_cw[:], reads=self.ml_cw.B, writes=cw.B)
        S.dma("sp", vec[:], self.ml_vec[:], reads=self.ml_vec.B, writes=vec.B)
        S.dma("sp", bg[:, 0:4], self.ml_bg[:], reads=self.ml_bg.B, writes=bg.B)
        utv = self.UT[:].rearrange("(k p) t -> p k t", p=128)
        wv = w_in[:].rearrange("(k p) n -> p k n", p=128)
        bdv = bdb[:].rearrange("(a c p) m -> c p a m", a=4, c=EC)
        psGa, psGb = self.next_ps(), self.next_ps()
        held = (psGa, psGb)

        def nps():
            p = self.next_ps()
            while p in held:
                p = self.next_ps()
            return p

        for tt in range(1 + SEQ // TT):
            uc0 = UT_CTX0 - 1 if tt == 0 else UT_LAT0 + (tt - 1) * TT - 1
            tok0 = 0 if tt == 0 else NCTX + (tt - 1) * TT
            S.dma("sp", uT[:], utv[:, :, uc0:uc0 + TT + 2], reads=self.UT.B, writes=uT.B)
            for g in range(EC // 4):
                wts = []
                for part in range(2):
                    w = self.next_w()
                    wt = w[:].rearrange("p (k n) -> p k n", k=16)
                    S.dma("sp", wt, wv[:, :, part * E + g * 512: part * E + (g + 1) * 512], reads=w_in.B, writes=w.B)
                    wts.append((w, wt))
                for q in range(4):
                    ec = g * 4 + q
                    b_ = bd[ec % 2]
                    if getattr(self, "ml_stop", 9) > 0.62:
                        S.dma("sp", b_[:], bdv[ec], reads=bdb.B, writes=b_.B)
                    pss = []
                    for part in range(2):
                        ps = nps()
                        wt = wts[part][1]
                        S.op("pe", [(lambda e, k=k, ps=ps, wt=wt: e.matmul(ps[:, 0:TT + 2], lhsT=wt[:, k, q * 128:(q + 1) * 128], rhs=uT[:, k, :],
                                                                           start=(k == 0), stop=(k == KC - 1))) for k in range(KC)],
                             reads=uT.B + wts[part][0].B, writes=ps.B)
                        pss.append(ps)
                    pxm, pz = pss
                    if getattr(self, "ml_stop", 9) < 0.606:
                        continue
                    lv = getattr(self, "ml_stop", 9)
                    S.op("dve", lambda e: e.tensor_copy(out=xmb[:], in_=pxm[:, 1:TT + 1]), reads=pxm.B, writes=xmb.B)
                    if lv < 0.6075:
                        continue
                    S.op("dve", lambda e: e.tensor_scalar(out=a_[:], in0=pxm[:, 1:TT + 1], scalar1=cw[:, ec, 1:2], scalar2=vec[:, 0, ec:ec + 1],
                                                          op0=ALU.mult, op1=ALU.add), reads=pxm.B + cw.B + vec.B, writes=a_.B)
                    S.op("dve", lambda e: e.scalar_tensor_tensor(out=a_[:], in0=pxm[:, 0:TT], scalar=cw[:, ec, 0:1], in1=a_[:],
                                                                 op0=ALU.mult, op1=ALU.add), reads=pxm.B + cw.B + a_.B, writes=a_.B)
                    S.op("dve", lambda e: e.scalar_tensor_tensor(out=a_[:], in0=pxm[:, 2:TT + 2], scalar=cw[:, ec, 2:3], in1=a_[:],
                                                                 op0=ALU.mult, op1=ALU.add), reads=pxm.B + cw.B + a_.B, writes=a_.B)
                    if lv < 0.6085:
                        continue
                    S.op("act", lambda e: e.activation(out=xc[:], in_=a_[:], func=AF.Silu), reads=a_.B, writes=xc.B)
                    S.op("act", lambda e: e.mul(out=xcb[:], in_=xc[:], mul=1.0), reads=xc.B, writes=xcb.B)
                    if lv < 0.6095:
                        continue
                    S.op("act", lambda e: e.activation(out=zs[:], in_=pz[:, 1:TT + 1], func=AF.Silu), reads=pz.B, writes=zs.B)
                    lvl = getattr(self, "ml_stop", 9)
                    if lvl < 0.65:
                        continue
                    pq, pk, pv, po = nps(), nps(), nps(), nps()
                    for idx, (pp, src) in enumerate(((pq, xcb), (pk, xcb), (pv, xmb), (po, xcb))):
                        S.op("pe", lambda e: e.matmul(pp[:, 0:TT], lhsT=b_[:, idx, :], rhs=src[:], start=True, stop=True),
                             reads=b_.B + src.B, writes=pp.B)
                    S.op("act", lambda e: e.mul(out=qb[:], in_=pq[:, 0:TT], mul=1.0), reads=pq.B, writes=qb.B)
                    S.op("dve", lambda e: e.tensor_copy(out=kb[:], in_=pk[:, 0:TT]), reads=pk.B, writes=kb.B)
                    S.op("act", lambda e: e.mul(out=vb[:], in_=pv[:, 0:TT], mul=1.0), reads=pv.B, writes=vb.B)
                    S.op("act", lambda e: e.activation(out=og[:], in_=po[:, 0:TT], func=AF.Sigmoid, bias=vec[:, 1, ec:ec + 1]),
                         reads=po.B + vec.B, writes=og.B)
                    S.op("dve", lambda e: e.tensor_tensor(out=p1[:], in0=og[:], in1=zs[:], op=ALU.mult), reads=og.B + zs.B, writes=p1.B)
                    S.op("dve", lambda e: e.scalar_tensor_tensor(out=p2[:], in0=xc[:], scalar=vec[:, 2, ec:ec + 1], in1=zs[:],
                                                                 op0=ALU.mult, op1=ALU.mult), reads=xc.B + vec.B + zs.B, writes=p2.B)
                    for grp in range(4 if lvl > 0.85 else 0):
                        pg = psGa if grp < 2 else psGb
                        cs_ = slice((grp % 2) * TT, (grp % 2 + 1) * TT)
                        S.op("pe", [(lambda e, i3=i3, src=src: e.matmul(pg[0:8, cs_], lhsT=wg[:, grp, i3 * 32 + ec, :], rhs=src[:],
                                                                       start=(ec == 0 and i3 == 0), stop=(ec == EC - 1 and i3 == 2)))
                                    for i3, src in enumerate((qb, kb, vb))], reads=wg.B + qb.B + kb.B + vb.B, writes=pg.B)
                    if lvl < 0.75:
                        continue
                    rows = slice(ec * 128, (ec + 1) * 128)
                    S.dma("pool", qTd[rows, tok0:tok0 + TT], qb[:], reads=qb.B, writes=qTd.B)
                    S.dma("pool", kTd[rows, tok0:tok0 + TT], kb[:], reads=kb.B, writes=kTd.B)
                    S.dma("pool", P1d[rows, tok0:tok0 + TT], p1[:], reads=p1.B, writes=P1d.B)
                    S.dma("pool", P2d[rows, tok0:tok0 + TT], p2[:], reads=p2.B, writes=P2d.B)
                    S.op("pe", [(lambda e, q2=q2: e.transpose(self.psb[:, q2 * 128:(q2 + 1) * 128],
                                                              (kb if q2 < 2 else vb)[:, (q2 % 2) * 128:(q2 % 2 + 1) * 128], self.ident[:]))
                                for q2 in range(4)], reads=kb.B + vb.B + self.ident.B, writes=self.psb.B)
                    S.op("act", lambda e: e.copy(out=ktm[:, :, ec * 128:(ec + 1) * 128],
                                                 in_=self.psb[:, 0:256].rearrange("p (s t) -> p s t", s=2)), reads=self.psb.B, writes=ktm.B)
                    S.op("act", lambda e: e.copy(out=vtm[:, :, ec * 128:(ec + 1) * 128],
                                                 in_=self.psb[:, 256:512].rearrange("p (s t) -> p s t", s=2)), reads=self.psb.B, writes=vtm.B)
            if getattr(self, "ml_stop", 9) < 0.75:
                continue
            for sub in range(2):
                r0 = tok0 + sub * 128
                S.dma("pool", ktmd[r0:r0 + 128, :], ktm[:, sub, :], reads=ktm.B, writes=ktmd.B)
                S.dma("pool", vtmd[r0:r0 + 128, :], vtm[:, sub, :], reads=vtm.B, writes=vtmd.B)
            if getattr(self, "ml_stop", 9) < 0.85:
                continue
            for grp in range(4):
                pg = psGa if grp < 2 else psGb
                cs_ = slice((grp % 2) * TT, (grp % 2 + 1) * TT)
                S.op("dve", lambda e: e.tensor_scalar(out=gst[:, grp, :], in0=pg[0:8, cs_], scalar1=bg[:, grp:grp + 1], scalar2=None, op0=ALU.add),
                     reads=pg.B + bg.B, writes=gst.B)
            S.dma("pool", GPd[:, :, tok0:tok0 + TT].rearrange("g h t -> h g t"), gst[:], reads=gst.B, writes=GPd.B)

    def ml_gates(self, GPd, GQd, DQd):
        S = self.S
        TS = NCTX + SEQ
        NCH = TS // LCH
        LNS = math.log(DH ** -0.5)
        with contextlib.ExitStack() as st:
            sb = lambda n, sh, dt=F32: self.sb(n, sh, dt, st=st)
            li, fp, t0, t1 = sb("mg_li", [8, TS]), sb("mg_fp", [8, TS]), sb("mg_t0", [8, TS]), sb("mg_t1", [8, TS])
            ones, Bc, Gg, Mx = sb("mg_one", [8, TS]), sb("mg_B", [8, TS]), sb("mg_Gg", [8, TS]), sb("mg_Mx", [8, TS])
            res = sb("mg_res", [8, TS])
            mp_, me_, dc_ = sb("mg_mp", [8, 72]), sb("mg_me", [8, 72]), sb("mg_dc", [8, 72])
            mp, me, dc = V(mp_, NCH), V(me_, NCH), V(dc_, NCH)
            S.op("dve", lambda e: e.memset(ones[:], 1.0), writes=ones.B)
            segs = ((0, NCTX), (NCTX, SEQ))

            def rev(dst, src):
                for (o, n) in segs:
                    a = src[:, o:o + n]
                    r = AP(a.tensor, a.offset + n - 1, [list(a.ap[0]), [-1, n]])
                    S.op("dve", lambda e: e.tensor_copy(out=dst[:, o:o + n], in_=r), reads=src.B, writes=dst.B)

            def c3(t):
                return t[:].rearrange("p (c l) -> p c l", l=LCH)

            def bc3(t):
                a = t[:]
                return AP(a.tensor, a.offset, [list(a.ap[0]), list(a.ap[1]), [0, LCH]])

            for d in range(2):
                S.dma("sp", t0[:], GPd[2 * d], reads=GPd.B, writes=t0.B)
                S.dma("sp", t1[:], GPd[2 * d + 1], reads=GPd.B, writes=t1.B)
                if d == 0:
                    S.op("dve", lambda e: e.tensor_copy(out=li[:], in_=t0[:]), reads=t0.B, writes=li.B)
                    S.op("dve", lambda e: e.tensor_copy(out=fp[:], in_=t1[:]), reads=t1.B, writes=fp.B)
                else:
                    rev(li, t0)
                    rev(fp, t1)
                S.op("act", lambda e: e.activation(out=t0[:], in_=fp[:], func=AF.Exp, scale=-1.0), reads=fp.B, writes=t0.B)
                S.op("act", lambda e: e.activation(out=t1[:], in_=t0[:], func=AF.Ln, bias=1.0), reads=t0.B, writes=t1.B)
                S.op("dve", lambda e: e.tensor_scalar(out=t1[:], in0=t1[:], scalar1=-1.0, scalar2=None, op0=ALU.mult), reads=t1.B, writes=t1.B)
                S.op("dve", lambda e: e.tensor_tensor_scan(out=Bc[:], data0=ones[:], data1=t1[:], initial=0.0, op0=ALU.mult, op1=ALU.add),
                     reads=ones.B + t1.B, writes=Bc.B)
                S.op("dve", lambda e: e.tensor_tensor(out=Gg[:], in0=li[:], in1=Bc[:], op=ALU.subtract), reads=li.B + Bc.B, writes=Gg.B)
                S.op("dve", lambda e: e.tensor_tensor_scan(out=Mx[:], data0=Gg[:], data1=Gg[:], initial=-1e30, op0=ALU.max, op1=ALU.max),
                     reads=Gg.B, writes=Mx.B)
                S.op("dve", lambda e: e.tensor_copy(out=me[:], in_=c3(Mx)[:, :, LCH - 1]), reads=Mx.B, writes=me.B)
                S.op("dve", lambda e: e.memset(mp[:, 0:1], -1e30), writes=mp.B)
                S.op("dve", lambda e: e.tensor_copy(out=mp[:, 1:NCH], in_=me[:, 0:NCH - 1]), reads=me.B, writes=mp.B)
                S.op("dve", lambda e: e.tensor_tensor(out=dc[:], in0=mp[:], in1=me[:], op=ALU.subtract), reads=mp.B + me.B, writes=dc.B)
                S.op("act", lambda e: e.activation(out=dc[:], in_=dc[:], func=AF.Exp), reads=dc.B, writes=dc.B)
                S.dma("pool", DQd[d], dc[:], reads=dc.B, writes=DQd.B)

                def emit(qi, src):
                    if d == 0:
                        S.dma("pool", GQd[d, qi], src[:], reads=src.B, writes=GQd.B)
                    else:
                        rev(res, src)
                        S.dma("pool", GQd[d, qi], res[:], reads=res.B, writes=GQd.B)

                S.op("dve", lambda e: e.tensor_scalar(out=t0[:], in0=Gg[:], scalar1=LNS, scalar2=None, op0=ALU.add), reads=Gg.B, writes=t0.B)
                emit(0, t0)
                emit(1, Mx)
                S.op("dve", lambda e: e.tensor_tensor(out=c3(t0), in0=bc3(mp), in1=c3(Mx), op=ALU.subtract), reads=mp.B + Mx.B, writes=t0.B)
                S.op("act", lambda e: e.activation(out=t0[:], in_=t0[:], func=AF.Exp, bias=LNS), reads=t0.B, writes=t0.B)
                emit(2, t0)
                S.op("dve", lambda e: e.tensor_tensor(out=t1[:], in0=Bc[:], in1=Mx[:], op=ALU.add), reads=Bc.B + Mx.B, writes=t1.B)
                S.op("act", lambda e: e.activation(out=t1[:], in_=t1[:], func=AF.Exp, scale=-1.0), reads=t1.B, writes=t1.B)
                emit(3, t1)
                S.op("dve", lambda e: e.tensor_tensor(out=c3(t0), in0=c3(Gg), in1=bc3(me), op=ALU.subtract), reads=me.B + Gg.B, writes=t0.B)
                S.op("act", lambda e: e.activation(out=t0[:], in_=t0[:], func=AF.Exp), reads=t0.B, writes=t0.B)
                emit(4, t0)
            S.barrier()

    def ml_scan(self, qTd, kTd, ktmd, vtmd, Hd, HNd, GQd, DQd):
        S = self.S
        TS = NCTX + SEQ
        NCH = TS // LCH
        NG = NCH // 4
        with contextlib.ExitStack() as st:
            sb = lambda n, sh, dt=F32: self.sb(n, sh, dt, st=st)
            C, Cb = sb("ms_C", [128, 4, DH]), sb("ms_Cb", [128, 4, DH], BF16)
            nn_, nb_ = sb("ms_n", [128, 8]), sb("ms_nb", [128, 16], BF16)
            nn, nb = V(nn_, 4), V(nb_, 4)
            cols = [V(sb(f"ms_col{i}", [64, 72]), NCH) for i in range(4)]
            mxr, dcr = sb("ms_mxr", [64, TS]), V(sb("ms_dcr", [128, 72]), NCH)
            mask = [sb(f"ms_mask{i}", [64, 64]) for i in range(2)]
            onesb = V(sb("ms_1b", [64, 16], BF16), 1)
            mhg = sb("ms_mhg", [64, DH])
            qT = [sb(f"ms_qT{i}", [128, 4, 256], BF16) for i in range(2)]
            kT = [sb(f"ms_kT{i}", [128, 4, 256], BF16) for i in range(2)]
            ktm = [sb(f"ms_ktm{i}", [64, 4, DH], BF16) for i in range(2)]
            vtm = [sb(f"ms_vtm{i}", [64, 4, DH], BF16) for i in range(2)]
            DT, DTm, STb = sb("ms_DT", [64, 64]), sb("ms_DTm", [64, 64]), sb("ms_STb", [64, 64], BF16)
            t1, num, hc, hf, sq = sb("ms_t1", [64, DH]), sb("ms_num", [64, DH]), sb("ms_hc", [64, DH]), sb("ms_hf", [64, DH]), sb("ms_sq", [64, DH])
            hnb = sb("ms_hnb", [64, DH], BF16)
            kw = sb("ms_kw", [64, DH], BF16)
            sm = sb("ms_sm", [64, 16])
            S.dma("sp", mask[0][:], self.ml_mask[0], reads=self.ml_mask.B, writes=mask[0].B)
            S.dma("sp", mask[1][:], self.ml_mask[1], reads=self.ml_mask.B, writes=mask[1].B)
            S.op("dve", lambda e: e.memset(onesb[:], 1.0), writes=onesb.B)
            for d in range(2):
                for h in range(NH):
                    for qi in range(4):
                        src = GQd[d, (0, 2, 3, 4)[qi], h].rearrange("(c l) -> l c", l=LCH)
                        S.dma("sp", cols[qi][:], src, reads=GQd.B, writes=cols[qi].B, allow_slow_non_contiguous=True)
                    S.dma("sp", mxr[:], GQd[d, 1, h].partition_broadcast(64), reads=GQd.B, writes=mxr.B)
                    S.dma("sp", dcr[:], DQd[d, h].partition_broadcast(128), reads=DQd.B, writes=dcr.B)
                    if d == 1:
                        S.dma("sp", mhg[:], self.ml_mhg[h * DH:(h + 1) * DH].partition_broadcast(64), reads=self.ml_mhg.B, writes=mhg.B)
                    S.op("dve", lambda e: e.memset(C[:], 0.0), writes=C.B)
                    S.op("dve", lambda e: e.memset(nn[:], 0.0), writes=nn.B)
                    S.op("act", lambda e: e.mul(out=Cb[:], in_=C[:], mul=1.0), reads=C.B, writes=Cb.B)
                    S.op("act", lambda e: e.mul(out=nb[:], in_=nn[:], mul=1.0), reads=nn.B, writes=nb.B)
                    hrows = slice(h * DH, (h + 1) * DH)
                    gorder = list(range(NG)) if d == 0 else [0] + list(range(NG - 1, 0, -1))
                    cp = 0
                    for gi_n, grp in enumerate(gorder):
                        tg0 = grp * 256
                        b_ = gi_n % 2
                        S.dma("sp", qT[b_][:], qTd[hrows, tg0:tg0 + 256].rearrange("(c p) t -> p c t", p=128), reads=qTd.B, writes=qT[b_].B)
                        S.dma("sp", kT[b_][:], kTd[hrows, tg0:tg0 + 256].rearrange("(c p) t -> p c t", p=128), reads=kTd.B, writes=kT[b_].B)
                        S.dma("sp", ktm[b_][:], ktmd[tg0:tg0 + 256, hrows].rearrange("(c l) e -> l c e", l=LCH), reads=ktmd.B, writes=ktm[b_].B)
                        S.dma("sp", vtm[b_][:], vtmd[tg0:tg0 + 256, hrows].rearrange("(c l) e -> l c e", l=LCH), reads=vtmd.B, writes=vtm[b_].B)
                        for ci in (range(4) if d == 0 else range(3, -1, -1)):
                            c = grp * 4 + ci
                            tok0 = c * LCH
                            lc = slice(ci * LCH, (ci + 1) * LCH)
                            ktc, vtc = ktm[b_][:, ci, :], vtm[b_][:, ci, :]
                            pST, pP1, pP2, pDN = self.next_ps(), self.next_ps(), self.next_ps(), self.next_ps()
                            S.op("pe", [(lambda e, k=k: e.matmul(pST[0:64, 0:64], lhsT=kT[b_][:, k, lc], rhs=qT[b_][:, k, lc],
                                                                 start=(k == 0), stop=(k == 3))) for k in range(4)],
                                 reads=kT[b_].B + qT[b_].B, writes=pST.B)
                            S.op("act", lambda e: e.activation(out=DT[:], in_=mxr[:, tok0:tok0 + LCH], func=AF.Exp, scale=-1.0,
                                                               bias=cols[0][:, c:c + 1]), reads=mxr.B + cols[0].B, writes=DT.B)
                            S.op("pool", lambda e: e.tensor_tensor(out=DTm[:], in0=DT[:], in1=mask[d][:], op=ALU.mult),
                                 reads=DT.B + mask[d].B, writes=DTm.B)
                            S.op("dve", lambda e: e.tensor_tensor(out=STb[:], in0=pST[0:64, 0:64], in1=DTm[:], op=ALU.mult),
                                 reads=pST.B + DTm.B, writes=STb.B)
                            S.op("pe", [(lambda e, k=k: e.matmul(pP1[0:64, :], lhsT=qT[b_][:, k, lc], rhs=Cb[:, k, :],
                                                                 start=(k == 0), stop=(k == 3))) for k in range(4)],
                                 reads=qT[b_].B + Cb.B, writes=pP1.B)
                            S.op("pe", lambda e: e.matmul(pP2[0:64, :], lhsT=STb[:], rhs=vtc, start=True, stop=True),
                                 reads=STb.B + vtm[b_].B, writes=pP2.B)
                            S.op("pe", [(lambda e, k=k: e.matmul(pDN[0:64, 0:1], lhsT=qT[b_][:, k, lc], rhs=nb[:, k:k + 1],
                                                                 start=(k == 0), stop=(k == 3))) for k in range(4)]
                                 + [lambda e: e.matmul(pDN[0:64, 1:2], lhsT=STb[:], rhs=onesb[:], start=True, stop=True)],
                                 reads=qT[b_].B + nb.B + STb.B + onesb.B, writes=pDN.B)
                            acol, fcol, wcol = cols[1][:, c:c + 1], cols[2][:, c:c + 1], cols[3][:, c:c + 1]
                            S.op("act", lambda e: e.mul(out=t1[:], in_=pP1[0:64, :], mul=acol), reads=pP1.B + cols[1].B, writes=t1.B)
                            S.op("dve", lambda e: e.tensor_tensor(out=num[:], in0=pP2[0:64, :], in1=t1[:], op=ALU.add),
                                 reads=pP2.B + t1.B, writes=num.B)
                            S.op("act", lambda e: e.copy(out=sm[:, 0:2], in_=pDN[0:64, 0:2]), reads=pDN.B, writes=sm.B)
                            S.op("dve", lambda e: e.scalar_tensor_tensor(out=sm[:, 2:3], in0=sm[:, 0:1], scalar=acol, in1=sm[:, 1:2],
                                                                         op0=ALU.mult, op1=ALU.add), reads=sm.B + cols[1].B, writes=sm.B)
                            S.op("dve", lambda e: e.tensor_scalar(out=sm[:, 12:13], in0=sm[:, 2:3], scalar1=-1.0, scalar2=None, op0=ALU.mult),
                                 reads=sm.B, writes=sm.B)
                            S.op("dve", lambda e: e.tensor_tensor(out=sm[:, 3:4], in0=sm[:, 2:3], in1=sm[:, 12:13], op=ALU.max), reads=sm.B, writes=sm.B)
                            S.op("dve", lambda e: e.tensor_tensor(out=sm[:, 3:4], in0=sm[:, 3:4], in1=fcol, op=ALU.max),
                                 reads=sm.B + cols[2].B, writes=sm.B)
                            S.op("dve", lambda e: e.reciprocal(out=sm[:, 4:5], in_=sm[:, 3:4]), reads=sm.B, writes=sm.B)
                            S.op("dve", lambda e: e.tensor_scalar(out=hc[:], in0=num[:], scalar1=sm[:, 4:5], scalar2=None, op0=ALU.mult),
                                 reads=num.B + sm.B, writes=hc.B)
                            if d == 0:
                                S.dma("pool", Hd[tok0:tok0 + LCH, hrows], hc[:], reads=hc.B, writes=Hd.B)
                            else:
                                S.dma("sp", hf[:], Hd[tok0:tok0 + LCH, hrows], reads=Hd.B, writes=hf.B)
                                S.op("dve", lambda e: e.tensor_tensor(out=hc[:], in0=hc[:], in1=hf[:], op=ALU.add), reads=hc.B + hf.B, writes=hc.B)
                                S.op("dve", lambda e: e.reduce_sum(out=sm[:, 5:6], in_=hc[:], axis=AX.X), reads=hc.B, writes=sm.B)
                                S.op("pool", lambda e: e.tensor_tensor(out=sq[:], in0=hc[:], in1=hc[:], op=ALU.mult), reads=hc.B, writes=sq.B)
                                S.op("dve", lambda e: e.reduce_sum(out=sm[:, 6:7], in_=sq[:], axis=AX.X), reads=sq.B, writes=sm.B)
                                S.op("dve", lambda e: e.tensor_scalar(out=sm[:, 5:7], in0=sm[:, 5:7], scalar1=1.0 / DH, scalar2=None, op0=ALU.mult),
                                     reads=sm.B, writes=sm.B)
                                S.op("dve", lambda e: e.tensor_tensor(out=sm[:, 7:8], in0=sm[:, 5:6], in1=sm[:, 5:6], op=ALU.mult), reads=sm.B, writes=sm.B)
                                S.op("dve", lambda e: e.tensor_tensor(out=sm[:, 8:9], in0=sm[:, 6:7], in1=sm[:, 7:8], op=ALU.subtract), reads=sm.B, writes=sm.B)
                                S.op("dve", lambda e: e.tensor_scalar(out=sm[:, 8:9], in0=sm[:, 8:9], scalar1=EPS, scalar2=None, op0=ALU.add), reads=sm.B, writes=sm.B)
                                S.op("act", lambda e: e.activation(out=sm[:, 9:10], in_=sm[:, 8:9], func=AF.Sqrt), reads=sm.B, writes=sm.B)
                                S.op("dve", lambda e: e.reciprocal(out=sm[:, 10:11], in_=sm[:, 9:10]), reads=sm.B, writes=sm.B)
                                S.op("dve", lambda e: e.scalar_tensor_tensor(out=sm[:, 11:12], in0=sm[:, 5:6], scalar=-1.0, in1=sm[:, 10:11],
                                                                             op0=ALU.mult, op1=ALU.mult), reads=sm.B, writes=sm.B)
                                S.op("dve", lambda e: e.tensor_scalar(out=hc[:], in0=hc[:], scalar1=sm[:, 10:11], scalar2=sm[:, 11:12],
                                                                      op0=ALU.mult, op1=ALU.add), reads=hc.B + sm.B, writes=hc.B)
                                S.op("dve", lambda e: e.tensor_tensor(out=hnb[:], in0=hc[:], in1=mhg[:], op=ALU.mult), reads=hc.B + mhg.B, writes=hnb.B)
                                S.dma("pool", HNd[tok0:tok0 + LCH, hrows], hnb[:], reads=hnb.B, writes=HNd.B)
                            dcol = dcr[:, cp:cp + 1]
                            S.op("act", lambda e: e.mul(out=kw[:], in_=ktc, mul=wcol), reads=ktm[b_].B + cols[3].B, writes=kw.B)
                            pNU = self.next_ps()
                            for k in range(4):
                                pU = self.next_ps()
                                while pU is pNU:
                                    pU = self.next_ps()
                                S.op("pe", lambda e: e.matmul(pU[:], lhsT=kw[:, k * 128:(k + 1) * 128], rhs=vtc, start=True, stop=True),
                                     reads=kw.B + vtm[b_].B, writes=pU.B)
                                S.op("dve", lambda e: e.scalar_tensor_tensor(out=C[:, k, :], in0=C[:, k, :], scalar=dcol, in1=pU[:],
                                                                             op0=ALU.mult, op1=ALU.add), reads=C.B + dcr.B + pU.B, writes=C.B)
                                S.op("act", lambda e: e.mul(out=Cb[:, k, :], in_=C[:, k, :], mul=1.0), reads=C.B, writes=Cb.B)
                            S.op("pe", [(lambda e, k=k: e.matmul(pNU[:, k:k + 1], lhsT=kw[:, k * 128:(k + 1) * 128], rhs=onesb[:], start=True, stop=True))
                                        for k in range(4)], reads=kw.B + onesb.B, writes=pNU.B)
                            S.op("dve", lambda e: e.scalar_tensor_tensor(out=nn[:], in0=nn[:], scalar=dcol, in1=pNU[:, 0:4],
                                                                         op0=ALU.mult, op1=ALU.add), reads=nn.B + dcr.B + pNU.B, writes=nn.B)
                            S.op("act", lambda e: e.mul(out=nb[:], in_=nn[:], mul=1.0), reads=nn.B, writes=nb.B)
                            cp += 1
            S.barrier()


    def layer_hyena(self, li, j):
        S = self.S
        w_in = self.convert_weight("hy_in", self.hy_w_in[:], D, 4 * E)
        w_out = self.convert_weight("hy_out", self.hy_w_out[:], E, D)
        ds = self.dram_scratch
        Ud, Gd = ds("hy_U", [SEQ, E], BF16), ds("hy_G", [SEQ, E], BF16)
        Ad, Bd = ds("hy_A", [SEQ, E], BF16), ds("hy_B", [SEQ, E], BF16)
        Krd, Kid = ds("hy_Kr", [SEQ, E], F32), ds("hy_Ki", [SEQ, E], F32)
        YGd = ds("hy_YG", [SEQ, E], BF16)
        S.barrier()
        self.hy_filter(Ad, Bd)
        S.barrier()
        with contextlib.ExitStack() as les:
            self.alloc_commons(les)
            self.ada_mod(li, "lat")
            self.pass1(li, "lat")
            self.hy_proj(les, w_in, Ud, Gd)
            S.barrier()
        self.hy_dft(Ud, Gd, Ad, Bd, Krd, Kid, YGd)
        S.barrier()
        with contextlib.ExitStack() as les:
            self.alloc_commons(les)
            self.ada_mod(li, "lat")
            vT = self.sb("hy_vT", [128, EC, 256], BF16, st=les)
            yg = [self.sb(f"hy_yg{i}", [128, E], BF16, st=les) for i in range(2)]
            for tt in range(SEQ // 256):
                for sub in range(2):
                    y = yg[sub]
                    r0 = tt * 256 + sub * 128
                    S.dma("sp", y[:], YGd[r0:r0 + 128, :], reads=YGd.B, writes=y.B)
                    for g in range(EC // 4):
                        S.op("pe", [(lambda e, q=q: e.transpose(self.psb[:, q * 128:(q + 1) * 128],
                                                                y[:, (g * 4 + q) * 128:(g * 4 + q + 1) * 128], self.ident[:]))
                                    for q in range(4)], reads=y.B + self.ident.B, writes=self.psb.B)
                        S.op("act", lambda e: e.copy(out=vT[:, g * 4:(g + 1) * 4, sub * 128:(sub + 1) * 128],
                                                     in_=self.psb[:, 0:512].rearrange("p (q t) -> p q t", q=4)),
                             reads=self.psb.B, writes=vT.B)
                self.out_proj(li, "lat", vT, w_out, tt * 256, 256)
            S.barrier()

    def hy_filter(self, Ad, Bd):
        S = self.S
        TWO_PI = 2.0 * math.pi
        with contextlib.ExitStack() as st:
            sb = lambda n, sh, dt=F32: self.sb(n, sh, dt, st=st)
            featT, h1, h2 = sb("hf_feat", [33, SEQ]), sb("hf_h1", [64, SEQ]), sb("hf_h2", [65, SEQ])
            w3, w1, w2 = sb("hf_w3", [65, 2 * E]), sb("hf_w1", [33, 64]), sb("hf_w2", [64, 64])
            vec = sb("hf_vec", [64, 8])
            dec, dl, ntl = sb("hf_dec", [128, E]), sb("hf_dl", [128, E]), sb("hf_ntl", [128, 32])
            arg = [sb(f"hf_arg{i}", [64, 512]) for i in range(2)]
            ki = [sb(f"hf_ki{i}", [64, 512], I32) for i in range(2)]
            kf = [sb(f"hf_kf{i}", [64, 512]) for i in range(2)]
            sB = [sb(f"hf_sB{i}", [128, 512]) for i in range(2)]
            t1 = [sb(f"hf_t1{i}", [128, 512]) for i in range(2)]
            oA = [sb(f"hf_oA{i}", [128, 512], BF16) for i in range(2)]
            oB = [sb(f"hf_oB{i}", [128, 512], BF16) for i in range(2)]
            for dst, src in ((featT[:], self.hy_featT), (w1[:], self.hy_w1), (w2[:], self.hy_w2), (vec[:, 0:4], self.hy_fvec),
                             (w3[0:64, :], self.hy_w3), (ntl[:], self.hy_ntl)):
                S.dma("sp", dst, src[:], reads=src.B, writes=[Buf()])
            S.dma("sp", w3[64:65, :], self.hy_b3[:].partition_broadcast(1), reads=self.hy_b3.B, writes=w3.B)
            S.dma("sp", dl[:], self.hy_delta[:].partition_broadcast(128), reads=self.hy_delta.B, writes=dl.B)
            S.barrier()
            S.op("dve", lambda e: e.memset(h2[64:65, :], 1.0), writes=h2.B)
            S.op("dve", lambda e: e.tensor_tensor(out=vec[:, 4:5], in0=vec[:, 0:1], in1=vec[:, 1:2], op=ALU.mult), reads=vec.B, writes=vec.B)
            S.op("dve", lambda e: e.tensor_tensor(out=vec[:, 5:6], in0=vec[:, 2:3], in1=vec[:, 3:4], op=ALU.mult), reads=vec.B, writes=vec.B)

            def sin_layer(w, src, kdim, dst, fcol, fbcol):
                for n in range(8):
                    cols = slice(n * 512, (n + 1) * 512)
                    ps = self.next_ps()
                    S.op("pe", lambda e: e.matmul(ps[0:64, :], lhsT=w[0:kdim, :], rhs=src[0:kdim, cols], start=True, stop=True),
                         reads=w.B + src.B, writes=ps.B)
                    a, k_i, k_f = arg[n % 2], ki[n % 2], kf[n % 2]
                    S.op("dve", lambda e: e.tensor_scalar(out=a[:], in0=ps[0:64, :], scalar1=vec[:, fcol:fcol + 1], scalar2=vec[:, fbcol:fbcol + 1],
                                                          op0=ALU.mult, op1=ALU.add), reads=ps.B + vec.B, writes=a.B)
                    S.op("dve", lambda e: e.tensor_scalar(out=k_f[:], in0=a[:], scalar1=1.0 / TWO_PI, scalar2=None, op0=ALU.mult), reads=a.B, writes=k_f.B)
                    S.op("dve", lambda e: e.tensor_copy(out=k_i[:], in_=k_f[:]), reads=k_f.B, writes=k_i.B)
                    S.op("dve", lambda e: e.tensor_copy(out=k_f[:], in_=k_i[:]), reads=k_i.B, writes=k_f.B)
                    S.op("dve", lambda e: e.scalar_tensor_tensor(out=a[:], in0=k_f[:], scalar=-TWO_PI, in1=a[:], op0=ALU.mult, op1=ALU.add),
                         reads=k_f.B + a.B, writes=a.B)
                    S.op("dve", lambda e: e.tensor_scalar(out=a[:], in0=a[:], scalar1=-3.141592, scalar2=3.141592, op0=ALU.max, op1=ALU.min),
                         reads=a.B, writes=a.B)
                    S.op("act", lambda e: e.activation(out=dst[0:64, cols], in_=a[:], func=AF.Sin), reads=a.B, writes=dst.B)

            sin_layer(w1, featT, 33, h1, 1, 4)
            sin_layer(w2, h1, 64, h2, 3, 5)
            for tc in range(32):
                S.op("act", lambda e: e.activation(out=dec[:], in_=dl[:], func=AF.Exp, scale=ntl[:, tc:tc + 1]), reads=dl.B + ntl.B, writes=dec.B)
                for n in range(8):
                    cols = slice(n * 512, (n + 1) * 512)
                    psF, psB = self.next_ps(), self.next_ps()
                    S.op("pe", lambda e: e.matmul(psF[:], lhsT=h2[0:65, tc * 128:(tc + 1) * 128], rhs=w3[0:65, cols], start=True, stop=True),
                         reads=h2.B + w3.B, writes=psF.B)
                    S.op("pe", lambda e: e.matmul(psB[:], lhsT=h2[0:65, tc * 128:(tc + 1) * 128], rhs=w3[0:65, E + n * 512:E + (n + 1) * 512],
                                                  start=True, stop=True), reads=h2.B + w3.B, writes=psB.B)
                    b_, t_, a_, o_ = sB[n % 2], t1[n % 2], oA[n % 2], oB[n % 2]
                    S.op("act", lambda e: e.copy(out=b_[:], in_=psB[:]), reads=psB.B, writes=b_.B)
                    if tc == 0:
                        S.op("dve", lambda e: e.memset(b_[0:1, :], 0.0), writes=b_.B)
                    S.op("dve", lambda e: e.tensor_tensor(out=t_[:], in0=psF[:], in1=b_[:], op=ALU.add), reads=psF.B + b_.B, writes=t_.B)
                    S.op("dve", lambda e: e.tensor_tensor(out=a_[:], in0=t_[:], in1=dec[:, cols], op=ALU.mult), reads=t_.B + dec.B, writes=a_.B)
                    S.op("dve", lambda e: e.tensor_tensor(out=t_[:], in0=psF[:], in1=b_[:], op=ALU.subtract), reads=psF.B + b_.B, writes=t_.B)
                    S.op("dve", lambda e: e.tensor_tensor(out=o_[:], in0=t_[:], in1=dec[:, cols], op=ALU.mult), reads=t_.B + dec.B, writes=o_.B)
                    S.dma("pool", Ad[tc * 128:(tc + 1) * 128, cols], a_[:], reads=a_.B, writes=Ad.B)
                    S.dma("pool", Bd[tc * 128:(tc + 1) * 128, cols], o_[:], reads=o_.B, writes=Bd.B)
            S.barrier()

    def hy_proj(self, les, w_in, Ud, Gd):
        S = self.S
        TT = 256
        sb = lambda n, sh, dt=F32: self.sb(n, sh, dt, st=les)
        uT = sb("hp_uT", [128, KC, TT + 2], BF16)
        Utm, Gtm = sb("hp_U", [128, 2, E], BF16), sb("hp_G", [128, 2, E], BF16)
        cva = [sb(f"hp_a{i}", [128, TT]) for i in range(3)]
        zs = sb("hp_zs", [128, TT])
        uu, gg = sb("hp_uu", [128, TT], BF16), sb("hp_gg", [128, TT], BF16)
        cw, cb = sb("hp_cw", [128, 96, 3]), sb("hp_cb", [128, 96])
        S.dma("sp", cw[:], self.hy_cw[:], reads=self.hy_cw.B, writes=cw.B)
        S.dma("sp", cb[:], self.hy_cb[:], reads=self.hy_cb.B, writes=cb.B)
        utv = self.UT[:].rearrange("(k p) t -> p k t", p=128)
        wv = w_in[:].rearrange("(k p) n -> p k n", p=128)
        for tt in range(SEQ // TT):
            t0 = tt * TT
            S.dma("sp", uT[:], utv[:, :, UT_LAT0 + t0 - 1: UT_LAT0 + t0 + TT + 1], reads=self.UT.B, writes=uT.B)
            for g in range(EC // 4):
                wts = []
                for part in range(4):
                    w = self.next_w()
                    wt = w[:].rearrange("p (k n) -> p k n", k=16)
                    S.dma("sp", wt, wv[:, :, part * E + g * 512: part * E + (g + 1) * 512], reads=w_in.B, writes=w.B)
                    wts.append((w, wt))
                for q in range(4):
                    ec = g * 4 + q
                    pss = []
                    for part in range(4):
                        ps = self.next_ps()
                        wt = wts[part][1]
                        S.op("pe", [(lambda e, k=k, ps=ps, wt=wt: e.matmul(ps[:, 0:TT + 2], lhsT=wt[:, k, q * 128:(q + 1) * 128], rhs=uT[:, k, :],
                                                                           start=(k == 0), stop=(k == KC - 1))) for k in range(KC)],
                             reads=uT.B + wts[part][0].B, writes=ps.B)
                        pss.append(ps)
                    for part in range(3):
                        a, ps, ci = cva[part], pss[part], part * 32 + ec
                        S.op("dve", lambda e: e.tensor_scalar(out=a[:], in0=ps[:, 1:TT + 1], scalar1=cw[:, ci, 1:2], scalar2=cb[:, ci:ci + 1],
                                                              op0=ALU.mult, op1=ALU.add), reads=ps.B + cw.B + cb.B, writes=a.B)
                        S.op("dve", lambda e: e.scalar_tensor_tensor(out=a[:], in0=ps[:, 0:TT], scalar=cw[:, ci, 0:1], in1=a[:],
                                                                     op0=ALU.mult, op1=ALU.add), reads=ps.B + cw.B + a.B, writes=a.B)
                        S.op("dve", lambda e: e.scalar_tensor_tensor(out=a[:], in0=ps[:, 2:TT + 2], scalar=cw[:, ci, 2:3], in1=a[:],
                                                                     op0=ALU.mult, op1=ALU.add), reads=ps.B + cw.B + a.B, writes=a.B)
                    S.op("act", lambda e: e.activation(out=zs[:], in_=pss[3][:, 1:TT + 1], func=AF.Silu), reads=pss[3].B, writes=zs.B)
                    S.op("dve", lambda e: e.tensor_tensor(out=uu[:], in0=cva[1][:], in1=cva[2][:], op=ALU.mult),
                         reads=cva[1].B + cva[2].B, writes=uu.B)
                    S.op("pool", lambda e: e.tensor_tensor(out=gg[:], in0=cva[0][:], in1=zs[:], op=ALU.mult),
                         reads=cva[0].B + zs.B, writes=gg.B)
                    S.op("pe", [(lambda e, q2=q2: e.transpose(self.psb[:, q2 * 128:(q2 + 1) * 128],
                                                              (uu if q2 < 2 else gg)[:, (q2 % 2) * 128:(q2 % 2 + 1) * 128], self.ident[:]))
                                for q2 in range(4)], reads=uu.B + gg.B + self.ident.B, writes=self.psb.B)
                    S.op("act", lambda e: e.copy(out=Utm[:, :, ec * 128:(ec + 1) * 128],
                                                 in_=self.psb[:, 0:256].rearrange("p (s t) -> p s t", s=2)), reads=self.psb.B, writes=Utm.B)
                    S.op("act", lambda e: e.copy(out=Gtm[:, :, ec * 128:(ec + 1) * 128],
                                                 in_=self.psb[:, 256:512].rearrange("p (s t) -> p s t", s=2)), reads=self.psb.B, writes=Gtm.B)
            for sub in range(2):
                r0 = t0 + sub * 128
                S.dma("pool", Ud[r0:r0 + 128, :], Utm[:, sub, :], reads=Utm.B, writes=Ud.B)
                S.dma("pool", Gd[r0:r0 + 128, :], Gtm[:, sub, :], reads=Gtm.B, writes=Gd.B)

    def hy_dft(self, Ud, Gd, Ad, Bd, Krd, Kid, YGd):
        S = self.S
        NF = 8192.0
        with contextlib.ExitStack() as st:
            sb = lambda n, sh, dt=F32: self.sb(n, sh, dt, st=st)
            Ut, Bt = sb("hd_U", [128, 32, 512], BF16), sb("hd_B", [128, 32, 512], BF16)
            Y = sb("hd_Y", [128, 64, 512], BF16)
            Cb = [sb(f"hd_C{i}", [128, 32, 128], BF16) for i in range(2)]
            Sb = [sb(f"hd_S{i}", [128, 32, 128], BF16) for i in range(2)]
            Kr = [sb(f"hd_Kr{i}", [128, 512]) for i in range(2)]
            Ki = [sb(f"hd_Ki{i}", [128, 512]) for i in range(2)]
            tm = [sb(f"hd_t{i}", [128, 512]) for i in range(4)]
            gt = [sb(f"hd_g{i}", [128, 512], BF16) for i in range(2)]
            og = [sb(f"hd_o{i}", [128, 512], BF16) for i in range(2)]
            hb = sb("hd_hb", [128, 512])
            tabC = self.hy_C[:].rearrange("(c p) k -> p c k", p=128)
            tabS = self.hy_S[:].rearrange("(c p) k -> p c k", p=128)
            tabCT = self.hy_CT[:].rearrange("(c p) k -> p c k", p=128)
            tabST = self.hy_ST[:].rearrange("(c p) k -> p c k", p=128)
            view = lambda d, n: d[:, n * 512:(n + 1) * 512].rearrange("(c p) e -> p c e", p=128)
            for n in range(8):
                cols = slice(n * 512, (n + 1) * 512)
                S.dma("sp", Ut[:], view(Ad, n), reads=Ad.B, writes=Ut.B)
                S.dma("sp", Bt[:], view(Bd, n), reads=Bd.B, writes=Bt.B)
                for j in range(32):
                    cb_, sb_ = Cb[j % 2], Sb[j % 2]
                    S.dma("sp", cb_[:], tabC[:, :, j * 128:(j + 1) * 128], reads=self.hy_C.B, writes=cb_.B)
                    S.dma("sp", sb_[:], tabS[:, :, j * 128:(j + 1) * 128], reads=self.hy_S.B, writes=sb_.B)
                    psR, psI = self.next_ps(), self.next_ps()
                    S.op("pe", [(lambda e, c=c: e.matmul(psR[:], lhsT=cb_[:, c, :], rhs=Ut[:, c, :], start=(c == 0), stop=(c == 31)))
                                for c in range(32)], reads=cb_.B + Ut.B, writes=psR.B)
                    S.op("pe", [(lambda e, c=c: e.matmul(psI[:], lhsT=sb_[:, c, :], rhs=Bt[:, c, :], start=(c == 0), stop=(c == 31)))
                                for c in range(32)], reads=sb_.B + Bt.B, writes=psI.B)
                    kr, ki_ = Kr[j % 2], Ki[j % 2]
                    S.op("act", lambda e: e.copy(out=kr[:], in_=psR[:]), reads=psR.B, writes=kr.B)
                    S.op("dve", lambda e: e.tensor_copy(out=ki_[:], in_=psI[:]), reads=psI.B, writes=ki_.B)
                    S.dma("pool", Krd[j * 128:(j + 1) * 128, cols], kr[:], reads=kr.B, writes=Krd.B)
                    S.dma("pool", Kid[j * 128:(j + 1) * 128, cols], ki_[:], reads=ki_.B, writes=Kid.B)
            S.barrier()
            for n in range(8):
                cols = slice(n * 512, (n + 1) * 512)
                S.dma("sp", Ut[:], view(Ud, n), reads=Ud.B, writes=Ut.B)
                S.dma("sp", hb[:], self.hy_hb[n * 512:(n + 1) * 512].partition_broadcast(128), reads=self.hy_hb.B, writes=hb.B)
                for j in range(32):
                    cb_, sb_ = Cb[j % 2], Sb[j % 2]
                    S.dma("sp", cb_[:], tabC[:, :, j * 128:(j + 1) * 128], reads=self.hy_C.B, writes=cb_.B)
                    S.dma("sp", sb_[:], tabS[:, :, j * 128:(j + 1) * 128], reads=self.hy_S.B, writes=sb_.B)
                    kr, ki_ = Kr[j % 2], Ki[j % 2]
                    S.dma("sp", kr[:], Krd[j * 128:(j + 1) * 128, cols], reads=Krd.B, writes=kr.B)
                    S.dma("sp", ki_[:], Kid[j * 128:(j + 1) * 128, cols], reads=Kid.B, writes=ki_.B)
                    psR, psI = self.next_ps(), self.next_ps()
                    S.op("pe", [(lambda e, c=c: e.matmul(psR[:], lhsT=cb_[:, c, :], rhs=Ut[:, c, :], start=(c == 0), stop=(c == 31)))
                                for c in range(32)], reads=cb_.B + Ut.B, writes=psR.B)
                    S.op("pe", [(lambda e, c=c: e.matmul(psI[:], lhsT=sb_[:, c, :], rhs=Ut[:, c, :], start=(c == 0), stop=(c == 31)))
                                for c in range(32)], reads=sb_.B + Ut.B, writes=psI.B)
                    S.op("dve", lambda e: e.tensor_tensor(out=tm[0][:], in0=psR[:], in1=kr[:], op=ALU.mult), reads=psR.B + kr.B, writes=tm[0].B)
                    S.op("dve", lambda e: e.tensor_tensor(out=tm[1][:], in0=psI[:], in1=ki_[:], op=ALU.mult), reads=psI.B + ki_.B, writes=tm[1].B)
                    S.op("pool", lambda e: e.tensor_tensor(out=Y[:, j, :], in0=tm[0][:], in1=tm[1][:], op=ALU.subtract),
                         reads=tm[0].B + tm[1].B, writes=Y.B)
                    S.op("dve", lambda e: e.tensor_tensor(out=tm[2][:], in0=psR[:], in1=ki_[:], op=ALU.mult), reads=psR.B + ki_.B, writes=tm[2].B)
                    S.op("dve", lambda e: e.tensor_tensor(out=tm[3][:], in0=psI[:], in1=kr[:], op=ALU.mult), reads=psI.B + kr.B, writes=tm[3].B)
                    S.op("pool", lambda e: e.tensor_tensor(out=Y[:, 32 + j, :], in0=tm[2][:], in1=tm[3][:], op=ALU.add),
                         reads=tm[2].B + tm[3].B, writes=Y.B)
                for tc in range(32):
                    cb_, sb_ = Cb[tc % 2], Sb[tc % 2]
                    S.dma("sp", cb_[:], tabCT[:, :, tc * 128:(tc + 1) * 128], reads=self.hy_CT.B, writes=cb_.B)
                    S.dma("sp", sb_[:], tabST[:, :, tc * 128:(tc + 1) * 128], reads=self.hy_ST.B, writes=sb_.B)
                    g_, o_ = gt[tc % 2], og[tc % 2]
                    S.dma("sp", g_[:], Gd[tc * 128:(tc + 1) * 128, cols], reads=Gd.B, writes=g_.B)
                    ps = self.next_ps()
                    S.op("pe", [(lambda e, c=c: e.matmul(ps[:], lhsT=(cb_ if c < 32 else sb_)[:, c % 32, :], rhs=Y[:, c, :],
                                                         start=(c == 0), stop=(c == 63))) for c in range(64)],
                         reads=cb_.B + sb_.B + Y.B, writes=ps.B)
                    ta, tb = tm[tc % 2], tm[2 + tc % 2]
                    S.op("dve", lambda e: e.tensor_tensor(out=ta[:], in0=Ut[:, tc, :], in1=hb[:], op=ALU.mult), reads=Ut.B + hb.B, writes=ta.B)
                    S.op("dve", lambda e: e.scalar_tensor_tensor(out=tb[:], in0=ps[:], scalar=2.0 / NF, in1=ta[:], op0=ALU.mult, op1=ALU.add),
                         reads=ps.B + ta.B, writes=tb.B)
                    S.op("pool", lambda e: e.tensor_tensor(out=o_[:], in0=tb[:], in1=g_[:], op=ALU.mult), reads=tb.B + g_.B, writes=o_.B)
                    S.dma("pool", YGd[tc * 128:(tc + 1) * 128, cols], o_[:], reads=o_.B, writes=YGd.B)
            S.barrier()


def _pk(v):
    return np.ascontiguousarray(v.reshape(-1, 128).T)


def make_inputs(b, inp):
    f = np.float32
    m = {
        "x": np.ascontiguousarray(inp["x"][b]), "ctx": np.ascontiguousarray(inp["ctx"][b]),
        "c_pk": _pk(inp["c"][b]), "cc_pk": _pk(inp["c_ctx"]),
        "norm_g": inp["norm_g"], "ada_w": inp["ada_w"], "ada_b": inp["ada_b"], "final_g": inp["final_g"],
        "cv_w_in": inp["cv_w_in"],
        "cv_dw": np.ascontiguousarray(inp["cv_dw_w"].reshape(2, CONVW, EC, 128).transpose(0, 3, 2, 1)),
        "cv_dwb": np.ascontiguousarray(inp["cv_dw_b"].reshape(2, EC, 128).transpose(0, 2, 1)),
        "cv_lng": np.ascontiguousarray(inp["cv_ln_g"].reshape(2, EC, 128).transpose(0, 2, 1)),
        "cv_lnb": np.ascontiguousarray(inp["cv_ln_b"].reshape(2, EC, 128).transpose(0, 2, 1)),
        "cv_w_out": inp["cv_w_out"],
        "ident": np.eye(128, dtype=f),
    }
    m.update(_hyena_consts())
    m.update(_mlstm_inputs(inp))
    m.update({
        "hy_w_in": inp["hy_w_in"][0], "hy_w_out": inp["hy_w_out"][0],
        "hy_cw": np.ascontiguousarray(inp["hy_conv_w"][0].reshape(3, 96, 128).transpose(2, 1, 0)),
        "hy_cb": np.ascontiguousarray(inp["hy_conv_b"][0].reshape(96, 128).T),
        "hy_fvec": np.ascontiguousarray(np.stack([inp["hy_f_b1"][0], inp["hy_f_freq1"][0], inp["hy_f_b2"][0], inp["hy_f_freq2"][0]], axis=1)),
        "hy_w1": inp["hy_f_w1"][0], "hy_w2": inp["hy_f_w2"][0], "hy_w3": inp["hy_f_w3"][0], "hy_b3": inp["hy_f_b3"][0],
        "hy_hb": inp["hy_h_bias"][0],
    })
    return m


def _mlstm_inputs(inp):
    f = np.float32
    bd = np.zeros((4, EC, 128, 128), f)
    for a, key in enumerate(("ml_w_q", "ml_w_k", "ml_w_v", "ml_w_o")):
        w = inp[key][0].reshape(EC, 32, 4, 4)
        for g in range(32):
            bd[a, :, 4 * g:4 * g + 4, 4 * g:4 * g + 4] = w[:, g]
    wg = inp["ml_w_gates"][0].reshape(96, 128, 2, 2, 8)
    wg = np.ascontiguousarray(wg.transpose(1, 2, 3, 0, 4)).reshape(128, 4 * 96 * 8)
    bg = np.ascontiguousarray(inp["ml_b_gates"][0].reshape(4, 8).T)
    pk = lambda v: np.ascontiguousarray(v.reshape(EC, 128).T)
    s_, t_ = np.meshgrid(np.arange(64), np.arange(64), indexing="ij")
    mask = np.stack([(s_ <= t_), (s_ >= t_)]).astype(f)
    return {
        "ml_w_in": inp["ml_w_in"][0], "ml_w_out": inp["ml_w_out"][0], "ml_bd": bd.reshape(4 * EC * 128, 128), "ml_wg": wg,
        "ml_cw": np.ascontiguousarray(inp["ml_conv_w"][0].reshape(3, EC, 128).transpose(2, 1, 0)),
        "ml_vec": np.ascontiguousarray(np.stack([pk(inp["ml_conv_b"][0]), pk(inp["ml_b_o"][0]), pk(inp["ml_skip"][0])], axis=1)),
        "ml_bg": bg, "ml_mhg": inp["ml_mh_g"][0], "ml_mask": mask,
    }


_HC = {}


def _hyena_consts():
    if _HC:
        return _HC
    f = np.float32
    L = SEQ
    t = np.linspace(0.0, 1.0, L, dtype=f)[:, None]
    bands = 16
    ang = (f(2.0 * math.pi) * np.arange(L, dtype=f)[:, None] / f(L)).astype(f)
    fr = np.linspace(1e-4, bands - 1, bands, dtype=f)[None, :]
    feat = np.concatenate([t, np.cos(fr * ang), -np.sin(fr * ang)], axis=-1).astype(f)
    lo = math.log(1e-2) / 0.3
    hi = math.log(1e-2) / 1.5
    deltas = np.abs(np.linspace(lo, hi, E, dtype=f)).astype(f)
    tl = t[:, 0]
    _HC["hy_featT"] = np.ascontiguousarray(feat.T)
    _HC["hy_delta"] = deltas
    _HC["hy_ntl"] = np.ascontiguousarray((-tl).reshape(32, 128).T)
    n = np.arange(SEQ, dtype=np.int64)
    ph = (np.outer(n, 2 * n + 1) % 16384).astype(np.float64) * (2.0 * math.pi / 16384.0)
    C = np.cos(ph).astype(ml_dtypes.bfloat16)
    Sn = np.sin(ph).astype(ml_dtypes.bfloat16)
    _HC["hy_C"] = C
    _HC["hy_S"] = Sn
    _HC["hy_CT"] = np.ascontiguousarray(C.T)
    _HC["hy_ST"] = np.ascontiguousarray(Sn.T)
    return _HC


_PROG = {}
ML_STOP = 9


def run(inputs, layers=(0, 1, 2, 3), final=True, cores=8):
    inp = {k: np.asarray(v, dtype=np.float32) for k, v in inputs.items()}
    key = (tuple(layers), final)
    p = Prog(layers, final)
    p.ml_stop = ML_STOP
    nc = p.build()
    in_maps = []
    for b in range(cores):
        m = make_inputs(b, inp)
        in_maps.append({k: m[k] for k in p.in_names})
    res = run_bass_kernel_spmd(nc, in_maps, core_ids=list(range(cores)))
    return np.stack([res.results[b]["out"] for b in range(cores)], axis=0)


def kernel(**inputs):
    return run(inputs).astype(np.float32)
```

```python
import contextlib
import math
import numpy as np
import ml_dtypes
import concourse.bass as bass
import concourse.mybir as mybir
from concourse.bass import AP
from concourse.bass_utils import run_bass_kernel_spmd

F32 = mybir.dt.float32
BF16 = mybir.dt.bfloat16
I32 = mybir.dt.int32
AF = mybir.ActivationFunctionType
ALU = mybir.AluOpType
AX = mybir.AxisListType

D = 2048
E = 4096
SEQ = 4096
NCTX = 256
EPS = 1e-6
KC = D // 128
EC = E // 128
CONVW = 31
NH = 8
DH = 512
LCH = 64
UT_CTX0 = 1
UT_LAT0 = 259
UT_COLS = 4356


class Buf:
    __slots__ = ("w", "r")

    def __init__(self):
        self.w = None
        self.r = {}


class Sched:
    def __init__(self, nc, es):
        self.nc = nc
        self.engs = {"pe": nc.tensor, "act": nc.scalar, "dve": nc.vector, "pool": nc.gpsimd, "sp": nc.sync}
        self.sem = {}
        self.cnt = {}
        for e in ("pe", "act", "dve", "pool"):
            self.sem[e] = es.enter_context(nc.semaphore("prog_" + e))
            self.cnt[e] = 0
        self.dsem = {"sp": [], "pool": [], "act": []}
        for q, n in (("sp", 28), ("pool", 20), ("act", 8)):
            for i in range(n):
                self.dsem[q].append([es.enter_context(nc.semaphore(f"d_{q}{i}")), 0, None])
        self.dnext = {"sp": 0, "pool": 0, "act": 0}
        self.seen = {e: {} for e in self.engs}
        self.semobj = {}

    def _wait(self, eng, deps):
        best = {}
        for tok in deps:
            if tok is None:
                continue
            sid, val = tok
            if best.get(sid, 0) < val:
                best[sid] = val
        seen = self.seen[eng]
        for sid, val in best.items():
            if seen.get(sid, 0) >= val:
                continue
            self.engs[eng].wait_ge(self.semobj[sid], val)
            seen[sid] = val

    def _deps(self, reads, writes):
        deps = []
        for b in reads:
            deps.append(b.w)
        for b in writes:
            deps.append(b.w)
            for sid, val in b.r.items():
                deps.append((sid, val))
        return deps

    def _commit(self, tok, reads, writes):
        sid, val = tok
        for b in reads:
            if b.r.get(sid, 0) < val:
                b.r[sid] = val
        for b in writes:
            b.w = tok
            b.r = {}

    def op(self, eng, fn, reads=(), writes=()):
        deps = self._deps(reads, writes)
        if eng == "pe":
            sid_own = id(self.sem["pe"])
            deps = [t for t in deps if t is not None and t[0] != sid_own]
        self._wait(eng, deps)
        e = self.engs[eng]
        fns = fn if isinstance(fn, (list, tuple)) else [fn]
        inst = None
        for f in fns:
            inst = f(e)
        self.cnt[eng] += 1
        sem = self.sem[eng]
        inst.then_inc(sem, 1)
        self.semobj[id(sem)] = sem
        tok = (id(sem), self.cnt[eng])
        self._commit(tok, reads, writes)
        return tok

    def dma(self, q, out, in_, reads=(), writes=(), **kw):
        slot = self.dsem[q][self.dnext[q]]
        self.dnext[q] = (self.dnext[q] + 1) % len(self.dsem[q])
        deps = self._deps(reads, writes)
        deps.append(slot[2])
        self._wait(q, deps)
        sem = slot[0]
        slot[1] += 16
        self.engs[q].dma_start(out=out, in_=in_, **kw).then_inc(sem, 16)
        self.semobj[id(sem)] = sem
        tok = (id(sem), slot[1])
        slot[2] = tok
        self._commit(tok, reads, writes)
        return tok

    def barrier(self):
        toks = [(id(self.sem[e]), self.cnt[e]) for e in self.sem if self.cnt[e] > 0]
        for q in self.dsem:
            for slot in self.dsem[q]:
                if slot[2] is not None:
                    toks.append(slot[2])
        for e in self.engs:
            self._wait(e, toks)

    def drain(self, eng, bufs):
        deps = []
        for b in bufs:
            deps.append(b.w)
        self._wait(eng, deps)


class T:
    def __init__(self, t, nb=1):
        self.t = t
        self.b = [Buf() for _ in range(nb)]

    def __getitem__(self, k):
        return self.t[k]

    @property
    def B(self):
        return self.b


class V:
    def __init__(self, t, n):
        self.t = t.t[:, 0:n]
        self.b = t.b

    def __getitem__(self, k):
        return self.t[k]

    @property
    def B(self):
        return self.b


def bcast_free(ap2d, n):
    a = ap2d.ap
    return AP(ap2d.tensor, ap2d.offset, [list(a[0]), list(a[1]), [0, n]])


class Prog:
    def __init__(self, layers=(0, 1, 2, 3), final=True):
        self.layers = layers
        self.final = final
        self.nc = bass.Bass("TRN2", target_bir_lowering=False)
        self.es = contextlib.ExitStack()

    def sb(self, name, shape, dt=F32, nb=1, st=None):
        self._uid = getattr(self, "_uid", 0) + 1
        return T((st or self.es).enter_context(self.nc.sbuf_tensor(f"{name}_{self._uid}", shape, dt)), nb)

    def dram_in(self, name, shape, dt=F32):
        t = self.nc.dram_tensor(name, list(shape), dt, kind="ExternalInput")
        self.in_names.append(name)
        return T(t.ap(), 1)

    def dram_scratch(self, name, shape, dt, nb=1):
        return T(self.nc.dram_tensor(name, list(shape), dt, kind="Internal").ap(), nb)

    def build(self):
        nc = self.nc
        self.in_names = []
        with self.es:
            self.S = Sched(nc, self.es)
            self._declare()
            self._consts()
            for li in self.layers:
                kind, j = li % 3, li // 3
                if kind == 0:
                    self.layer_conv(li, j)
                elif kind == 1:
                    self.layer_mlstm(li, j)
                else:
                    self.layer_hyena(li, j)
            if self.final:
                self.final_norm()
            else:
                self.copy_out()
            S = self.S
            S.drain("sp", self.out.B)
            S.drain("pool", self.out.B)
        return nc

    def _declare(self):
        di = self.dram_in
        self.x = di("x", [SEQ, D])
        self.ctx = di("ctx", [NCTX, D])
        self.c_pk = di("c_pk", [128, KC])
        self.cc_pk = di("cc_pk", [128, KC])
        self.norm_g = di("norm_g", [4, D])
        self.ada_w = di("ada_w", [4, D, 3 * D])
        self.ada_b = di("ada_b", [4, 3 * D])
        self.final_g = di("final_g", [D])
        if 0 in self.layers or 3 in self.layers:
            self.cv_w_in = di("cv_w_in", [2, D, 3 * E])
            self.cv_dw = di("cv_dw", [2, 128, EC, CONVW])
            self.cv_dwb = di("cv_dwb", [2, 128, EC])
            self.cv_lng = di("cv_lng", [2, 128, EC])
            self.cv_lnb = di("cv_lnb", [2, 128, EC])
            self.cv_w_out = di("cv_w_out", [2, E, D])
        self.ident_d = di("ident", [128, 128])
        if 1 in self.layers:
            self.ml_w_in = di("ml_w_in", [D, 2 * E])
            self.ml_w_out = di("ml_w_out", [E, D])
            self.ml_bd = di("ml_bd", [4 * EC * 128, 128])
            self.ml_wg = di("ml_wg", [128, 4 * 96 * 8])
            self.ml_cw = di("ml_cw", [128, EC, 3])
            self.ml_vec = di("ml_vec", [128, 3, EC])
            self.ml_bg = di("ml_bg", [8, 4])
            self.ml_mhg = di("ml_mhg", [E])
            self.ml_mask = di("ml_mask", [2, 64, 64])
        if 2 in self.layers:
            self.hy_w_in = di("hy_w_in", [D, 4 * E])
            self.hy_w_out = di("hy_w_out", [E, D])
            self.hy_cw = di("hy_cw", [128, 96, 3])
            self.hy_cb = di("hy_cb", [128, 96])
            self.hy_featT = di("hy_featT", [33, SEQ])
            self.hy_fvec = di("hy_fvec", [64, 4])
            self.hy_w1 = di("hy_w1", [33, 64])
            self.hy_w2 = di("hy_w2", [64, 64])
            self.hy_w3 = di("hy_w3", [64, 2 * E])
            self.hy_b3 = di("hy_b3", [2 * E])
            self.hy_hb = di("hy_hb", [E])
            self.hy_delta = di("hy_delta", [E])
            self.hy_ntl = di("hy_ntl", [128, 32])
            self.hy_C = di("hy_C", [SEQ, SEQ], BF16)
            self.hy_S = di("hy_S", [SEQ, SEQ], BF16)
            self.hy_CT = di("hy_CT", [SEQ, SEQ], BF16)
            self.hy_ST = di("hy_ST", [SEQ, SEQ], BF16)
        self.out = T(self.nc.dram_tensor("out", [SEQ, D], F32, kind="ExternalOutput").ap(), SEQ // 128)
        self.hctx = self.dram_scratch("hctx", [NCTX, D], F32, NCTX // 128)
        self.UT = self.dram_scratch("UT", [D, UT_COLS], BF16, 1)
        self.wb = {}

    def _consts(self):
        S, nc = self.S, self.nc
        self.ident_f = self.sb("ident_f", [128, 128], F32)
        self.ident = self.sb("ident_b", [128, 128], BF16)
        self.ones_f = self.sb("ones_f", [128, 128], F32)
        S.dma("sp", self.ident_f[:], self.ident_d[:], reads=self.ident_d.B, writes=self.ident_f.B)
        S.op("dve", lambda e: e.tensor_copy(out=self.ident[:], in_=self.ident_f[:]), reads=self.ident_f.B, writes=self.ident.B)
        S.op("dve", lambda e: e.memset(self.ones_f[:], 1.0), writes=self.ones_f.B)
        self.ps = [T(self.es.enter_context(nc.psum_tensor(f"ps{i}", [128, 512], F32))) for i in range(7)]
        self.psb = T(self.es.enter_context(nc.psum_tensor("psb", [128, 1024], BF16)))
        self.psi = 0
        self.csb = {}
        for nm, src in (("lat", self.c_pk), ("ctx", self.cc_pk)):
            cf = self.sb("cf_" + nm, [128, KC], F32)
            cs = self.sb("cs_" + nm, [128, KC], F32)
            S.dma("sp", cf[:], src[:], reads=src.B, writes=cf.B)
            S.op("act", lambda e: e.activation(out=cs[:], in_=cf[:], func=AF.Silu), reads=cf.B, writes=cs.B)
            self.csb[nm] = cs
        self.zcol = self.sb("zcol", [128, KC, 2], BF16)
        S.op("dve", lambda e: e.memset(self.zcol[:], 0.0), writes=self.zcol.B)
        utv = self.UT[:].rearrange("(k p) t -> p k t", p=128)
        for c0 in (0, 257, 4355):
            n = 2 if c0 == 257 else 1
            S.dma("sp", utv[:, :, c0:c0 + n], self.zcol[:, :, 0:n], reads=self.zcol.B, writes=self.UT.B,
                  allow_slow_non_contiguous=True)

    def alloc_commons(self, st, mods=True):
        self.G1 = self.sb("G1", [128, D], st=st)
        self.SH = self.sb("SH", [128, D], st=st)
        self.GT = self.sb("GT", [128, D], st=st)
        self.wpool = [self.sb(f"wp{i}", [128, 16 * 512], BF16, st=st) for i in range(4)]
        self.wpi = 0
        self.bb = [self.sb(f"bb{i}", [128, 512], st=st) for i in range(2)]
        self.tmpA = [self.sb(f"tmpA{i}", [128, 512], st=st) for i in range(2)]
        self.ht = [self.sb(f"ht{i}", [128, D], st=st) for i in range(2)]
        self.nt = self.sb("ntmp", [128, D], st=st)
        self.ngb = self.nt
        self.ub = self.sb("ub", [128, D], BF16, st=st)
        self.ss = self.sb("ss", [128, 4], st=st)
        self.uTs = [self.sb(f"uTs{i}", [128, KC, 128], BF16, st=st) for i in range(2)]

    def next_ps(self):
        p = self.ps[self.psi]
        self.psi = (self.psi + 1) % len(self.ps)
        return p

    def next_w(self):
        w = self.wpool[self.wpi]
        self.wpi = (self.wpi + 1) % len(self.wpool)
        return w

    def convert_weight(self, key, src_ap, rows, cols):
        if key in self.wb:
            return self.wb[key]
        nblk = rows // 128
        dst = self.dram_scratch("wb_" + key, [rows, cols], BF16, nblk)
        srcbuf = Buf()
        for r in range(nblk):
            self.S.dma("pool", dst[r * 128:(r + 1) * 128, :], src_ap[r * 128:(r + 1) * 128, :],
                       reads=[srcbuf], writes=[dst.b[r]])
        self.wb[key] = dst
        return dst

    def ada_mod(self, li, stream):
        S = self.S
        cs = self.csb[stream]
        cb = self.uTs[0]
        S.op("dve", lambda e: e.tensor_copy(out=cb[:], in_=bcast_free(cs[:], 128)), reads=cs.B, writes=cb.B)
        wv = self.ada_w[li].rearrange("(k p) n -> p k n", p=128)
        S.dma("sp", self.ngb[:], self.norm_g[li].partition_broadcast(128), reads=self.norm_g.B, writes=self.ngb.B)
        for n in range(12):
            w = self.next_w()
            wt = w[:].rearrange("p (k n) -> p k n", k=16)
            S.dma("pool", wt, wv[:, :, n * 512:(n + 1) * 512], reads=self.ada_w.B, writes=w.B)
            bb = self.bb[n % 2]
            S.dma("sp", bb[:], self.ada_b[li, n * 512:(n + 1) * 512].partition_broadcast(128),
                  reads=self.ada_b.B, writes=bb.B)
            ps = self.next_ps()
            S.op("pe", [(lambda e, k=k: e.matmul(ps[:], lhsT=cb[:, k, :], rhs=wt[:, k, :], start=(k == 0), stop=(k == 15)))
                        for k in range(16)], reads=cb.B + w.B, writes=ps.B)
            cols = slice((n % 4) * 512, (n % 4 + 1) * 512)
            if n < 4:
                S.op("dve", lambda e: e.tensor_tensor(out=self.SH[:, cols], in0=ps[:], in1=bb[:], op=ALU.add),
                     reads=ps.B + bb.B, writes=self.SH.B)
            elif n < 8:
                tmp = self.tmpA[n % 2]
                S.op("dve", lambda e: e.tensor_tensor(out=tmp[:], in0=ps[:], in1=bb[:], op=ALU.add),
                     reads=ps.B + bb.B, writes=tmp.B)
                S.op("dve", lambda e: e.scalar_tensor_tensor(out=self.G1[:, cols], in0=tmp[:], scalar=1.0, in1=self.ngb[:, cols],
                                                             op0=ALU.add, op1=ALU.mult),
                     reads=tmp.B + self.ngb.B, writes=self.G1.B)
            else:
                S.op("dve", lambda e: e.tensor_tensor(out=self.GT[:, cols], in0=ps[:], in1=bb[:], op=ALU.add),
                     reads=ps.B + bb.B, writes=self.GT.B)

    def hsrc(self, li, stream):
        if stream == "ctx":
            return self.ctx if (0 not in self.layers or li == 0) else self.hctx
        return self.x if li == self.layers[0] else self.out

    def pass1(self, li, stream):
        S = self.S
        src = self.hsrc(li, stream)
        ntile = (NCTX if stream == "ctx" else SEQ) // 128
        col0 = UT_CTX0 if stream == "ctx" else UT_LAT0
        utv = self.UT[:].rearrange("(k p) t -> p k t", p=128)
        for j in range(ntile):
            ht = self.ht[j % 2]
            sbuf = src.b[j] if len(src.b) > 1 else src.b[0]
            S.dma("sp", ht[:], src[j * 128:(j + 1) * 128, :], reads=[sbuf], writes=ht.B)
            self.norm_u(ht, self.G1, self.SH)
            uTs = self.uTs[j % 2]
            for g in range(4):
                S.op("pe", [(lambda e, q=q: e.transpose(self.psb[:, q * 128:(q + 1) * 128],
                                                        self.ub[:, (g * 4 + q) * 128:(g * 4 + q + 1) * 128], self.ident[:]))
                            for q in range(4)], reads=self.ub.B + self.ident.B, writes=self.psb.B)
                S.op("act", lambda e: e.copy(out=uTs[:, g * 4:(g + 1) * 4, :],
                                             in_=self.psb[:, 0:512].rearrange("p (q t) -> p q t", q=4)),
                     reads=self.psb.B, writes=uTs.B)
            S.dma("pool", utv[:, :, col0 + j * 128: col0 + (j + 1) * 128], uTs[:], reads=uTs.B, writes=self.UT.B)

    def norm_u(self, ht, G1, SH):
        S = self.S
        nt, ss = self.nt, self.ss
        S.op("dve", lambda e: e.tensor_tensor(out=nt[:], in0=ht[:], in1=ht[:], op=ALU.mult), reads=ht.B, writes=nt.B)
        S.op("dve", lambda e: e.reduce_sum(out=ss[:, 0:1], in_=nt[:], axis=AX.X), reads=nt.B, writes=ss.B)
        S.op("dve", lambda e: e.tensor_scalar(out=ss[:, 1:2], in0=ss[:, 0:1], scalar1=1.0 / D, scalar2=EPS,
                                              op0=ALU.mult, op1=ALU.add), reads=ss.B, writes=ss.B)
        S.op("act", lambda e: e.activation(out=ss[:, 2:3], in_=ss[:, 1:2], func=AF.Sqrt), reads=ss.B, writes=ss.B)
        S.op("dve", lambda e: e.reciprocal(out=ss[:, 3:4], in_=ss[:, 2:3]), reads=ss.B, writes=ss.B)
        S.op("dve", lambda e: e.scalar_tensor_tensor(out=nt[:], in0=ht[:], scalar=ss[:, 3:4], in1=G1[:],
                                                     op0=ALU.mult, op1=ALU.mult), reads=ht.B + ss.B + G1.B, writes=nt.B)
        S.op("dve", lambda e: e.tensor_tensor(out=self.ub[:], in0=nt[:], in1=SH[:], op=ALU.add),
             reads=nt.B + SH.B, writes=self.ub.B)

    def out_proj(self, li, stream, vT, wout, t0, ntok, dst_only_lat=True):
        S = self.S
        src = self.hsrc(li, stream)
        dst = self.hctx if stream == "ctx" else self.out
        nsub = ntok // 128
        hts = []
        for j in range(nsub):
            ht = self.ht[j % 2]
            jj = t0 // 128 + j
            sbuf = src.b[jj] if len(src.b) > 1 else src.b[0]
            S.dma("sp", ht[:], src[t0 + j * 128: t0 + (j + 1) * 128, :], reads=[sbuf], writes=ht.B)
            hts.append(ht)
        wv = wout[:].rearrange("(c p) d -> p c d", p=128)
        for dch in range(4):
            ws = []
            for half in range(2):
                w = self.next_w()
                wt = w[:].rearrange("p (k n) -> p k n", k=16)
                S.dma("sp", wt, wv[:, half * 16:(half + 1) * 16, dch * 512:(dch + 1) * 512],
                      reads=wout.b[half * 16:(half + 1) * 16], writes=w.B)
                ws.append((w, wt))
            for j in range(nsub):
                ps = self.next_ps()
                S.op("pe", [(lambda e, c=c: e.matmul(ps[:], lhsT=vT[:, c, j * 128:(j + 1) * 128], rhs=ws[c // 16][1][:, c % 16, :],
                                                     start=(c == 0), stop=(c == EC - 1))) for c in range(EC)],
                     reads=vT.B + ws[0][0].B + ws[1][0].B, writes=ps.B)
                tmp = self.tmpA[j % 2]
                cols = slice(dch * 512, (dch + 1) * 512)
                S.op("dve", lambda e: e.tensor_tensor(out=tmp[:], in0=ps[:], in1=self.GT[:, cols], op=ALU.mult),
                     reads=ps.B + self.GT.B, writes=tmp.B)
                S.op("dve", lambda e: e.tensor_tensor(out=hts[j][:, cols], in0=hts[j][:, cols], in1=tmp[:], op=ALU.add),
                     reads=tmp.B + hts[j].B, writes=hts[j].B)
        for j in range(nsub):
            jj = t0 // 128 + j
            S.dma("pool", dst[t0 + j * 128: t0 + (j + 1) * 128, :], hts[j][:], reads=hts[j].B, writes=[dst.b[jj]])

    def layer_conv(self, li, j):
        S = self.S
        TT = 256
        w_in = self.convert_weight(f"cv_in{j}", self.cv_w_in[j], D, 3 * E)
        w_out = self.convert_weight(f"cv_out{j}", self.cv_w_out[j], E, D)
        with contextlib.ExitStack() as les:
            self.alloc_commons(les)
            self.cv_uT = self.sb("cv_uT", [128, KC, TT], BF16, st=les)
            self.cv_c = self.sb("cv_c", [128, EC, TT], F32, nb=EC, st=les)
            self.cv_z = self.sb("cv_z", [128, EC, TT], BF16, nb=EC, st=les)
            self.cv_y = [self.sb(f"cv_y{i}", [128, 512], BF16, st=les) for i in range(2)]
            self.cv_sig = [self.sb(f"cv_sig{i}", [128, TT], F32, st=les) for i in range(2)]
            self.cv_sq = [self.sb(f"cv_sq{i}", [128, TT], F32, st=les) for i in range(2)]
            self.cv_dwt = self.sb("cv_dwt", [128, EC, CONVW], st=les)
            self.cv_vec = self.sb("cv_vec", [128, 3, EC], st=les)
            self.cv_st = self.sb("cv_st", [128, 4, TT], st=les)
            self._layer_conv_body(li, j, w_in, w_out, TT)
            S.barrier()

    def _layer_conv_body(self, li, j, w_in, w_out, TT):
        S = self.S
        S.dma("sp", self.cv_dwt[:], self.cv_dw[j], reads=self.cv_dw.B, writes=self.cv_dwt.B)
        for q, src in enumerate((self.cv_dwb, self.cv_lng, self.cv_lnb)):
            S.dma("sp", self.cv_vec[:, q, :], src[j], reads=src.B, writes=self.cv_vec.B)
        streams = ["ctx", "lat"] if li == 0 else ["lat"]
        for stream in streams:
            self.ada_mod(li, stream)
            self.pass1(li, stream)
            ntok = NCTX if stream == "ctx" else SEQ
            rowlen = NCTX if stream == "ctx" else 64
            col0 = UT_CTX0 if stream == "ctx" else UT_LAT0
            for yb in self.cv_y:
                S.op("dve", lambda e: e.memset(yb[:], 0.0), writes=yb.B)
            for tt in range(ntok // TT):
                self.conv_tile(li, stream, w_in, w_out, tt * TT, TT, rowlen, col0)

    def conv_tile(self, li, stream, w_in, w_out, t0, TT, rowlen, col0):
        S = self.S
        uT = self.cv_uT
        utv = self.UT[:].rearrange("(k p) t -> p k t", p=128)
        S.dma("sp", uT[:], utv[:, :, col0 + t0: col0 + t0 + TT], reads=self.UT.B, writes=uT.B)
        wv = w_in[:].rearrange("(k p) n -> p k n", p=128)
        nrow = TT // rowlen
        psS, psQ = self.next_ps(), self.next_ps()
        for g in range(EC // 4):
            wts = []
            for part in range(3):
                w = self.next_w()
                wt = w[:].rearrange("p (k n) -> p k n", k=16)
                S.dma("sp", wt, wv[:, :, part * E + g * 512: part * E + (g + 1) * 512], reads=w_in.B, writes=w.B)
                wts.append((w, wt))
            for q in range(4):
                ec = g * 4 + q
                pss = []
                for part in range(3):
                    ps = self.next_ps()
                    while ps is psS or ps is psQ:
                        ps = self.next_ps()
                    wt = wts[part][1]
                    S.op("pe", [(lambda e, k=k, ps=ps, wt=wt: e.matmul(ps[:, 0:TT], lhsT=wt[:, k, q * 128:(q + 1) * 128], rhs=uT[:, k, :],
                                                                       start=(k == 0), stop=(k == KC - 1))) for k in range(KC)],
                         reads=uT.B + wts[part][0].B, writes=ps.B)
                    pss.append(ps)
                sig, y, sq = self.cv_sig[ec % 2], self.cv_y[ec % 2], self.cv_sq[ec % 2]
                S.op("act", lambda e: e.activation(out=sig[:], in_=pss[1][:, 0:TT], func=AF.Sigmoid), reads=pss[1].B, writes=sig.B)
                PW = rowlen + 30
                yp3 = y[:, 0:nrow * PW].rearrange("p (r t) -> p r t", r=nrow)
                r3 = lambda ap: ap.rearrange("p (r t) -> p r t", r=nrow)
                S.op("dve", lambda e: e.tensor_tensor(out=yp3[:, :, 15:15 + rowlen], in0=r3(pss[0][:, 0:TT]), in1=r3(sig[:]), op=ALU.mult),
                     reads=pss[0].B + sig.B, writes=y.B)
                S.op("act", lambda e: e.activation(out=self.cv_z[:, ec, :], in_=pss[2][:, 0:TT], func=AF.Silu),
                     reads=pss[2].B, writes=[self.cv_z.b[ec]])
                dgT = (self.G1, self.SH)[ec % 2]
                dg = dgT[:].bitcast(BF16)[:, 0:CONVW * 128].rearrange("p (t m) -> p t m", t=CONVW)
                ia = self.ident[:]
                ident_bc = AP(ia.tensor, ia.offset, [list(ia.ap[0]), [0, CONVW], list(ia.ap[1])])
                S.op("dve", lambda e: e.tensor_tensor(out=dg, in0=ident_bc, in1=bcast_free(self.cv_dwt[:, ec, :], 128), op=ALU.mult),
                     reads=self.ident.B + self.cv_dwt.B, writes=dgT.B)
                psC = self.next_ps()
                while psC is psS or psC is psQ:
                    psC = self.next_ps()
                S.op("pe", [(lambda e, tap=tap: e.matmul(r3(psC[:, 0:TT]), lhsT=dg[:, tap, :], rhs=yp3[:, :, tap:tap + rowlen],
                                                         start=(tap == 0), stop=(tap == CONVW - 1))) for tap in range(CONVW)],
                     reads=dgT.B + y.B, writes=psC.B)
                cb = self.cv_c.b[ec]
                cflat = self.cv_c[:, ec, :]
                S.op("dve", lambda e: e.tensor_scalar(out=cflat, in0=psC[:, 0:TT], scalar1=self.cv_vec[:, 0, ec:ec + 1], scalar2=None, op0=ALU.add),
                     reads=psC.B + self.cv_vec.B, writes=[cb])
                S.op("act", lambda e: e.activation(out=sq[:], in_=cflat, func=AF.Square), reads=[cb], writes=sq.B)
                S.op("pe", lambda e: e.matmul(psS[:, 0:TT], lhsT=self.ones_f[:], rhs=cflat, start=(ec == 0), stop=(ec == EC - 1)),
                     reads=[cb] + self.ones_f.B, writes=psS.B)
                S.op("pe", lambda e: e.matmul(psQ[:, 0:TT], lhsT=self.ones_f[:], rhs=sq[:], start=(ec == 0), stop=(ec == EC - 1)),
                     reads=sq.B + self.ones_f.B, writes=psQ.B)
        st = self.cv_st
        S.op("act", lambda e: e.mul(out=st[:, 0, :], in_=psS[:, 0:TT], mul=1.0 / E), reads=psS.B, writes=st.B)
        S.op("act", lambda e: e.mul(out=st[:, 1, :], in_=psQ[:, 0:TT], mul=1.0 / E), reads=psQ.B, writes=st.B)
        S.op("dve", lambda e: e.tensor_tensor(out=st[:, 2, :], in0=st[:, 0, :], in1=st[:, 0, :], op=ALU.mult), reads=st.B, writes=st.B)
        S.op("dve", lambda e: e.tensor_tensor(out=st[:, 1, :], in0=st[:, 1, :], in1=st[:, 2, :], op=ALU.subtract), reads=st.B, writes=st.B)
        S.op("dve", lambda e: e.tensor_scalar(out=st[:, 1, :], in0=st[:, 1, :], scalar1=EPS, scalar2=None, op0=ALU.add), reads=st.B, writes=st.B)
        S.op("act", lambda e: e.activation(out=st[:, 2, :], in_=st[:, 1, :], func=AF.Sqrt), reads=st.B, writes=st.B)
        S.op("dve", lambda e: e.reciprocal(out=st[:, 3, :], in_=st[:, 2, :]), reads=st.B, writes=st.B)
        for ec in range(EC):
            cb = self.cv_c.b[ec]
            cflat = self.cv_c[:, ec, :]
            zb = self.cv_z.b[ec]
            S.op("dve", lambda e: e.tensor_tensor(out=cflat, in0=cflat, in1=st[:, 0, :], op=ALU.subtract), reads=[cb] + st.B, writes=[cb])
            S.op("dve", lambda e: e.tensor_tensor(out=cflat, in0=cflat, in1=st[:, 3, :], op=ALU.mult), reads=[cb] + st.B, writes=[cb])
            S.op("act", lambda e: e.activation(out=cflat, in_=cflat, func=AF.Silu, scale=self.cv_vec[:, 1, ec:ec + 1],
                                               bias=self.cv_vec[:, 2, ec:ec + 1]), reads=[cb] + self.cv_vec.B, writes=[cb])
            S.op("dve", lambda e: e.tensor_tensor(out=self.cv_z[:, ec, :], in0=self.cv_z[:, ec, :], in1=cflat, op=ALU.mult),
                 reads=[cb, zb], writes=[zb])
        self.out_proj(li, stream, self.cv_z, w_out, t0, TT)

    def final_norm(self):
        S = self.S
        src = self.x if len(self.layers) == 0 else self.out
        with contextlib.ExitStack() as les:
            self.alloc_commons(les)
            self._final_body()
            S.barrier()

    def _final_body(self):
        S = self.S
        src = self.x if len(self.layers) == 0 else self.out
        gb = self.G1
        S.dma("sp", gb[:], self.final_g[:].partition_broadcast(128), reads=self.final_g.B, writes=gb.B)
        for j in range(SEQ // 128):
            ht = self.ht[j % 2]
            sbuf = src.b[j] if len(src.b) > 1 else src.b[0]
            S.dma("sp", ht[:], src[j * 128:(j + 1) * 128, :], reads=[sbuf], writes=ht.B)
            nt, ss = self.nt, self.ss
            S.op("dve", lambda e: e.tensor_tensor(out=nt[:], in0=ht[:], in1=ht[:], op=ALU.mult), reads=ht.B, writes=nt.B)
            S.op("dve", lambda e: e.reduce_sum(out=ss[:, 0:1], in_=nt[:], axis=AX.X), reads=nt.B, writes=ss.B)
            S.op("dve", lambda e: e.tensor_scalar(out=ss[:, 1:2], in0=ss[:, 0:1], scalar1=1.0 / D, scalar2=EPS,
                                                  op0=ALU.mult, op1=ALU.add), reads=ss.B, writes=ss.B)
            S.op("act", lambda e: e.activation(out=ss[:, 2:3], in_=ss[:, 1:2], func=AF.Sqrt), reads=ss.B, writes=ss.B)
            S.op("dve", lambda e: e.reciprocal(out=ss[:, 3:4], in_=ss[:, 2:3]), reads=ss.B, writes=ss.B)
            S.op("dve", lambda e: e.scalar_tensor_tensor(out=ht[:], in0=ht[:], scalar=ss[:, 3:4], in1=gb[:],
                                                         op0=ALU.mult, op1=ALU.mult), reads=ht.B + ss.B + gb.B, writes=ht.B)
            S.dma("pool", self.out[j * 128:(j + 1) * 128, :], ht[:], reads=ht.B, writes=[self.out.b[j]])

    def copy_out(self):
        pass


    def layer_mlstm(self, li, j):
        S = self.S
        TS = NCTX + SEQ
        NCH = TS // LCH
        w_in = self.convert_weight("ml_in", self.ml_w_in[:], D, 2 * E)
        w_out = self.convert_weight("ml_out", self.ml_w_out[:], E, D)
        bdb = self.convert_weight("ml_bd", self.ml_bd[:], 4 * EC * 128, 128)
        ds = self.dram_scratch
        qTd, kTd = ds("ml_qT", [E, TS], BF16), ds("ml_kT", [E, TS], BF16)
        ktmd, vtmd = ds("ml_ktm", [TS, E], BF16), ds("ml_vtm", [TS, E], BF16)
        P1d, P2d = ds("ml_P1", [E, TS], BF16), ds("ml_P2", [E, TS], BF16)
        Hd, HNd = ds("ml_H", [TS, E], F32), ds("ml_HN", [TS, E], BF16)
        GPd = ds("ml_GP", [4, 8, TS], F32)
        GQd = ds("ml_GQ", [2, 5, 8, TS], F32)
        DQd = ds("ml_DQ", [2, 8, NCH], F32)
        S.barrier()
        with contextlib.ExitStack() as les:
            self.alloc_commons(les)
            for stream in ("ctx", "lat"):
                self.ada_mod(li, stream)
                self.pass1(li, stream)
            if getattr(self, "ml_stop", 9) > 0.5:
                self.ml_proj(les, w_in, bdb, qTd, kTd, ktmd, vtmd, P1d, P2d, GPd)
            S.barrier()
        stop = getattr(self, "ml_stop", 9)
        if stop <= 1:
            return
        self.ml_gates(GPd, GQd, DQd)
        S.barrier()
        if stop <= 2:
            return
        self.ml_scan(qTd, kTd, ktmd, vtmd, Hd, HNd, GQd, DQd)
        S.barrier()
        if stop <= 3:
            return
        with contextlib.ExitStack() as les:
            self.alloc_commons(les)
            self.ada_mod(li, "lat")
            sb = lambda n, sh, dt=F32: self.sb(n, sh, dt, st=les)
            vT, p1, p2 = sb("mo_vT", [128, EC, 256], BF16), sb("mo_p1", [128, EC, 256], BF16), sb("mo_p2", [128, EC, 256], BF16)
            hn = [sb(f"mo_hn{i}", [128, E], BF16) for i in range(2)]
            for tt in range(SEQ // 256):
                c0 = NCTX + tt * 256
                S.dma("sp", p1[:], P1d[:, c0:c0 + 256].rearrange("(c p) t -> p c t", p=128), reads=P1d.B, writes=p1.B)
                S.dma("sp", p2[:], P2d[:, c0:c0 + 256].rearrange("(c p) t -> p c t", p=128), reads=P2d.B, writes=p2.B)
                for sub in range(2):
                    y = hn[sub]
                    S.dma("sp", y[:], HNd[c0 + sub * 128:c0 + (sub + 1) * 128, :], reads=HNd.B, writes=y.B)
                    for g in range(EC // 4):
                        S.op("pe", [(lambda e, q=q: e.transpose(self.psb[:, q * 128:(q + 1) * 128],
                                                                y[:, (g * 4 + q) * 128:(g * 4 + q + 1) * 128], self.ident[:]))
                                    for q in range(4)], reads=y.B + self.ident.B, writes=self.psb.B)
                        S.op("act", lambda e: e.copy(out=vT[:, g * 4:(g + 1) * 4, sub * 128:(sub + 1) * 128],
                                                     in_=self.psb[:, 0:512].rearrange("p (q t) -> p q t", q=4)),
                             reads=self.psb.B, writes=vT.B)
                S.op("dve", lambda e: e.tensor_tensor(out=vT[:], in0=vT[:], in1=p1[:], op=ALU.mult), reads=vT.B + p1.B, writes=vT.B)
                S.op("dve", lambda e: e.tensor_tensor(out=vT[:], in0=vT[:], in1=p2[:], op=ALU.add), reads=vT.B + p2.B, writes=vT.B)
                self.out_proj(li, "lat", vT, w_out, tt * 256, 256)
            S.barrier()

    def ml_proj(self, les, w_in, bdb, qTd, kTd, ktmd, vtmd, P1d, P2d, GPd):
        S = self.S
        TT = 256
        sb = lambda n, sh, dt=F32: self.sb(n, sh, dt, st=les)
        uT = sb("mp_uT", [128, KC, TT + 2], BF16)
        ktm, vtm = sb("mp_ktm", [128, 2, E], BF16), sb("mp_vtm", [128, 2, E], BF16)
        wgf, wg = sb("mp_wgf", [128, 4 * 96 * 8]), sb("mp_wg", [128, 4, 96, 8], BF16)
        bd = [sb(f"mp_bd{i}", [128, 4, 128], BF16) for i in range(2)]
        cw, vec = sb("mp_cw", [128, EC, 3]), sb("mp_vec", [128, 3, EC])
        bg = sb("mp_bg", [8, 8])
        a_, xc, zs, og = sb("mp_a", [128, TT]), sb("mp_xc", [128, TT]), sb("mp_zs", [128, TT]), sb("mp_o", [128, TT])
        xmb, xcb = sb("mp_xmb", [128, TT], BF16), sb("mp_xcb", [128, TT], BF16)
        qb, kb, vb = sb("mp_qb", [128, TT], BF16), sb("mp_kb", [128, TT], BF16), sb("mp_vb", [128, TT], BF16)
        p1, p2 = sb("mp_p1", [128, TT], BF16), sb("mp_p2", [128, TT], BF16)
        gst = sb("mp_gst", [8, 4, TT])
        S.dma("sp", wgf[:], self.ml_wg[:], reads=self.ml_wg.B, writes=wgf.B)
        S.op("dve", lambda e: e.tensor_copy(out=wg[:].rearrange("p a b c -> p (a b c)"), in_=wgf[:]), reads=wgf.B, writes=wg.B)
        S.dma("sp", cw[:], self.ml_cw[:], reads=self.ml_cw.B, writes=cw.B)
        S.dma("sp", vec[:], self.ml_vec[:], reads=self.ml_vec.B, writes=vec.B)
        S.dma("sp", bg[:, 0:4], self.ml_bg[:], reads=self.ml_bg.B, writes=bg.B)
        utv = self.UT[:].rearrange("(k p) t -> p k t", p=128)
        wv = w_in[:].rearrange("(k p) n -> p k n", p=128)
        bdv = bdb[:].rearrange("(a c p) m -> c p a m", a=4, c=EC)
        psGa, psGb = self.next_ps(), self.next_ps()
        held = (psGa, psGb)

        def nps():
            p = self.next_ps()
            while p in held:
                p = self.next_ps()
            return p

        for tt in range(1 + SEQ // TT):
            uc0 = UT_CTX0 - 1 if tt == 0 else UT_LAT0 + (tt - 1) * TT - 1
            tok0 = 0 if tt == 0 else NCTX + (tt - 1) * TT
            S.dma("sp", uT[:], utv[:, :, uc0:uc0 + TT + 2], reads=self.UT.B, writes=uT.B)
            for g in range(EC // 4):
                wts = []
                for part in range(2):
                    w = self.next_w()
                    wt = w[:].rearrange("p (k n) -> p k n", k=16)
                    S.dma("sp", wt, wv[:, :, part * E + g * 512: part * E + (g + 1) * 512], reads=w_in.B, writes=w.B)
                    wts.append((w, wt))
                for q in range(4):
                    ec = g * 4 + q
                    b_ = bd[ec % 2]
                    if getattr(self, "ml_stop", 9) > 0.62:
                        S.dma("sp", b_[:], bdv[ec], reads=bdb.B, writes=b_.B)
                    pss = []
                    for part in range(2):
                        ps = nps()
                        wt = wts[part][1]
                        S.op("pe", [(lambda e, k=k, ps=ps, wt=wt: e.matmul(ps[:, 0:TT + 2], lhsT=wt[:, k, q * 128:(q + 1) * 128], rhs=uT[:, k, :],
                                                                           start=(k == 0), stop=(k == KC - 1))) for k in range(KC)],
                             reads=uT.B + wts[part][0].B, writes=ps.B)
                        pss.append(ps)
                    pxm, pz = pss
                    if getattr(self, "ml_stop", 9) < 0.606:
                        continue
                    lv = getattr(self, "ml_stop", 9)
                    S.op("dve", lambda e: e.tensor_copy(out=xmb[:], in_=pxm[:, 1:TT + 1]), reads=pxm.B, writes=xmb.B)
                    if lv < 0.6075:
                        continue
                    S.op("dve", lambda e: e.tensor_scalar(out=a_[:], in0=pxm[:, 1:TT + 1], scalar1=cw[:, ec, 1:2], scalar2=vec[:, 0, ec:ec + 1],
                                                          op0=ALU.mult, op1=ALU.add), reads=pxm.B + cw.B + vec.B, writes=a_.B)
                    S.op("dve", lambda e: e.scalar_tensor_tensor(out=a_[:], in0=pxm[:, 0:TT], scalar=cw[:, ec, 0:1], in1=a_[:],
                                                                 op0=ALU.mult, op1=ALU.add), reads=pxm.B + cw.B + a_.B, writes=a_.B)
                    S.op("dve", lambda e: e.scalar_tensor_tensor(out=a_[:], in0=pxm[:, 2:TT + 2], scalar=cw[:, ec, 2:3], in1=a_[:],
                                                                 op0=ALU.mult, op1=ALU.add), reads=pxm.B + cw.B + a_.B, writes=a_.B)
                    if lv < 0.6085:
                        continue
                    S.op("act", lambda e: e.activation(out=xc[:], in_=a_[:], func=AF.Silu), reads=a_.B, writes=xc.B)
                    S.op("act", lambda e: e.mul(out=xcb[:], in_=xc[:], mul=1.0), reads=xc.B, writes=xcb.B)
                    if lv < 0.6095:
                        continue
                    S.op("act", lambda e: e.activation(out=zs[:], in_=pz[:, 1:TT + 1], func=AF.Silu), reads=pz.B, writes=zs.B)
                    lvl = getattr(self, "ml_stop", 9)
                    if lvl < 0.65:
                        continue
                    pq, pk, pv, po = nps(), nps(), nps(), nps()
                    for idx, (pp, src) in enumerate(((pq, xcb), (pk, xcb), (pv, xmb), (po, xcb))):
                        S.op("pe", lambda e: e.matmul(pp[:, 0:TT], lhsT=b_[:, idx, :], rhs=src[:], start=True, stop=True),
                             reads=b_.B + src.B, writes=pp.B)
                    S.op("act", lambda e: e.mul(out=qb[:], in_=pq[:, 0:TT], mul=1.0), reads=pq.B, writes=qb.B)
                    S.op("dve", lambda e: e.tensor_copy(out=kb[:], in_=pk[:, 0:TT]), reads=pk.B, writes=kb.B)
                    S.op("act", lambda e: e.mul(out=vb[:], in_=pv[:, 0:TT], mul=1.0), reads=pv.B, writes=vb.B)
                    S.op("act", lambda e: e.activation(out=og[:], in_=po[:, 0:TT], func=AF.Sigmoid, bias=vec[:, 1, ec:ec + 1]),
                         reads=po.B + vec.B, writes=og.B)
                    S.op("dve", lambda e: e.tensor_tensor(out=p1[:], in0=og[:], in1=zs[:], op=ALU.mult), reads=og.B + zs.B, writes=p1.B)
                    S.op("dve", lambda e: e.scalar_tensor_tensor(out=p2[:], in0=xc[:], scalar=vec[:, 2, ec:ec + 1], in1=zs[:],
                                                                 op0=ALU.mult, op1=ALU.mult), reads=xc.B + vec.B + zs.B, writes=p2.B)
                    for grp in range(4 if lvl > 0.85 else 0):
                        pg = psGa if grp < 2 else psGb
                        cs_ = slice((grp % 2) * TT, (grp % 2 + 1) * TT)
                        S.op("pe", [(lambda e, i3=i3, src=src: e.matmul(pg[0:8, cs_], lhsT=wg[:, grp, i3 * 32 + ec, :], rhs=src[:],
                                                                       start=(ec == 0 and i3 == 0), stop=(ec == EC - 1 and i3 == 2)))
                                    for i3, src in enumerate((qb, kb, vb))], reads=wg.B + qb.B + kb.B + vb.B, writes=pg.B)
                    if lvl < 0.75:
                        continue
                    rows = slice(ec * 128, (ec + 1) * 128)
                    S.dma("pool", qTd[rows, tok0:tok0 + TT], qb[:], reads=qb.B, writes=qTd.B)
                    S.dma("pool", kTd[rows, tok0:tok0 + TT], kb[:], reads=kb.B, writes=kTd.B)
                    S.dma("pool", P1d[rows, tok0:tok0 + TT], p1[:], reads=p1.B, writes=P1d.B)
                    S.dma("pool", P2d[rows, tok0:tok0 + TT], p2[:], reads=p2.B, writes=P2d.B)
                    S.op("pe", [(lambda e, q2=q2: e.transpose(self.psb[:, q2 * 128:(q2 + 1) * 128],
                                                              (kb if q2 < 2 else vb)[:, (q2 % 2) * 128:(q2 % 2 + 1) * 128], self.ident[:]))
                                for q2 in range(4)], reads=kb.B + vb.B + self.ident.B, writes=self.psb.B)
                    S.op("act", lambda e: e.copy(out=ktm[:, :, ec * 128:(ec + 1) * 128],
                                                 in_=self.psb[:, 0:256].rearrange("p (s t) -> p s t", s=2)), reads=self.psb.B, writes=ktm.B)
                    S.op("act", lambda e: e.copy(out=vtm[:, :, ec * 128:(ec + 1) * 128],
                                                 in_=self.psb[:, 256:512].rearrange("p (s t) -> p s t", s=2)), reads=self.psb.B, writes=vtm.B)
            if getattr(self, "ml_stop", 9) < 0.75:
                continue
            for sub in range(2):
                r0 = tok0 + sub * 128
                S.dma("pool", ktmd[r0:r0 + 128, :], ktm[:, sub, :], reads=ktm.B, writes=ktmd.B)
                S.dma("pool", vtmd[r0:r0 + 128, :], vtm[:, sub, :], reads=vtm.B, writes=vtmd.B)
            if getattr(self, "ml_stop", 9) < 0.85:
                continue
            for grp in range(4):
                pg = psGa if grp < 2 else psGb
                cs_ = slice((grp % 2) * TT, (grp % 2 + 1) * TT)
                S.op("dve", lambda e: e.tensor_scalar(out=gst[:, grp, :], in0=pg[0:8, cs_], scalar1=bg[:, grp:grp + 1], scalar2=None, op0=ALU.add),
                     reads=pg.B + bg.B, writes=gst.B)
            S.dma("pool", GPd[:, :, tok0:tok0 + TT].rearrange("g h t -> h g t"), gst[:], reads=gst.B, writes=GPd.B)

    def ml_gates(self, GPd, GQd, DQd):
        S = self.S
        TS = NCTX + SEQ
        NCH = TS // LCH
        LNS = math.log(DH ** -0.5)
        with contextlib.ExitStack() as st:
            sb = lambda n, sh, dt=F32: self.sb(n, sh, dt, st=st)
            li, fp, t0, t1 = sb("mg_li", [8, TS]), sb("mg_fp", [8, TS]), sb("mg_t0", [8, TS]), sb("mg_t1", [8, TS])
            ones, Bc, Gg, Mx = sb("mg_one", [8, TS]), sb("mg_B", [8, TS]), sb("mg_Gg", [8, TS]), sb("mg_Mx", [8, TS])
            res = sb("mg_res", [8, TS])
            mp_, me_, dc_ = sb("mg_mp", [8, 72]), sb("mg_me", [8, 72]), sb("mg_dc", [8, 72])
            mp, me, dc = V(mp_, NCH), V(me_, NCH), V(dc_, NCH)
            S.op("dve", lambda e: e.memset(ones[:], 1.0), writes=ones.B)
            segs = ((0, NCTX), (NCTX, SEQ))

            def rev(dst, src):
                for (o, n) in segs:
                    a = src[:, o:o + n]
                    r = AP(a.tensor, a.offset + n - 1, [list(a.ap[0]), [-1, n]])
                    S.op("dve", lambda e: e.tensor_copy(out=dst[:, o:o + n], in_=r), reads=src.B, writes=dst.B)

            def c3(t):
                return t[:].rearrange("p (c l) -> p c l", l=LCH)

            def bc3(t):
                a = t[:]
                return AP(a.tensor, a.offset, [list(a.ap[0]), list(a.ap[1]), [0, LCH]])

            for d in range(2):
                S.dma("sp", t0[:], GPd[2 * d], reads=GPd.B, writes=t0.B)
                S.dma("sp", t1[:], GPd[2 * d + 1], reads=GPd.B, writes=t1.B)
                if d == 0:
                    S.op("dve", lambda e: e.tensor_copy(out=li[:], in_=t0[:]), reads=t0.B, writes=li.B)
                    S.op("dve", lambda e: e.tensor_copy(out=fp[:], in_=t1[:]), reads=t1.B, writes=fp.B)
                else:
                    rev(li, t0)
                    rev(fp, t1)
                S.op("act", lambda e: e.activation(out=t0[:], in_=fp[:], func=AF.Exp, scale=-1.0), reads=fp.B, writes=t0.B)
                S.op("act", lambda e: e.activation(out=t1[:], in_=t0[:], func=AF.Ln, bias=1.0), reads=t0.B, writes=t1.B)
                S.op("dve", lambda e: e.tensor_scalar(out=t1[:], in0=t1[:], scalar1=-1.0, scalar2=None, op0=ALU.mult), reads=t1.B, writes=t1.B)
                S.op("dve", lambda e: e.tensor_tensor_scan(out=Bc[:], data0=ones[:], data1=t1[:], initial=0.0, op0=ALU.mult, op1=ALU.add),
                     reads=ones.B + t1.B, writes=Bc.B)
                S.op("dve", lambda e: e.tensor_tensor(out=Gg[:], in0=li[:], in1=Bc[:], op=ALU.subtract), reads=li.B + Bc.B, writes=Gg.B)
                S.op("dve", lambda e: e.tensor_tensor_scan(out=Mx[:], data0=Gg[:], data1=Gg[:], initial=-1e30, op0=ALU.max, op1=ALU.max),
                     reads=Gg.B, writes=Mx.B)
                S.op("dve", lambda e: e.tensor_copy(out=me[:], in_=c3(Mx)[:, :, LCH - 1]), reads=Mx.B, writes=me.B)
                S.op("dve", lambda e: e.memset(mp[:, 0:1], -1e30), writes=mp.B)
                S.op("dve", lambda e: e.tensor_copy(out=mp[:, 1:NCH], in_=me[:, 0:NCH - 1]), reads=me.B, writes=mp.B)
                S.op("dve", lambda e: e.tensor_tensor(out=dc[:], in0=mp[:], in1=me[:], op=ALU.subtract), reads=mp.B + me.B, writes=dc.B)
                S.op("act", lambda e: e.activation(out=dc[:], in_=dc[:], func=AF.Exp), reads=dc.B, writes=dc.B)
                S.dma("pool", DQd[d], dc[:], reads=dc.B, writes=DQd.B)

                def emit(qi, src):
                    if d == 0:
                        S.dma("pool", GQd[d, qi], src[:], reads=src.B, writes=GQd.B)
                    else:
                        rev(res, src)
                        S.dma("pool", GQd[d, qi], res[:], reads=res.B, writes=GQd.B)

                S.op("dve", lambda e: e.tensor_scalar(out=t0[:], in0=Gg[:], scalar1=LNS, scalar2=None, op0=ALU.add), reads=Gg.B, writes=t0.B)
                emit(0, t0)
                emit(1, Mx)
                S.op("dve", lambda e: e.tensor_tensor(out=c3(t0), in0=bc3(mp), in1=c3(Mx), op=ALU.subtract), reads=mp.B + Mx.B, writes=t0.B)
                S.op("act", lambda e: e.activation(out=t0[:], in_=t0[:], func=AF.Exp, bias=LNS), reads=t0.B, writes=t0.B)
                emit(2, t0)
                S.op("dve", lambda e: e.tensor_tensor(out=t1[:], in0=Bc[:], in1=Mx[:], op=ALU.add), reads=Bc.B + Mx.B, writes=t1.B)
                S.op("act", lambda e: e.activation(out=t1[:], in_=t1[:], func=AF.Exp, scale=-1.0), reads=t1.B, writes=t1.B)
                emit(3, t1)
                S.op("dve", lambda e: e.tensor_tensor(out=c3(t0), in0=c3(Gg), in1=bc3(me), op=ALU.subtract), reads=me.B + Gg.B, writes=t0.B)
                S.op("act", lambda e: e.activation(out=t0[:], in_=t0[:], func=AF.Exp), reads=t0.B, writes=t0.B)
                emit(4, t0)
            S.barrier()

    def ml_scan(self, qTd, kTd, ktmd, vtmd, Hd, HNd, GQd, DQd):
        S = self.S
        TS = NCTX + SEQ
        NCH = TS // LCH
        NG = NCH // 4
        NSL = 2
        with contextlib.ExitStack() as st:
            sb = lambda n, sh, dt=F32: self.sb(n, sh, dt, st=st)
            mask = [sb(f"ms_mask{i}", [64, 64]) for i in range(2)]
            onesb = V(sb("ms_1b", [64, 16], BF16), 1)
            S.dma("sp", mask[0][:], self.ml_mask[0], reads=self.ml_mask.B, writes=mask[0].B)
            S.dma("sp", mask[1][:], self.ml_mask[1], reads=self.ml_mask.B, writes=mask[1].B)
            S.op("dve", lambda e: e.memset(onesb[:], 1.0), writes=onesb.B)
            slots = []
            for si in range(NSL):
                o = {}
                n_ = lambda x: f"ms{si}_{x}"
                o["C"], o["Cb"] = sb(n_("C"), [128, 4, DH]), sb(n_("Cb"), [128, 4, DH], BF16)
                o["nn"], o["nb"] = V(sb(n_("n"), [128, 8]), 4), V(sb(n_("nb"), [128, 16], BF16), 4)
                o["cols"] = [V(sb(n_(f"col{i}"), [64, 72]), NCH) for i in range(4)]
                o["mxr"], o["dcr"] = sb(n_("mxr"), [64, TS]), V(sb(n_("dcr"), [128, 72]), NCH)
                o["mhg"] = sb(n_("mhg"), [64, DH])
                for nm in ("qT", "kT"):
                    o[nm] = [sb(n_(f"{nm}{i}"), [128, 4, 256], BF16) for i in range(2)]
                for nm in ("ktm", "vtm"):
                    o[nm] = [sb(n_(f"{nm}{i}"), [64, 4, DH], BF16) for i in range(2)]
                o["DT"], o["DTm"], o["STb"] = sb(n_("DT"), [64, 64]), sb(n_("DTm"), [64, 64]), sb(n_("STb"), [64, 64], BF16)
                for nm in ("t1", "num", "hc", "hf", "sq"):
                    o[nm] = sb(n_(nm), [64, DH])
                o["hnb"], o["kw"] = sb(n_("hnb"), [64, DH], BF16), sb(n_("kw"), [64, DH], BF16)
                o["sm"] = sb(n_("sm"), [64, 16])
                slots.append(o)
            for d in range(2):
                if d == 1:
                    S.barrier()
                gorder = list(range(NG)) if d == 0 else [0] + list(range(NG - 1, 0, -1))
                for hp in range(0, NH, NSL):
                    for si in range(NSL):
                        o, h = slots[si], hp + si
                        for qi in range(4):
                            src = GQd[d, (0, 2, 3, 4)[qi], h].rearrange("(c l) -> l c", l=LCH)
                            S.dma("sp", o["cols"][qi][:], src, reads=GQd.B, writes=o["cols"][qi].B, allow_slow_non_contiguous=True)
                        S.dma("sp", o["mxr"][:], GQd[d, 1, h].partition_broadcast(64), reads=GQd.B, writes=o["mxr"].B)
                        S.dma("sp", o["dcr"][:], DQd[d, h].partition_broadcast(128), reads=DQd.B, writes=o["dcr"].B)
                        if d == 1:
                            S.dma("sp", o["mhg"][:], self.ml_mhg[h * DH:(h + 1) * DH].partition_broadcast(64), reads=self.ml_mhg.B, writes=o["mhg"].B)
                        S.op("dve", lambda e: e.memset(o["C"][:], 0.0), writes=o["C"].B)
                        S.op("dve", lambda e: e.memset(o["nn"][:], 0.0), writes=o["nn"].B)
                        S.op("act", lambda e: e.mul(out=o["Cb"][:], in_=o["C"][:], mul=1.0), reads=o["C"].B, writes=o["Cb"].B)
                        S.op("act", lambda e: e.mul(out=o["nb"][:], in_=o["nn"][:], mul=1.0), reads=o["nn"].B, writes=o["nb"].B)
                    cp = 0
                    for gi_n, grp in enumerate(gorder):
                        tg0 = grp * 256
                        b_ = gi_n % 2
                        for si in range(NSL):
                            o, h = slots[si], hp + si
                            hrows = slice(h * DH, (h + 1) * DH)
                            S.dma("sp", o["qT"][b_][:], qTd[hrows, tg0:tg0 + 256].rearrange("(c p) t -> p c t", p=128), reads=qTd.B, writes=o["qT"][b_].B)
                            S.dma("sp", o["kT"][b_][:], kTd[hrows, tg0:tg0 + 256].rearrange("(c p) t -> p c t", p=128), reads=kTd.B, writes=o["kT"][b_].B)
                            S.dma("sp", o["ktm"][b_][:], ktmd[tg0:tg0 + 256, hrows].rearrange("(c l) e -> l c e", l=LCH), reads=ktmd.B, writes=o["ktm"][b_].B)
                            S.dma("sp", o["vtm"][b_][:], vtmd[tg0:tg0 + 256, hrows].rearrange("(c l) e -> l c e", l=LCH), reads=vtmd.B, writes=o["vtm"][b_].B)
                        for ci in (range(4) if d == 0 else range(3, -1, -1)):
                            for si in range(NSL):
                                self.ml_unit(slots[si], d, hp + si, grp * 4 + ci, ci, b_, cp, mask, onesb, Hd, HNd)
                            cp += 1
            S.barrier()

    def ml_unit(self, o, d, h, c, ci, b_, cp, mask, onesb, Hd, HNd):
        S = self.S
        tok0 = c * LCH
        lc = slice(ci * LCH, (ci + 1) * LCH)
        hrows = slice(h * DH, (h + 1) * DH)
        qT, kT, ktm, vtm = o["qT"][b_], o["kT"][b_], o["ktm"][b_], o["vtm"][b_]
        C, Cb, nn, nb, cols, mxr, dcr, mhg = o["C"], o["Cb"], o["nn"], o["nb"], o["cols"], o["mxr"], o["dcr"], o["mhg"]
        DT, DTm, STb, t1, num, hc, hf, sq, hnb, kw, sm = (o[k] for k in ("DT", "DTm", "STb", "t1", "num", "hc", "hf", "sq", "hnb", "kw", "sm"))
        ktc, vtc = ktm[:, ci, :], vtm[:, ci, :]
        pST, pP1, pP2, pDN = self.next_ps(), self.next_ps(), self.next_ps(), self.next_ps()
        S.op("pe", [(lambda e, k=k: e.matmul(pST[0:64, 0:64], lhsT=kT[:, k, lc], rhs=qT[:, k, lc],
                                             start=(k == 0), stop=(k == 3))) for k in range(4)],
             reads=kT.B + qT.B, writes=pST.B)
        S.op("act", lambda e: e.activation(out=DT[:], in_=mxr[:, tok0:tok0 + LCH], func=AF.Exp, scale=-1.0,
                                           bias=cols[0][:, c:c + 1]), reads=mxr.B + cols[0].B, writes=DT.B)
        S.op("pool", lambda e: e.tensor_tensor(out=DTm[:], in0=DT[:], in1=mask[d][:], op=ALU.mult),
             reads=DT.B + mask[d].B, writes=DTm.B)
        S.op("dve", lambda e: e.tensor_tensor(out=STb[:], in0=pST[0:64, 0:64], in1=DTm[:], op=ALU.mult),
             reads=pST.B + DTm.B, writes=STb.B)
        S.op("pe", [(lambda e, k=k: e.matmul(pP1[0:64, :], lhsT=qT[:, k, lc], rhs=Cb[:, k, :],
                                             start=(k == 0), stop=(k == 3))) for k in range(4)],
             reads=qT.B + Cb.B, writes=pP1.B)
        S.op("pe", lambda e: e.matmul(pP2[0:64, :], lhsT=STb[:], rhs=vtc, start=True, stop=True),
             reads=STb.B + vtm.B, writes=pP2.B)
        S.op("pe", [(lambda e, k=k: e.matmul(pDN[0:64, 0:1], lhsT=qT[:, k, lc], rhs=nb[:, k:k + 1],
                                             start=(k == 0), stop=(k == 3))) for k in range(4)]
             + [lambda e: e.matmul(pDN[0:64, 1:2], lhsT=STb[:], rhs=onesb[:], start=True, stop=True)],
             reads=qT.B + nb.B + STb.B + onesb.B, writes=pDN.B)
        acol, fcol, wcol = cols[1][:, c:c + 1], cols[2][:, c:c + 1], cols[3][:, c:c + 1]
        S.op("act", lambda e: e.mul(out=t1[:], in_=pP1[0:64, :], mul=acol), reads=pP1.B + cols[1].B, writes=t1.B)
        S.op("dve", lambda e: e.tensor_tensor(out=num[:], in0=pP2[0:64, :], in1=t1[:], op=ALU.add),
             reads=pP2.B + t1.B, writes=num.B)
        S.op("act", lambda e: e.copy(out=sm[:, 0:2], in_=pDN[0:64, 0:2]), reads=pDN.B, writes=sm.B)
        S.op("dve", lambda e: e.scalar_tensor_tensor(out=sm[:, 2:3], in0=sm[:, 0:1], scalar=acol, in1=sm[:, 1:2],
                                                     op0=ALU.mult, op1=ALU.add), reads=sm.B + cols[1].B, writes=sm.B)
        S.op("dve", lambda e: e.tensor_scalar(out=sm[:, 12:13], in0=sm[:, 2:3], scalar1=-1.0, scalar2=None, op0=ALU.mult),
             reads=sm.B, writes=sm.B)
        S.op("dve", lambda e: e.tensor_tensor(out=sm[:, 3:4], in0=sm[:, 2:3], in1=sm[:, 12:13], op=ALU.max), reads=sm.B, writes=sm.B)
        S.op("dve", lambda e: e.tensor_tensor(out=sm[:, 3:4], in0=sm[:, 3:4], in1=fcol, op=ALU.max),
             reads=sm.B + cols[2].B, writes=sm.B)
        S.op("dve", lambda e: e.reciprocal(out=sm[:, 4:5], in_=sm[:, 3:4]), reads=sm.B, writes=sm.B)
        S.op("dve", lambda e: e.tensor_scalar(out=hc[:], in0=num[:], scalar1=sm[:, 4:5], scalar2=None, op0=ALU.mult),
             reads=num.B + sm.B, writes=hc.B)
        if d == 0:
            S.dma("pool", Hd[tok0:tok0 + LCH, hrows], hc[:], reads=hc.B, writes=[Buf()])
        else:
            S.dma("sp", hf[:], Hd[tok0:tok0 + LCH, hrows], reads=Hd.B, writes=hf.B)
            S.op("dve", lambda e: e.tensor_tensor(out=hc[:], in0=hc[:], in1=hf[:], op=ALU.add), reads=hc.B + hf.B, writes=hc.B)
            S.op("dve", lambda e: e.reduce_sum(out=sm[:, 5:6], in_=hc[:], axis=AX.X), reads=hc.B, writes=sm.B)
            S.op("pool", lambda e: e.tensor_tensor(out=sq[:], in0=hc[:], in1=hc[:], op=ALU.mult), reads=hc.B, writes=sq.B)
            S.op("dve", lambda e: e.reduce_sum(out=sm[:, 6:7], in_=sq[:], axis=AX.X), reads=sq.B, writes=sm.B)
            S.op("dve", lambda e: e.tensor_scalar(out=sm[:, 5:7], in0=sm[:, 5:7], scalar1=1.0 / DH, scalar2=None, op0=ALU.mult),
                 reads=sm.B, writes=sm.B)
            S.op("dve", lambda e: e.tensor_tensor(out=sm[:, 7:8], in0=sm[:, 5:6], in1=sm[:, 5:6], op=ALU.mult), reads=sm.B, writes=sm.B)
            S.op("dve", lambda e: e.tensor_tensor(out=sm[:, 8:9], in0=sm[:, 6:7], in1=sm[:, 7:8], op=ALU.subtract), reads=sm.B, writes=sm.B)
            S.op("dve", lambda e: e.tensor_scalar(out=sm[:, 8:9], in0=sm[:, 8:9], scalar1=EPS, scalar2=None, op0=ALU.add), reads=sm.B, writes=sm.B)
            S.op("act", lambda e: e.activation(out=sm[:, 9:10], in_=sm[:, 8:9], func=AF.Sqrt), reads=sm.B, writes=sm.B)
            S.op("dve", lambda e: e.reciprocal(out=sm[:, 10:11], in_=sm[:, 9:10]), reads=sm.B, writes=sm.B)
            S.op("dve", lambda e: e.scalar_tensor_tensor(out=sm[:, 11:12], in0=sm[:, 5:6], scalar=-1.0, in1=sm[:, 10:11],
                                                         op0=ALU.mult, op1=ALU.mult), reads=sm.B, writes=sm.B)
            S.op("dve", lambda e: e.tensor_scalar(out=hc[:], in0=hc[:], scalar1=sm[:, 10:11], scalar2=sm[:, 11:12],
                                                  op0=ALU.mult, op1=ALU.add), reads=hc.B + sm.B, writes=hc.B)
            S.op("dve", lambda e: e.tensor_tensor(out=hnb[:], in0=hc[:], in1=mhg[:], op=ALU.mult), reads=hc.B + mhg.B, writes=hnb.B)
            S.dma("pool", HNd[tok0:tok0 + LCH, hrows], hnb[:], reads=hnb.B, writes=[Buf()])
        dcol = dcr[:, cp:cp + 1]
        S.op("act", lambda e: e.mul(out=kw[:], in_=ktc, mul=wcol), reads=ktm.B + cols[3].B, writes=kw.B)
        pNU = self.next_ps()
        for k in range(4):
            pU = self.next_ps()
            while pU is pNU:
                pU = self.next_ps()
            S.op("pe", lambda e: e.matmul(pU[:], lhsT=kw[:, k * 128:(k + 1) * 128], rhs=vtc, start=True, stop=True),
                 reads=kw.B + vtm.B, writes=pU.B)
            S.op("dve", lambda e: e.scalar_tensor_tensor(out=C[:, k, :], in0=C[:, k, :], scalar=dcol, in1=pU[:],
                                                         op0=ALU.mult, op1=ALU.add), reads=C.B + dcr.B + pU.B, writes=C.B)
            S.op("act", lambda e: e.mul(out=Cb[:, k, :], in_=C[:, k, :], mul=1.0), reads=C.B, writes=Cb.B)
        S.op("pe", [(lambda e, k=k: e.matmul(pNU[:, k:k + 1], lhsT=kw[:, k * 128:(k + 1) * 128], rhs=onesb[:], start=True, stop=True))
                    for k in range(4)], reads=kw.B + onesb.B, writes=pNU.B)
        S.op("dve", lambda e: e.scalar_tensor_tensor(out=nn[:], in0=nn[:], scalar=dcol, in1=pNU[:, 0:4],
                                                     op0=ALU.mult, op1=ALU.add), reads=nn.B + dcr.B + pNU.B, writes=nn.B)
        S.op("act", lambda e: e.mul(out=nb[:], in_=nn[:], mul=1.0), reads=nn.B, writes=nb.B)

    def layer_hyena(self, li, j):
        S = self.S
        w_in = self.convert_weight("hy_in", self.hy_w_in[:], D, 4 * E)
        w_out = self.convert_weight("hy_out", self.hy_w_out[:], E, D)
        ds = self.dram_scratch
        Ud, Gd = ds("hy_U", [SEQ, E], BF16), ds("hy_G", [SEQ, E], BF16)
        Ad, Bd = ds("hy_A", [SEQ, E], BF16), ds("hy_B", [SEQ, E], BF16)
        Krd, Kid = ds("hy_Kr", [SEQ, E], F32), ds("hy_Ki", [SEQ, E], F32)
        YGd = ds("hy_YG", [SEQ, E], BF16)
        S.barrier()
        self.hy_filter(Ad, Bd)
        S.barrier()
        with contextlib.ExitStack() as les:
            self.alloc_commons(les)
            self.ada_mod(li, "lat")
            self.pass1(li, "lat")
            self.hy_proj(les, w_in, Ud, Gd)
            S.barrier()
        self.hy_dft(Ud, Gd, Ad, Bd, Krd, Kid, YGd)
        S.barrier()
        with contextlib.ExitStack() as les:
            self.alloc_commons(les)
            self.ada_mod(li, "lat")
            vT = self.sb("hy_vT", [128, EC, 256], BF16, st=les)
            yg = [self.sb(f"hy_yg{i}", [128, E], BF16, st=les) for i in range(2)]
            for tt in range(SEQ // 256):
                for sub in range(2):
                    y = yg[sub]
                    r0 = tt * 256 + sub * 128
                    S.dma("sp", y[:], YGd[r0:r0 + 128, :], reads=YGd.B, writes=y.B)
                    for g in range(EC // 4):
                        S.op("pe", [(lambda e, q=q: e.transpose(self.psb[:, q * 128:(q + 1) * 128],
                                                                y[:, (g * 4 + q) * 128:(g * 4 + q + 1) * 128], self.ident[:]))
                                    for q in range(4)], reads=y.B + self.ident.B, writes=self.psb.B)
                        S.op("act", lambda e: e.copy(out=vT[:, g * 4:(g + 1) * 4, sub * 128:(sub + 1) * 128],
                                                     in_=self.psb[:, 0:512].rearrange("p (q t) -> p q t", q=4)),
                             reads=self.psb.B, writes=vT.B)
                self.out_proj(li, "lat", vT, w_out, tt * 256, 256)
            S.barrier()

    def hy_filter(self, Ad, Bd):
        S = self.S
        TWO_PI = 2.0 * math.pi
        with contextlib.ExitStack() as st:
            sb = lambda n, sh, dt=F32: self.sb(n, sh, dt, st=st)
            featT, h1, h2 = sb("hf_feat", [33, SEQ]), sb("hf_h1", [64, SEQ]), sb("hf_h2", [65, SEQ])
            w3, w1, w2 = sb("hf_w3", [65, 2 * E]), sb("hf_w1", [33, 64]), sb("hf_w2", [64, 64])
            vec = sb("hf_vec", [64, 8])
            dec, dl, ntl = sb("hf_dec", [128, E]), sb("hf_dl", [128, E]), sb("hf_ntl", [128, 32])
            arg = [sb(f"hf_arg{i}", [64, 512]) for i in range(2)]
            ki = [sb(f"hf_ki{i}", [64, 512], I32) for i in range(2)]
            kf = [sb(f"hf_kf{i}", [64, 512]) for i in range(2)]
            sB = [sb(f"hf_sB{i}", [128, 512]) for i in range(2)]
            t1 = [sb(f"hf_t1{i}", [128, 512]) for i in range(2)]
            oA = [sb(f"hf_oA{i}", [128, 512], BF16) for i in range(2)]
            oB = [sb(f"hf_oB{i}", [128, 512], BF16) for i in range(2)]
            for dst, src in ((featT[:], self.hy_featT), (w1[:], self.hy_w1), (w2[:], self.hy_w2), (vec[:, 0:4], self.hy_fvec),
                             (w3[0:64, :], self.hy_w3), (ntl[:], self.hy_ntl)):
                S.dma("sp", dst, src[:], reads=src.B, writes=[Buf()])
            S.dma("sp", w3[64:65, :], self.hy_b3[:].partition_broadcast(1), reads=self.hy_b3.B, writes=w3.B)
            S.dma("sp", dl[:], self.hy_delta[:].partition_broadcast(128), reads=self.hy_delta.B, writes=dl.B)
            S.barrier()
            S.op("dve", lambda e: e.memset(h2[64:65, :], 1.0), writes=h2.B)
            S.op("dve", lambda e: e.tensor_tensor(out=vec[:, 4:5], in0=vec[:, 0:1], in1=vec[:, 1:2], op=ALU.mult), reads=vec.B, writes=vec.B)
            S.op("dve", lambda e: e.tensor_tensor(out=vec[:, 5:6], in0=vec[:, 2:3], in1=vec[:, 3:4], op=ALU.mult), reads=vec.B, writes=vec.B)

            def sin_layer(w, src, kdim, dst, fcol, fbcol):
                for n in range(8):
                    cols = slice(n * 512, (n + 1) * 512)
                    ps = self.next_ps()
                    S.op("pe", lambda e: e.matmul(ps[0:64, :], lhsT=w[0:kdim, :], rhs=src[0:kdim, cols], start=True, stop=True),
                         reads=w.B + src.B, writes=ps.B)
                    a, k_i, k_f = arg[n % 2], ki[n % 2], kf[n % 2]
                    S.op("dve", lambda e: e.tensor_scalar(out=a[:], in0=ps[0:64, :], scalar1=vec[:, fcol:fcol + 1], scalar2=vec[:, fbcol:fbcol + 1],
                                                          op0=ALU.mult, op1=ALU.add), reads=ps.B + vec.B, writes=a.B)
                    S.op("dve", lambda e: e.tensor_scalar(out=k_f[:], in0=a[:], scalar1=1.0 / TWO_PI, scalar2=None, op0=ALU.mult), reads=a.B, writes=k_f.B)
                    S.op("dve", lambda e: e.tensor_copy(out=k_i[:], in_=k_f[:]), reads=k_f.B, writes=k_i.B)
                    S.op("dve", lambda e: e.tensor_copy(out=k_f[:], in_=k_i[:]), reads=k_i.B, writes=k_f.B)
                    S.op("dve", lambda e: e.scalar_tensor_tensor(out=a[:], in0=k_f[:], scalar=-TWO_PI, in1=a[:], op0=ALU.mult, op1=ALU.add),
                         reads=k_f.B + a.B, writes=a.B)
                    S.op("dve", lambda e: e.tensor_scalar(out=a[:], in0=a[:], scalar1=-3.141592, scalar2=3.141592, op0=ALU.max, op1=ALU.min),
                         reads=a.B, writes=a.B)
                    S.op("act", lambda e: e.activation(out=dst[0:64, cols], in_=a[:], func=AF.Sin), reads=a.B, writes=dst.B)

            sin_layer(w1, featT, 33, h1, 1, 4)
            sin_layer(w2, h1, 64, h2, 3, 5)
            for tc in range(32):
                S.op("act", lambda e: e.activation(out=dec[:], in_=dl[:], func=AF.Exp, scale=ntl[:, tc:tc + 1]), reads=dl.B + ntl.B, writes=dec.B)
                for n in range(8):
                    cols = slice(n * 512, (n + 1) * 512)
                    psF, psB = self.next_ps(), self.next_ps()
                    S.op("pe", lambda e: e.matmul(psF[:], lhsT=h2[0:65, tc * 128:(tc + 1) * 128], rhs=w3[0:65, cols], start=True, stop=True),
                         reads=h2.B + w3.B, writes=psF.B)
                    S.op("pe", lambda e: e.matmul(psB[:], lhsT=h2[0:65, tc * 128:(tc + 1) * 128], rhs=w3[0:65, E + n * 512:E + (n + 1) * 512],
                                                  start=True, stop=True), reads=h2.B + w3.B, writes=psB.B)
                    b_, t_, a_, o_ = sB[n % 2], t1[n % 2], oA[n % 2], oB[n % 2]
                    S.op("act", lambda e: e.copy(out=b_[:], in_=psB[:]), reads=psB.B, writes=b_.B)
                    if tc == 0:
                        S.op("dve", lambda e: e.memset(b_[0:1, :], 0.0), writes=b_.B)
                    S.op("dve", lambda e: e.tensor_tensor(out=t_[:], in0=psF[:], in1=b_[:], op=ALU.add), reads=psF.B + b_.B, writes=t_.B)
                    S.op("dve", lambda e: e.tensor_tensor(out=a_[:], in0=t_[:], in1=dec[:, cols], op=ALU.mult), reads=t_.B + dec.B, writes=a_.B)
                    S.op("dve", lambda e: e.tensor_tensor(out=t_[:], in0=psF[:], in1=b_[:], op=ALU.subtract), reads=psF.B + b_.B, writes=t_.B)
                    S.op("dve", lambda e: e.tensor_tensor(out=o_[:], in0=t_[:], in1=dec[:, cols], op=ALU.mult), reads=t_.B + dec.B, writes=o_.B)
                    S.dma("pool", Ad[tc * 128:(tc + 1) * 128, cols], a_[:], reads=a_.B, writes=Ad.B)
                    S.dma("pool", Bd[tc * 128:(tc + 1) * 128, cols], o_[:], reads=o_.B, writes=Bd.B)
            S.barrier()

    def hy_proj(self, les, w_in, Ud, Gd):
        S = self.S
        TT = 256
        sb = lambda n, sh, dt=F32: self.sb(n, sh, dt, st=les)
        uT = sb("hp_uT", [128, KC, TT + 2], BF16)
        Utm, Gtm = sb("hp_U", [128, 2, E], BF16), sb("hp_G", [128, 2, E], BF16)
        cva = [sb(f"hp_a{i}", [128, TT]) for i in range(3)]
        zs = sb("hp_zs", [128, TT])
        uu, gg = sb("hp_uu", [128, TT], BF16), sb("hp_gg", [128, TT], BF16)
        cw, cb = sb("hp_cw", [128, 96, 3]), sb("hp_cb", [128, 96])
        S.dma("sp", cw[:], self.hy_cw[:], reads=self.hy_cw.B, writes=cw.B)
        S.dma("sp", cb[:], self.hy_cb[:], reads=self.hy_cb.B, writes=cb.B)
        utv = self.UT[:].rearrange("(k p) t -> p k t", p=128)
        wv = w_in[:].rearrange("(k p) n -> p k n", p=128)
        for tt in range(SEQ // TT):
            t0 = tt * TT
            S.dma("sp", uT[:], utv[:, :, UT_LAT0 + t0 - 1: UT_LAT0 + t0 + TT + 1], reads=self.UT.B, writes=uT.B)
            for g in range(EC // 4):
                wts = []
                for part in range(4):
                    w = self.next_w()
                    wt = w[:].rearrange("p (k n) -> p k n", k=16)
                    S.dma("sp", wt, wv[:, :, part * E + g * 512: part * E + (g + 1) * 512], reads=w_in.B, writes=w.B)
                    wts.append((w, wt))
                for q in range(4):
                    ec = g * 4 + q
                    pss = []
                    for part in range(4):
                        ps = self.next_ps()
                        wt = wts[part][1]
                        S.op("pe", [(lambda e, k=k, ps=ps, wt=wt: e.matmul(ps[:, 0:TT + 2], lhsT=wt[:, k, q * 128:(q + 1) * 128], rhs=uT[:, k, :],
                                                                           start=(k == 0), stop=(k == KC - 1))) for k in range(KC)],
                             reads=uT.B + wts[part][0].B, writes=ps.B)
                        pss.append(ps)
                    for part in range(3):
                        a, ps, ci = cva[part], pss[part], part * 32 + ec
                        S.op("dve", lambda e: e.tensor_scalar(out=a[:], in0=ps[:, 1:TT + 1], scalar1=cw[:, ci, 1:2], scalar2=cb[:, ci:ci + 1],
                                                              op0=ALU.mult, op1=ALU.add), reads=ps.B + cw.B + cb.B, writes=a.B)
                        S.op("dve", lambda e: e.scalar_tensor_tensor(out=a[:], in0=ps[:, 0:TT], scalar=cw[:, ci, 0:1], in1=a[:],
                                                                     op0=ALU.mult, op1=ALU.add), reads=ps.B + cw.B + a.B, writes=a.B)
                        S.op("dve", lambda e: e.scalar_tensor_tensor(out=a[:], in0=ps[:, 2:TT + 2], scalar=cw[:, ci, 2:3], in1=a[:],
                                                                     op0=ALU.mult, op1=ALU.add), reads=ps.B + cw.B + a.B, writes=a.B)
                    S.op("act", lambda e: e.activation(out=zs[:], in_=pss[3][:, 1:TT + 1], func=AF.Silu), reads=pss[3].B, writes=zs.B)
                    S.op("dve", lambda e: e.tensor_tensor(out=uu[:], in0=cva[1][:], in1=cva[2][:], op=ALU.mult),
                         reads=cva[1].B + cva[2].B, writes=uu.B)
                    S.op("pool", lambda e: e.tensor_tensor(out=gg[:], in0=cva[0][:], in1=zs[:], op=ALU.mult),
                         reads=cva[0].B + zs.B, writes=gg.B)
                    S.op("pe", [(lambda e, q2=q2: e.transpose(self.psb[:, q2 * 128:(q2 + 1) * 128],
                                                              (uu if q2 < 2 else gg)[:, (q2 % 2) * 128:(q2 % 2 + 1) * 128], self.ident[:]))
                                for q2 in range(4)], reads=uu.B + gg.B + self.ident.B, writes=self.psb.B)
                    S.op("act", lambda e: e.copy(out=Utm[:, :, ec * 128:(ec + 1) * 128],
                                                 in_=self.psb[:, 0:256].rearrange("p (s t) -> p s t", s=2)), reads=self.psb.B, writes=Utm.B)
                    S.op("act", lambda e: e.copy(out=Gtm[:, :, ec * 128:(ec + 1) * 128],
                                                 in_=self.psb[:, 256:512].rearrange("p (s t) -> p s t", s=2)), reads=self.psb.B, writes=Gtm.B)
            for sub in range(2):
                r0 = t0 + sub * 128
                S.dma("pool", Ud[r0:r0 + 128, :], Utm[:, sub, :], reads=Utm.B, writes=Ud.B)
                S.dma("pool", Gd[r0:r0 + 128, :], Gtm[:, sub, :], reads=Gtm.B, writes=Gd.B)

    def hy_dft(self, Ud, Gd, Ad, Bd, Krd, Kid, YGd):
        S = self.S
        NF = 8192.0
        with contextlib.ExitStack() as st:
            sb = lambda n, sh, dt=F32: self.sb(n, sh, dt, st=st)
            Ut, Bt = sb("hd_U", [128, 32, 512], BF16), sb("hd_B", [128, 32, 512], BF16)
            Y = sb("hd_Y", [128, 64, 512], BF16)
            Cb = [sb(f"hd_C{i}", [128, 32, 128], BF16) for i in range(2)]
            Sb = [sb(f"hd_S{i}", [128, 32, 128], BF16) for i in range(2)]
            Kr = [sb(f"hd_Kr{i}", [128, 512]) for i in range(2)]
            Ki = [sb(f"hd_Ki{i}", [128, 512]) for i in range(2)]
            tm = [sb(f"hd_t{i}", [128, 512]) for i in range(4)]
            gt = [sb(f"hd_g{i}", [128, 512], BF16) for i in range(2)]
            og = [sb(f"hd_o{i}", [128, 512], BF16) for i in range(2)]
            hb = sb("hd_hb", [128, 512])
            tabC = self.hy_C[:].rearrange("(c p) k -> p c k", p=128)
            tabS = self.hy_S[:].rearrange("(c p) k -> p c k", p=128)
            tabCT = self.hy_CT[:].rearrange("(c p) k -> p c k", p=128)
            tabST = self.hy_ST[:].rearrange("(c p) k -> p c k", p=128)
            view = lambda d, n: d[:, n * 512:(n + 1) * 512].rearrange("(c p) e -> p c e", p=128)
            for n in range(8):
                cols = slice(n * 512, (n + 1) * 512)
                S.dma("sp", Ut[:], view(Ad, n), reads=Ad.B, writes=Ut.B)
                S.dma("sp", Bt[:], view(Bd, n), reads=Bd.B, writes=Bt.B)
                for j in range(32):
                    cb_, sb_ = Cb[j % 2], Sb[j % 2]
                    S.dma("sp", cb_[:], tabC[:, :, j * 128:(j + 1) * 128], reads=self.hy_C.B, writes=cb_.B)
                    S.dma("sp", sb_[:], tabS[:, :, j * 128:(j + 1) * 128], reads=self.hy_S.B, writes=sb_.B)
                    psR, psI = self.next_ps(), self.next_ps()
                    S.op("pe", [(lambda e, c=c: e.matmul(psR[:], lhsT=cb_[:, c, :], rhs=Ut[:, c, :], start=(c == 0), stop=(c == 31)))
                                for c in range(32)], reads=cb_.B + Ut.B, writes=psR.B)
                    S.op("pe", [(lambda e, c=c: e.matmul(psI[:], lhsT=sb_[:, c, :], rhs=Bt[:, c, :], start=(c == 0), stop=(c == 31)))
                                for c in range(32)], reads=sb_.B + Bt.B, writes=psI.B)
                    kr, ki_ = Kr[j % 2], Ki[j % 2]
                    S.op("act", lambda e: e.copy(out=kr[:], in_=psR[:]), reads=psR.B, writes=kr.B)
                    S.op("dve", lambda e: e.tensor_copy(out=ki_[:], in_=psI[:]), reads=psI.B, writes=ki_.B)
                    S.dma("pool", Krd[j * 128:(j + 1) * 128, cols], kr[:], reads=kr.B, writes=Krd.B)
                    S.dma("pool", Kid[j * 128:(j + 1) * 128, cols], ki_[:], reads=ki_.B, writes=Kid.B)
            S.barrier()
            for n in range(8):
                cols = slice(n * 512, (n + 1) * 512)
                S.dma("sp", Ut[:], view(Ud, n), reads=Ud.B, writes=Ut.B)
                S.dma("sp", hb[:], self.hy_hb[n * 512:(n + 1) * 512].partition_broadcast(128), reads=self.hy_hb.B, writes=hb.B)
                for j in range(32):
                    cb_, sb_ = Cb[j % 2], Sb[j % 2]
                    S.dma("sp", cb_[:], tabC[:, :, j * 128:(j + 1) * 128], reads=self.hy_C.B, writes=cb_.B)
                    S.dma("sp", sb_[:], tabS[:, :, j * 128:(j + 1) * 128], reads=self.hy_S.B, writes=sb_.B)
                    kr, ki_ = Kr[j % 2], Ki[j % 2]
                    S.dma("sp", kr[:], Krd[j * 128:(j + 1) * 128, cols], reads=Krd.B, writes=kr.B)
                    S.dma("sp", ki_[:], Kid[j * 128:(j + 1) * 128, cols], reads=Kid.B, writes=ki_.B)
                    psR, psI = self.next_ps(), self.next_ps()
                    S.op("pe", [(lambda e, c=c: e.matmul(psR[:], lhsT=cb_[:, c, :], rhs=Ut[:, c, :], start=(c == 0), stop=(c == 31)))
                                for c in range(32)], reads=cb_.B + Ut.B, writes=psR.B)
                    S.op("pe", [(lambda e, c=c: e.matmul(psI[:], lhsT=sb_[:, c, :], rhs=Ut[:, c, :], start=(c == 0), stop=(c == 31)))
                                for c in range(32)], reads=sb_.B + Ut.B, writes=psI.B)
                    S.op("dve", lambda e: e.tensor_tensor(out=tm[0][:], in0=psR[:], in1=kr[:], op=ALU.mult), reads=psR.B + kr.B, writes=tm[0].B)
                    S.op("dve", lambda e: e.tensor_tensor(out=tm[1][:], in0=psI[:], in1=ki_[:], op=ALU.mult), reads=psI.B + ki_.B, writes=tm[1].B)
                    S.op("pool", lambda e: e.tensor_tensor(out=Y[:, j, :], in0=tm[0][:], in1=tm[1][:], op=ALU.subtract),
                         reads=tm[0].B + tm[1].B, writes=Y.B)
                    S.op("dve", lambda e: e.tensor_tensor(out=tm[2][:], in0=psR[:], in1=ki_[:], op=ALU.mult), reads=psR.B + ki_.B, writes=tm[2].B)
                    S.op("dve", lambda e: e.tensor_tensor(out=tm[3][:], in0=psI[:], in1=kr[:], op=ALU.mult), reads=psI.B + kr.B, writes=tm[3].B)
                    S.op("pool", lambda e: e.tensor_tensor(out=Y[:, 32 + j, :], in0=tm[2][:], in1=tm[3][:], op=ALU.add),
                         reads=tm[2].B + tm[3].B, writes=Y.B)
                for tc in range(32):
                    cb_, sb_ = Cb[tc % 2], Sb[tc % 2]
                    S.dma("sp", cb_[:], tabCT[:, :, tc * 128:(tc + 1) * 128], reads=self.hy_CT.B, writes=cb_.B)
                    S.dma("sp", sb_[:], tabST[:, :, tc * 128:(tc + 1) * 128], reads=self.hy_ST.B, writes=sb_.B)
                    g_, o_ = gt[tc % 2], og[tc % 2]
                    S.dma("sp", g_[:], Gd[tc * 128:(tc + 1) * 128, cols], reads=Gd.B, writes=g_.B)
                    ps = self.next_ps()
                    S.op("pe", [(lambda e, c=c: e.matmul(ps[:], lhsT=(cb_ if c < 32 else sb_)[:, c % 32, :], rhs=Y[:, c, :],
                                                         start=(c == 0), stop=(c == 63))) for c in range(64)],
                         reads=cb_.B + sb_.B + Y.B, writes=ps.B)
                    ta, tb = tm[tc % 2], tm[2 + tc % 2]
                    S.op("dve", lambda e: e.tensor_tensor(out=ta[:], in0=Ut[:, tc, :], in1=hb[:], op=ALU.mult), reads=Ut.B + hb.B, writes=ta.B)
                    S.op("dve", lambda e: e.scalar_tensor_tensor(out=tb[:], in0=ps[:], scalar=2.0 / NF, in1=ta[:], op0=ALU.mult, op1=ALU.add),
                         reads=ps.B + ta.B, writes=tb.B)
                    S.op("pool", lambda e: e.tensor_tensor(out=o_[:], in0=tb[:], in1=g_[:], op=ALU.mult), reads=tb.B + g_.B, writes=o_.B)
                    S.dma("pool", YGd[tc * 128:(tc + 1) * 128, cols], o_[:], reads=o_.B, writes=YGd.B)
            S.barrier()


def _pk(v):
    return np.ascontiguousarray(v.reshape(-1, 128).T)


def make_inputs(b, inp):
    f = np.float32
    m = {
        "x": np.ascontiguousarray(inp["x"][b]), "ctx": np.ascontiguousarray(inp["ctx"][b]),
        "c_pk": _pk(inp["c"][b]), "cc_pk": _pk(inp["c_ctx"]),
        "norm_g": inp["norm_g"], "ada_w": inp["ada_w"], "ada_b": inp["ada_b"], "final_g": inp["final_g"],
        "cv_w_in": inp["cv_w_in"],
        "cv_dw": np.ascontiguousarray(inp["cv_dw_w"].reshape(2, CONVW, EC, 128).transpose(0, 3, 2, 1)),
        "cv_dwb": np.ascontiguousarray(inp["cv_dw_b"].reshape(2, EC, 128).transpose(0, 2, 1)),
        "cv_lng": np.ascontiguousarray(inp["cv_ln_g"].reshape(2, EC, 128).transpose(0, 2, 1)),
        "cv_lnb": np.ascontiguousarray(inp["cv_ln_b"].reshape(2, EC, 128).transpose(0, 2, 1)),
        "cv_w_out": inp["cv_w_out"],
        "ident": np.eye(128, dtype=f),
    }
    m.update(_hyena_consts())
    m.update(_mlstm_inputs(inp))
    m.update({
        "hy_w_in": inp["hy_w_in"][0], "hy_w_out": inp["hy_w_out"][0],
        "hy_cw": np.ascontiguousarray(inp["hy_conv_w"][0].reshape(3, 96, 128).transpose(2, 1, 0)),
        "hy_cb": np.ascontiguousarray(inp["hy_conv_b"][0].reshape(96, 128).T),
        "hy_fvec": np.ascontiguousarray(np.stack([inp["hy_f_b1"][0], inp["hy_f_freq1"][0], inp["hy_f_b2"][0], inp["hy_f_freq2"][0]], axis=1)),
        "hy_w1": inp["hy_f_w1"][0], "hy_w2": inp["hy_f_w2"][0], "hy_w3": inp["hy_f_w3"][0], "hy_b3": inp["hy_f_b3"][0],
        "hy_hb": inp["hy_h_bias"][0],
    })
    return m


def _mlstm_inputs(inp):
    f = np.float32
    bd = np.zeros((4, EC, 128, 128), f)
    for a, key in enumerate(("ml_w_q", "ml_w_k", "ml_w_v", "ml_w_o")):
        w = inp[key][0].reshape(EC, 32, 4, 4)
        for g in range(32):
            bd[a, :, 4 * g:4 * g + 4, 4 * g:4 * g + 4] = w[:, g]
    wg = inp["ml_w_gates"][0].reshape(96, 128, 2, 2, 8)
    wg = np.ascontiguousarray(wg.transpose(1, 2, 3, 0, 4)).reshape(128, 4 * 96 * 8)
    bg = np.ascontiguousarray(inp["ml_b_gates"][0].reshape(4, 8).T)
    pk = lambda v: np.ascontiguousarray(v.reshape(EC, 128).T)
    s_, t_ = np.meshgrid(np.arange(64), np.arange(64), indexing="ij")
    mask = np.stack([(s_ <= t_), (s_ >= t_)]).astype(f)
    return {
        "ml_w_in": inp["ml_w_in"][0], "ml_w_out": inp["ml_w_out"][0], "ml_bd": bd.reshape(4 * EC * 128, 128), "ml_wg": wg,
        "ml_cw": np.ascontiguousarray(inp["ml_conv_w"][0].reshape(3, EC, 128).transpose(2, 1, 0)),
        "ml_vec": np.ascontiguousarray(np.stack([pk(inp["ml_conv_b"][0]), pk(inp["ml_b_o"][0]), pk(inp["ml_skip"][0])], axis=1)),
        "ml_bg": bg, "ml_mhg": inp["ml_mh_g"][0], "ml_mask": mask,
    }


_HC = {}


def _hyena_consts():
    if _HC:
        return _HC
    f = np.float32
    L = SEQ
    t = np.linspace(0.0, 1.0, L, dtype=f)[:, None]
    bands = 16
    ang = (f(2.0 * math.pi) * np.arange(L, dtype=f)[:, None] / f(L)).astype(f)
    fr = np.linspace(1e-4, bands - 1, bands, dtype=f)[None, :]
    feat = np.concatenate([t, np.cos(fr * ang), -np.sin(fr * ang)], axis=-1).astype(f)
    lo = math.log(1e-2) / 0.3
    hi = math.log(1e-2) / 1.5
    deltas = np.abs(np.linspace(lo, hi, E, dtype=f)).astype(f)
    tl = t[:, 0]
    _HC["hy_featT"] = np.ascontiguousarray(feat.T)
    _HC["hy_delta"] = deltas
    _HC["hy_ntl"] = np.ascontiguousarray((-tl).reshape(32, 128).T)
    n = np.arange(SEQ, dtype=np.int64)
    ph = (np.outer(n, 2 * n + 1) % 16384).astype(np.float64) * (2.0 * math.pi / 16384.0)
    C = np.cos(ph).astype(ml_dtypes.bfloat16)
    Sn = np.sin(ph).astype(ml_dtypes.bfloat16)
    _HC["hy_C"] = C
    _HC["hy_S"] = Sn
    _HC["hy_CT"] = np.ascontiguousarray(C.T)
    _HC["hy_ST"] = np.ascontiguousarray(Sn.T)
    return _HC


_PROG = {}
ML_STOP = 9


def run(inputs, layers=(0, 1, 2, 3), final=True, cores=8):
    inp = {k: np.asarray(v, dtype=np.float32) for k, v in inputs.items()}
    key = (tuple(layers), final)
    p = Prog(layers, final)
    p.ml_stop = ML_STOP
    nc = p.build()
    in_maps = []
    for b in range(cores):
        m = make_inputs(b, inp)
        in_maps.append({k: m[k] for k in p.in_names})
    res = run_bass_kernel_spmd(nc, in_maps, core_ids=list(range(cores)))
    return np.stack([res.results[b]["out"] for b in range(cores)], axis=0)


def kernel(**inputs):
    return run(inputs).astype(np.float32)
```

```python
import contextlib
import math
import numpy as np
import ml_dtypes
import concourse.bass as bass
import concourse.mybir as mybir
from concourse.bass import AP
from concourse.bass_utils import run_bass_kernel_spmd

F32 = mybir.dt.float32
BF16 = mybir.dt.bfloat16
I32 = mybir.dt.int32
AF = mybir.ActivationFunctionType
ALU = mybir.AluOpType
AX = mybir.AxisListType

D = 2048
E = 4096
SEQ = 4096
NCTX = 256
EPS = 1e-6
KC = D // 128
EC = E // 128
CONVW = 31
NH = 8
DH = 512
LCH = 64
UT_CTX0 = 1
UT_LAT0 = 259
UT_COLS = 4356


class Buf:
    __slots__ = ("w", "r")

    def __init__(self):
        self.w = None
        self.r = {}


class Sched:
    def __init__(self, nc, es):
        self.nc = nc
        self.engs = {"pe": nc.tensor, "act": nc.scalar, "dve": nc.vector, "pool": nc.gpsimd, "sp": nc.sync}
        self.sem = {}
        self.cnt = {}
        for e in ("pe", "act", "dve", "pool"):
            self.sem[e] = es.enter_context(nc.semaphore("prog_" + e))
            self.cnt[e] = 0
        self.dsem = {"sp": [], "pool": [], "act": []}
        for q, n in (("sp", 28), ("pool", 20), ("act", 8)):
            for i in range(n):
                self.dsem[q].append([es.enter_context(nc.semaphore(f"d_{q}{i}")), 0, None])
        self.dnext = {"sp": 0, "pool": 0, "act": 0}
        self.seen = {e: {} for e in self.engs}
        self.semobj = {}

    def _wait(self, eng, deps):
        best = {}
        for tok in deps:
            if tok is None:
                continue
            sid, val = tok
            if best.get(sid, 0) < val:
                best[sid] = val
        seen = self.seen[eng]
        for sid, val in best.items():
            if seen.get(sid, 0) >= val:
                continue
            self.engs[eng].wait_ge(self.semobj[sid], val)
            seen[sid] = val

    def _deps(self, reads, writes):
        deps = []
        for b in reads:
            deps.append(b.w)
        for b in writes:
            deps.append(b.w)
            for sid, val in b.r.items():
                deps.append((sid, val))
        return deps

    def _commit(self, tok, reads, writes):
        sid, val = tok
        for b in reads:
            if b.r.get(sid, 0) < val:
                b.r[sid] = val
        for b in writes:
            b.w = tok
            b.r = {}

    def op(self, eng, fn, reads=(), writes=()):
        deps = self._deps(reads, writes)
        if eng == "pe":
            sid_own = id(self.sem["pe"])
            deps = [t for t in deps if t is not None and t[0] != sid_own]
        self._wait(eng, deps)
        e = self.engs[eng]
        fns = fn if isinstance(fn, (list, tuple)) else [fn]
        inst = None
        for f in fns:
            inst = f(e)
        self.cnt[eng] += 1
        sem = self.sem[eng]
        inst.then_inc(sem, 1)
        self.semobj[id(sem)] = sem
        tok = (id(sem), self.cnt[eng])
        self._commit(tok, reads, writes)
        return tok

    def dma(self, q, out, in_, reads=(), writes=(), **kw):
        slot = self.dsem[q][self.dnext[q]]
        self.dnext[q] = (self.dnext[q] + 1) % len(self.dsem[q])
        deps = self._deps(reads, writes)
        deps.append(slot[2])
        self._wait(q, deps)
        sem = slot[0]
        slot[1] += 16
        self.engs[q].dma_start(out=out, in_=in_, **kw).then_inc(sem, 16)
        self.semobj[id(sem)] = sem
        tok = (id(sem), slot[1])
        slot[2] = tok
        self._commit(tok, reads, writes)
        return tok

    def barrier(self):
        toks = [(id(self.sem[e]), self.cnt[e]) for e in self.sem if self.cnt[e] > 0]
        for q in self.dsem:
            for slot in self.dsem[q]:
                if slot[2] is not None:
                    toks.append(slot[2])
        for e in self.engs:
            self._wait(e, toks)

    def drain(self, eng, bufs):
        deps = []
        for b in bufs:
            deps.append(b.w)
        self._wait(eng, deps)


class T:
    def __init__(self, t, nb=1):
        self.t = t
        self.b = [Buf() for _ in range(nb)]

    def __getitem__(self, k):
        return self.t[k]

    @property
    def B(self):
        return self.b


class V:
    def __init__(self, t, n):
        self.t = t.t[:, 0:n]
        self.b = t.b

    def __getitem__(self, k):
        return self.t[k]

    @property
    def B(self):
        return self.b


def bcast_free(ap2d, n):
    a = ap2d.ap
    return AP(ap2d.tensor, ap2d.offset, [list(a[0]), list(a[1]), [0, n]])


class Prog:
    def __init__(self, layers=(0, 1, 2, 3), final=True):
        self.layers = layers
        self.final = final
        self.nc = bass.Bass("TRN2", target_bir_lowering=False)
        self.es = contextlib.ExitStack()

    def sb(self, name, shape, dt=F32, nb=1, st=None):
        self._uid = getattr(self, "_uid", 0) + 1
        return T((st or self.es).enter_context(self.nc.sbuf_tensor(f"{name}_{self._uid}", shape, dt)), nb)

    def dram_in(self, name, shape, dt=F32):
        t = self.nc.dram_tensor(name, list(shape), dt, kind="ExternalInput")
        self.in_names.append(name)
        return T(t.ap(), 1)

    def dram_scratch(self, name, shape, dt, nb=1):
        return T(self.nc.dram_tensor(name, list(shape), dt, kind="Internal").ap(), nb)

    def build(self):
        nc = self.nc
        self.in_names = []
        with self.es:
            self.S = Sched(nc, self.es)
            self._declare()
            self._consts()
            for li in self.layers:
                kind, j = li % 3, li // 3
                if kind == 0:
                    self.layer_conv(li, j)
                elif kind == 1:
                    self.layer_mlstm(li, j)
                else:
                    self.layer_hyena(li, j)
            if self.final:
                self.final_norm()
            else:
                self.copy_out()
            S = self.S
            S.drain("sp", self.out.B)
            S.drain("pool", self.out.B)
        return nc

    def _declare(self):
        di = self.dram_in
        self.x = di("x", [SEQ, D])
        self.ctx = di("ctx", [NCTX, D])
        self.c_pk = di("c_pk", [128, KC])
        self.cc_pk = di("cc_pk", [128, KC])
        self.norm_g = di("norm_g", [4, D])
        self.ada_w = di("ada_w", [4, D, 3 * D])
        self.ada_b = di("ada_b", [4, 3 * D])
        self.final_g = di("final_g", [D])
        if 0 in self.layers or 3 in self.layers:
            self.cv_w_in = di("cv_w_in", [2, D, 3 * E])
            self.cv_dw = di("cv_dw", [2, 128, EC, CONVW])
            self.cv_dwb = di("cv_dwb", [2, 128, EC])
            self.cv_lng = di("cv_lng", [2, 128, EC])
            self.cv_lnb = di("cv_lnb", [2, 128, EC])
            self.cv_w_out = di("cv_w_out", [2, E, D])
        self.ident_d = di("ident", [128, 128])
        if 1 in self.layers:
            self.ml_w_in = di("ml_w_in", [D, 2 * E])
            self.ml_w_out = di("ml_w_out", [E, D])
            self.ml_bd = di("ml_bd", [4 * EC * 128, 128])
            self.ml_wg = di("ml_wg", [128, 4 * 96 * 8])
            self.ml_cw = di("ml_cw", [128, EC, 3])
            self.ml_vec = di("ml_vec", [128, 3, EC])
            self.ml_bg = di("ml_bg", [8, 4])
            self.ml_mhg = di("ml_mhg", [E])
            self.ml_mask = di("ml_mask", [2, 64, 64])
        if 2 in self.layers:
            self.hy_w_in = di("hy_w_in", [D, 4 * E])
            self.hy_w_out = di("hy_w_out", [E, D])
            self.hy_cw = di("hy_cw", [128, 96, 3])
            self.hy_cb = di("hy_cb", [128, 96])
            self.hy_featT = di("hy_featT", [33, SEQ])
            self.hy_fvec = di("hy_fvec", [64, 4])
            self.hy_w1 = di("hy_w1", [33, 64])
            self.hy_w2 = di("hy_w2", [64, 64])
            self.hy_w3 = di("hy_w3", [64, 2 * E])
            self.hy_b3 = di("hy_b3", [2 * E])
            self.hy_hb = di("hy_hb", [E])
            self.hy_delta = di("hy_delta", [E])
            self.hy_ntl = di("hy_ntl", [128, 32])
            self.hy_C = di("hy_C", [SEQ, SEQ], BF16)
            self.hy_S = di("hy_S", [SEQ, SEQ], BF16)
            self.hy_CT = di("hy_CT", [SEQ, SEQ], BF16)
            self.hy_ST = di("hy_ST", [SEQ, SEQ], BF16)
        self.out = T(self.nc.dram_tensor("out", [SEQ, D], F32, kind="ExternalOutput").ap(), SEQ // 128)
        self.hctx = self.dram_scratch("hctx", [NCTX, D], F32, NCTX // 128)
        self.UT = self.dram_scratch("UT", [D, UT_COLS], BF16, 1)
        self.wb = {}

    def _consts(self):
        S, nc = self.S, self.nc
        self.ident_f = self.sb("ident_f", [128, 128], F32)
        self.ident = self.sb("ident_b", [128, 128], BF16)
        self.ones_f = self.sb("ones_f", [128, 128], F32)
        S.dma("sp", self.ident_f[:], self.ident_d[:], reads=self.ident_d.B, writes=self.ident_f.B)
        S.op("dve", lambda e: e.tensor_copy(out=self.ident[:], in_=self.ident_f[:]), reads=self.ident_f.B, writes=self.ident.B)
        S.op("dve", lambda e: e.memset(self.ones_f[:], 1.0), writes=self.ones_f.B)
        self.ps = [T(self.es.enter_context(nc.psum_tensor(f"ps{i}", [128, 512], F32))) for i in range(7)]
        self.psb = T(self.es.enter_context(nc.psum_tensor("psb", [128, 1024], BF16)))
        self.psi = 0
        self.csb = {}
        for nm, src in (("lat", self.c_pk), ("ctx", self.cc_pk)):
            cf = self.sb("cf_" + nm, [128, KC], F32)
            cs = self.sb("cs_" + nm, [128, KC], F32)
            S.dma("sp", cf[:], src[:], reads=src.B, writes=cf.B)
            S.op("act", lambda e: e.activation(out=cs[:], in_=cf[:], func=AF.Silu), reads=cf.B, writes=cs.B)
            self.csb[nm] = cs
        self.zcol = self.sb("zcol", [128, KC, 2], BF16)
        S.op("dve", lambda e: e.memset(self.zcol[:], 0.0), writes=self.zcol.B)
        utv = self.UT[:].rearrange("(k p) t -> p k t", p=128)
        for c0 in (0, 257, 4355):
            n = 2 if c0 == 257 else 1
            S.dma("sp", utv[:, :, c0:c0 + n], self.zcol[:, :, 0:n], reads=self.zcol.B, writes=self.UT.B,
                  allow_slow_non_contiguous=True)

    def alloc_commons(self, st, mods=True):
        self.G1 = self.sb("G1", [128, D], st=st)
        self.SH = self.sb("SH", [128, D], st=st)
        self.GT = self.sb("GT", [128, D], st=st)
        self.wpool = [self.sb(f"wp{i}", [128, 16 * 512], BF16, st=st) for i in range(4)]
        self.wpi = 0
        self.bb = [self.sb(f"bb{i}", [128, 512], st=st) for i in range(2)]
        self.tmpA = [self.sb(f"tmpA{i}", [128, 512], st=st) for i in range(2)]
        self.ht = [self.sb(f"ht{i}", [128, D], st=st) for i in range(2)]
        self.nt = self.sb("ntmp", [128, D], st=st)
        self.ngb = self.nt
        self.ub = self.sb("ub", [128, D], BF16, st=st)
        self.ss = self.sb("ss", [128, 4], st=st)
        self.uTs = [self.sb(f"uTs{i}", [128, KC, 128], BF16, st=st) for i in range(2)]

    def next_ps(self):
        p = self.ps[self.psi]
        self.psi = (self.psi + 1) % len(self.ps)
        return p

    def next_w(self):
        w = self.wpool[self.wpi]
        self.wpi = (self.wpi + 1) % len(self.wpool)
        return w

    def convert_weight(self, key, src_ap, rows, cols):
        if key in self.wb:
            return self.wb[key]
        nblk = rows // 128
        dst = self.dram_scratch("wb_" + key, [rows, cols], BF16, nblk)
        srcbuf = Buf()
        for r in range(nblk):
            self.S.dma("pool", dst[r * 128:(r + 1) * 128, :], src_ap[r * 128:(r + 1) * 128, :],
                       reads=[srcbuf], writes=[dst.b[r]])
        self.wb[key] = dst
        return dst

    def ada_mod(self, li, stream):
        S = self.S
        cs = self.csb[stream]
        cb = self.uTs[0]
        S.op("dve", lambda e: e.tensor_copy(out=cb[:], in_=bcast_free(cs[:], 128)), reads=cs.B, writes=cb.B)
        wv = self.ada_w[li].rearrange("(k p) n -> p k n", p=128)
        S.dma("sp", self.ngb[:], self.norm_g[li].partition_broadcast(128), reads=self.norm_g.B, writes=self.ngb.B)
        for n in range(12):
            w = self.next_w()
            wt = w[:].rearrange("p (k n) -> p k n", k=16)
            S.dma("pool", wt, wv[:, :, n * 512:(n + 1) * 512], reads=self.ada_w.B, writes=w.B)
            bb = self.bb[n % 2]
            S.dma("sp", bb[:], self.ada_b[li, n * 512:(n + 1) * 512].partition_broadcast(128),
                  reads=self.ada_b.B, writes=bb.B)
            ps = self.next_ps()
            S.op("pe", [(lambda e, k=k: e.matmul(ps[:], lhsT=cb[:, k, :], rhs=wt[:, k, :], start=(k == 0), stop=(k == 15)))
                        for k in range(16)], reads=cb.B + w.B, writes=ps.B)
            cols = slice((n % 4) * 512, (n % 4 + 1) * 512)
            if n < 4:
                S.op("dve", lambda e: e.tensor_tensor(out=self.SH[:, cols], in0=ps[:], in1=bb[:], op=ALU.add),
                     reads=ps.B + bb.B, writes=self.SH.B)
            elif n < 8:
                tmp = self.tmpA[n % 2]
                S.op("dve", lambda e: e.tensor_tensor(out=tmp[:], in0=ps[:], in1=bb[:], op=ALU.add),
                     reads=ps.B + bb.B, writes=tmp.B)
                S.op("dve", lambda e: e.scalar_tensor_tensor(out=self.G1[:, cols], in0=tmp[:], scalar=1.0, in1=self.ngb[:, cols],
                                                             op0=ALU.add, op1=ALU.mult),
                     reads=tmp.B + self.ngb.B, writes=self.G1.B)
            else:
                S.op("dve", lambda e: e.tensor_tensor(out=self.GT[:, cols], in0=ps[:], in1=bb[:], op=ALU.add),
                     reads=ps.B + bb.B, writes=self.GT.B)

    def hsrc(self, li, stream):
        if stream == "ctx":
            return self.ctx if (0 not in self.layers or li == 0) else self.hctx
        return self.x if li == self.layers[0] else self.out

    def pass1(self, li, stream):
        S = self.S
        src = self.hsrc(li, stream)
        ntile = (NCTX if stream == "ctx" else SEQ) // 128
        col0 = UT_CTX0 if stream == "ctx" else UT_LAT0
        utv = self.UT[:].rearrange("(k p) t -> p k t", p=128)
        for j in range(ntile):
            ht = self.ht[j % 2]
            sbuf = src.b[j] if len(src.b) > 1 else src.b[0]
            S.dma("sp", ht[:], src[j * 128:(j + 1) * 128, :], reads=[sbuf], writes=ht.B)
            self.norm_u(ht, self.G1, self.SH)
            uTs = self.uTs[j % 2]
            for g in range(4):
                S.op("pe", [(lambda e, q=q: e.transpose(self.psb[:, q * 128:(q + 1) * 128],
                                                        self.ub[:, (g * 4 + q) * 128:(g * 4 + q + 1) * 128], self.ident[:]))
                            for q in range(4)], reads=self.ub.B + self.ident.B, writes=self.psb.B)
                S.op("act", lambda e: e.copy(out=uTs[:, g * 4:(g + 1) * 4, :],
                                             in_=self.psb[:, 0:512].rearrange("p (q t) -> p q t", q=4)),
                     reads=self.psb.B, writes=uTs.B)
            S.dma("pool", utv[:, :, col0 + j * 128: col0 + (j + 1) * 128], uTs[:], reads=uTs.B, writes=self.UT.B)

    def norm_u(self, ht, G1, SH):
        S = self.S
        nt, ss = self.nt, self.ss
        S.op("dve", lambda e: e.tensor_tensor(out=nt[:], in0=ht[:], in1=ht[:], op=ALU.mult), reads=ht.B, writes=nt.B)
        S.op("dve", lambda e: e.reduce_sum(out=ss[:, 0:1], in_=nt[:], axis=AX.X), reads=nt.B, writes=ss.B)
        S.op("dve", lambda e: e.tensor_scalar(out=ss[:, 1:2], in0=ss[:, 0:1], scalar1=1.0 / D, scalar2=EPS,
                                              op0=ALU.mult, op1=ALU.add), reads=ss.B, writes=ss.B)
        S.op("act", lambda e: e.activation(out=ss[:, 2:3], in_=ss[:, 1:2], func=AF.Sqrt), reads=ss.B, writes=ss.B)
        S.op("dve", lambda e: e.reciprocal(out=ss[:, 3:4], in_=ss[:, 2:3]), reads=ss.B, writes=ss.B)
        S.op("dve", lambda e: e.scalar_tensor_tensor(out=nt[:], in0=ht[:], scalar=ss[:, 3:4], in1=G1[:],
                                                     op0=ALU.mult, op1=ALU.mult), reads=ht.B + ss.B + G1.B, writes=nt.B)
        S.op("dve", lambda e: e.tensor_tensor(out=self.ub[:], in0=nt[:], in1=SH[:], op=ALU.add),
             reads=nt.B + SH.B, writes=self.ub.B)

    def out_proj(self, li, stream, vT, wout, t0, ntok, dst_only_lat=True):
        S = self.S
        src = self.hsrc(li, stream)
        dst = self.hctx if stream == "ctx" else self.out
        nsub = ntok // 128
        hts = []
        for j in range(nsub):
            ht = self.ht[j % 2]
            jj = t0 // 128 + j
            sbuf = src.b[jj] if len(src.b) > 1 else src.b[0]
            S.dma("sp", ht[:], src[t0 + j * 128: t0 + (j + 1) * 128, :], reads=[sbuf], writes=ht.B)
            hts.append(ht)
        wv = wout[:].rearrange("(c p) d -> p c d", p=128)
        for dch in range(4):
            ws = []
            for half in range(2):
                w = self.next_w()
                wt = w[:].rearrange("p (k n) -> p k n", k=16)
                S.dma("sp", wt, wv[:, half * 16:(half + 1) * 16, dch * 512:(dch + 1) * 512],
                      reads=wout.b[half * 16:(half + 1) * 16], writes=w.B)
                ws.append((w, wt))
            for j in range(nsub):
                ps = self.next_ps()
                S.op("pe", [(lambda e, c=c: e.matmul(ps[:], lhsT=vT[:, c, j * 128:(j + 1) * 128], rhs=ws[c // 16][1][:, c % 16, :],
                                                     start=(c == 0), stop=(c == EC - 1))) for c in range(EC)],
                     reads=vT.B + ws[0][0].B + ws[1][0].B, writes=ps.B)
                tmp = self.tmpA[j % 2]
                cols = slice(dch * 512, (dch + 1) * 512)
                S.op("dve", lambda e: e.tensor_tensor(out=tmp[:], in0=ps[:], in1=self.GT[:, cols], op=ALU.mult),
                     reads=ps.B + self.GT.B, writes=tmp.B)
                S.op("dve", lambda e: e.tensor_tensor(out=hts[j][:, cols], in0=hts[j][:, cols], in1=tmp[:], op=ALU.add),
                     reads=tmp.B + hts[j].B, writes=hts[j].B)
        for j in range(nsub):
            jj = t0 // 128 + j
            S.dma("pool", dst[t0 + j * 128: t0 + (j + 1) * 128, :], hts[j][:], reads=hts[j].B, writes=[dst.b[jj]])

    def layer_conv(self, li, j):
        S = self.S
        TT = 256
        w_in = self.convert_weight(f"cv_in{j}", self.cv_w_in[j], D, 3 * E)
        w_out = self.convert_weight(f"cv_out{j}", self.cv_w_out[j], E, D)
        with contextlib.ExitStack() as les:
            self.alloc_commons(les)
            self.cv_uT = self.sb("cv_uT", [128, KC, TT], BF16, st=les)
            self.cv_c = self.sb("cv_c", [128, EC, TT], F32, nb=EC, st=les)
            self.cv_z = self.sb("cv_z", [128, EC, TT], BF16, nb=EC, st=les)
            self.cv_y = [self.sb(f"cv_y{i}", [128, 512], BF16, st=les) for i in range(2)]
            self.cv_sig = [self.sb(f"cv_sig{i}", [128, TT], F32, st=les) for i in range(2)]
            self.cv_sq = [self.sb(f"cv_sq{i}", [128, TT], F32, st=les) for i in range(2)]
            self.cv_dwt = self.sb("cv_dwt", [128, EC, CONVW], st=les)
            self.cv_vec = self.sb("cv_vec", [128, 3, EC], st=les)
            self.cv_st = self.sb("cv_st", [128, 4, TT], st=les)
            self._layer_conv_body(li, j, w_in, w_out, TT)
            S.barrier()

    def _layer_conv_body(self, li, j, w_in, w_out, TT):
        S = self.S
        S.dma("sp", self.cv_dwt[:], self.cv_dw[j], reads=self.cv_dw.B, writes=self.cv_dwt.B)
        for q, src in enumerate((self.cv_dwb, self.cv_lng, self.cv_lnb)):
            S.dma("sp", self.cv_vec[:, q, :], src[j], reads=src.B, writes=self.cv_vec.B)
        streams = ["ctx", "lat"] if li == 0 else ["lat"]
        for stream in streams:
            self.ada_mod(li, stream)
            self.pass1(li, stream)
            ntok = NCTX if stream == "ctx" else SEQ
            rowlen = NCTX if stream == "ctx" else 64
            col0 = UT_CTX0 if stream == "ctx" else UT_LAT0
            for yb in self.cv_y:
                S.op("dve", lambda e: e.memset(yb[:], 0.0), writes=yb.B)
            for tt in range(ntok // TT):
                self.conv_tile(li, stream, w_in, w_out, tt * TT, TT, rowlen, col0)

    def conv_tile(self, li, stream, w_in, w_out, t0, TT, rowlen, col0):
        S = self.S
        uT = self.cv_uT
        utv = self.UT[:].rearrange("(k p) t -> p k t", p=128)
        S.dma("sp", uT[:], utv[:, :, col0 + t0: col0 + t0 + TT], reads=self.UT.B, writes=uT.B)
        wv = w_in[:].rearrange("(k p) n -> p k n", p=128)
        nrow = TT // rowlen
        psS, psQ = self.next_ps(), self.next_ps()
        for g in range(EC // 4):
            wts = []
            for part in range(3):
                w = self.next_w()
                wt = w[:].rearrange("p (k n) -> p k n", k=16)
                S.dma("sp", wt, wv[:, :, part * E + g * 512: part * E + (g + 1) * 512], reads=w_in.B, writes=w.B)
                wts.append((w, wt))
            for q in range(4):
                ec = g * 4 + q
                pss = []
                for part in range(3):
                    ps = self.next_ps()
                    while ps is psS or ps is psQ:
                        ps = self.next_ps()
                    wt = wts[part][1]
                    S.op("pe", [(lambda e, k=k, ps=ps, wt=wt: e.matmul(ps[:, 0:TT], lhsT=wt[:, k, q * 128:(q + 1) * 128], rhs=uT[:, k, :],
                                                                       start=(k == 0), stop=(k == KC - 1))) for k in range(KC)],
                         reads=uT.B + wts[part][0].B, writes=ps.B)
                    pss.append(ps)
                sig, y, sq = self.cv_sig[ec % 2], self.cv_y[ec % 2], self.cv_sq[ec % 2]
                S.op("act", lambda e: e.activation(out=sig[:], in_=pss[1][:, 0:TT], func=AF.Sigmoid), reads=pss[1].B, writes=sig.B)
                PW = rowlen + 30
                yp3 = y[:, 0:nrow * PW].rearrange("p (r t) -> p r t", r=nrow)
                r3 = lambda ap: ap.rearrange("p (r t) -> p r t", r=nrow)
                S.op("dve", lambda e: e.tensor_tensor(out=yp3[:, :, 15:15 + rowlen], in0=r3(pss[0][:, 0:TT]), in1=r3(sig[:]), op=ALU.mult),
                     reads=pss[0].B + sig.B, writes=y.B)
                S.op("act", lambda e: e.activation(out=self.cv_z[:, ec, :], in_=pss[2][:, 0:TT], func=AF.Silu),
                     reads=pss[2].B, writes=[self.cv_z.b[ec]])
                dgT = (self.G1, self.SH)[ec % 2]
                dg = dgT[:].bitcast(BF16)[:, 0:CONVW * 128].rearrange("p (t m) -> p t m", t=CONVW)
                ia = self.ident[:]
                ident_bc = AP(ia.tensor, ia.offset, [list(ia.ap[0]), [0, CONVW], list(ia.ap[1])])
                S.op("dve", lambda e: e.tensor_tensor(out=dg, in0=ident_bc, in1=bcast_free(self.cv_dwt[:, ec, :], 128), op=ALU.mult),
                     reads=self.ident.B + self.cv_dwt.B, writes=dgT.B)
                psC = self.next_ps()
                while psC is psS or psC is psQ:
                    psC = self.next_ps()
                S.op("pe", [(lambda e, tap=tap: e.matmul(r3(psC[:, 0:TT]), lhsT=dg[:, tap, :], rhs=yp3[:, :, tap:tap + rowlen],
                                                         start=(tap == 0), stop=(tap == CONVW - 1))) for tap in range(CONVW)],
                     reads=dgT.B + y.B, writes=psC.B)
                cb = self.cv_c.b[ec]
                cflat = self.cv_c[:, ec, :]
                S.op("dve", lambda e: e.tensor_scalar(out=cflat, in0=psC[:, 0:TT], scalar1=self.cv_vec[:, 0, ec:ec + 1], scalar2=None, op0=ALU.add),
                     reads=psC.B + self.cv_vec.B, writes=[cb])
                S.op("act", lambda e: e.activation(out=sq[:], in_=cflat, func=AF.Square), reads=[cb], writes=sq.B)
                S.op("pe", lambda e: e.matmul(psS[:, 0:TT], lhsT=self.ones_f[:], rhs=cflat, start=(ec == 0), stop=(ec == EC - 1)),
                     reads=[cb] + self.ones_f.B, writes=psS.B)
                S.op("pe", lambda e: e.matmul(psQ[:, 0:TT], lhsT=self.ones_f[:], rhs=sq[:], start=(ec == 0), stop=(ec == EC - 1)),
                     reads=sq.B + self.ones_f.B, writes=psQ.B)
        st = self.cv_st
        S.op("act", lambda e: e.mul(out=st[:, 0, :], in_=psS[:, 0:TT], mul=1.0 / E), reads=psS.B, writes=st.B)
        S.op("act", lambda e: e.mul(out=st[:, 1, :], in_=psQ[:, 0:TT], mul=1.0 / E), reads=psQ.B, writes=st.B)
        S.op("dve", lambda e: e.tensor_tensor(out=st[:, 2, :], in0=st[:, 0, :], in1=st[:, 0, :], op=ALU.mult), reads=st.B, writes=st.B)
        S.op("dve", lambda e: e.tensor_tensor(out=st[:, 1, :], in0=st[:, 1, :], in1=st[:, 2, :], op=ALU.subtract), reads=st.B, writes=st.B)
        S.op("dve", lambda e: e.tensor_scalar(out=st[:, 1, :], in0=st[:, 1, :], scalar1=EPS, scalar2=None, op0=ALU.add), reads=st.B, writes=st.B)
        S.op("act", lambda e: e.activation(out=st[:, 2, :], in_=st[:, 1, :], func=AF.Sqrt), reads=st.B, writes=st.B)
        S.op("dve", lambda e: e.reciprocal(out=st[:, 3, :], in_=st[:, 2, :]), reads=st.B, writes=st.B)
        for ec in range(EC):
            cb = self.cv_c.b[ec]
            cflat = self.cv_c[:, ec, :]
            zb = self.cv_z.b[ec]
            S.op("dve", lambda e: e.tensor_tensor(out=cflat, in0=cflat, in1=st[:, 0, :], op=ALU.subtract), reads=[cb] + st.B, writes=[cb])
            S.op("dve", lambda e: e.tensor_tensor(out=cflat, in0=cflat, in1=st[:, 3, :], op=ALU.mult), reads=[cb] + st.B, writes=[cb])
            S.op("act", lambda e: e.activation(out=cflat, in_=cflat, func=AF.Silu, scale=self.cv_vec[:, 1, ec:ec + 1],
                                               bias=self.cv_vec[:, 2, ec:ec + 1]), reads=[cb] + self.cv_vec.B, writes=[cb])
            S.op("dve", lambda e: e.tensor_tensor(out=self.cv_z[:, ec, :], in0=self.cv_z[:, ec, :], in1=cflat, op=ALU.mult),
                 reads=[cb, zb], writes=[zb])
        self.out_proj(li, stream, self.cv_z, w_out, t0, TT)

    def final_norm(self):
        S = self.S
        src = self.x if len(self.layers) == 0 else self.out
        with contextlib.ExitStack() as les:
            self.alloc_commons(les)
            self._final_body()
            S.barrier()

    def _final_body(self):
        S = self.S
        src = self.x if len(self.layers) == 0 else self.out
        gb = self.G1
        S.dma("sp", gb[:], self.final_g[:].partition_broadcast(128), reads=self.final_g.B, writes=gb.B)
        for j in range(SEQ // 128):
            ht = self.ht[j % 2]
            sbuf = src.b[j] if len(src.b) > 1 else src.b[0]
            S.dma("sp", ht[:], src[j * 128:(j + 1) * 128, :], reads=[sbuf], writes=ht.B)
            nt, ss = self.nt, self.ss
            S.op("dve", lambda e: e.tensor_tensor(out=nt[:], in0=ht[:], in1=ht[:], op=ALU.mult), reads=ht.B, writes=nt.B)
            S.op("dve", lambda e: e.reduce_sum(out=ss[:, 0:1], in_=nt[:], axis=AX.X), reads=nt.B, writes=ss.B)
            S.op("dve", lambda e: e.tensor_scalar(out=ss[:, 1:2], in0=ss[:, 0:1], scalar1=1.0 / D, scalar2=EPS,
                                                  op0=ALU.mult, op1=ALU.add), reads=ss.B, writes=ss.B)
            S.op("act", lambda e: e.activation(out=ss[:, 2:3], in_=ss[:, 1:2], func=AF.Sqrt), reads=ss.B, writes=ss.B)
            S.op("dve", lambda e: e.reciprocal(out=ss[:, 3:4], in_=ss[:, 2:3]), reads=ss.B, writes=ss.B)
            S.op("dve", lambda e: e.scalar_tensor_tensor(out=ht[:], in0=ht[:], scalar=ss[:, 3:4], in1=gb[:],
                                                         op0=ALU.mult, op1=ALU.mult), reads=ht.B + ss.B + gb.B, writes=ht.B)
            S.dma("pool", self.out[j * 128:(j + 1) * 128, :], ht[:], reads=ht.B, writes=[self.out.b[j]])

    def copy_out(self):
        pass


    def layer_mlstm(self, li, j):
        S = self.S
        TS = NCTX + SEQ
        NCH = TS // LCH
        w_in = self.convert_weight("ml_in", self.ml_w_in[:], D, 2 * E)
        w_out = self.convert_weight("ml_out", self.ml_w_out[:], E, D)
        bdb = self.convert_weight("ml_bd", self.ml_bd[:], 4 * EC * 128, 128)
        ds = self.dram_scratch
        qTd, kTd = ds("ml_qT", [E, TS], BF16), ds("ml_kT", [E, TS], BF16)
        ktmd, vtmd = ds("ml_ktm", [TS, E], BF16), ds("ml_vtm", [TS, E], BF16)
        P1d, P2d = ds("ml_P1", [E, TS], BF16), ds("ml_P2", [E, TS], BF16)
        Hd, HNd = ds("ml_H", [TS, E], F32), ds("ml_HN", [TS, E], BF16)
        GPd = ds("ml_GP", [4, 8, TS], F32)
        GQd = ds("ml_GQ", [2, 5, 8, TS], F32)
        DQd = ds("ml_DQ", [2, 8, NCH], F32)
        S.barrier()
        with contextlib.ExitStack() as les:
            self.alloc_commons(les)
            for stream in ("ctx", "lat"):
                self.ada_mod(li, stream)
                self.pass1(li, stream)
            if getattr(self, "ml_stop", 9) > 0.5:
                self.ml_proj(les, w_in, bdb, qTd, kTd, ktmd, vtmd, P1d, P2d, GPd)
            S.barrier()
        stop = getattr(self, "ml_stop", 9)
        if stop <= 1:
            return
        self.ml_gates(GPd, GQd, DQd)
        S.barrier()
        if stop <= 2:
            return
        self.ml_scan(qTd, kTd, ktmd, vtmd, Hd, HNd, GQd, DQd)
        S.barrier()
        if stop <= 3:
            return
        with contextlib.ExitStack() as les:
            self.alloc_commons(les)
            self.ada_mod(li, "lat")
            sb = lambda n, sh, dt=F32: self.sb(n, sh, dt, st=les)
            vT, p1, p2 = sb("mo_vT", [128, EC, 256], BF16), sb("mo_p1", [128, EC, 256], BF16), sb("mo_p2", [128, EC, 256], BF16)
            hn = [sb(f"mo_hn{i}", [128, E], BF16) for i in range(2)]
            for tt in range(SEQ // 256):
                c0 = NCTX + tt * 256
                S.dma("sp", p1[:], P1d[:, c0:c0 + 256].rearrange("(c p) t -> p c t", p=128), reads=P1d.B, writes=p1.B)
                S.dma("sp", p2[:], P2d[:, c0:c0 + 256].rearrange("(c p) t -> p c t", p=128), reads=P2d.B, writes=p2.B)
                for sub in range(2):
                    y = hn[sub]
                    S.dma("sp", y[:], HNd[c0 + sub * 128:c0 + (sub + 1) * 128, :], reads=HNd.B, writes=y.B)
                    for g in range(EC // 4):
                        S.op("pe", [(lambda e, q=q: e.transpose(self.psb[:, q * 128:(q + 1) * 128],
                                                                y[:, (g * 4 + q) * 128:(g * 4 + q + 1) * 128], self.ident[:]))
                                    for q in range(4)], reads=y.B + self.ident.B, writes=self.psb.B)
                        S.op("act", lambda e: e.copy(out=vT[:, g * 4:(g + 1) * 4, sub * 128:(sub + 1) * 128],
                                                     in_=self.psb[:, 0:512].rearrange("p (q t) -> p q t", q=4)),
                             reads=self.psb.B, writes=vT.B)
                S.op("dve", lambda e: e.tensor_tensor(out=vT[:], in0=vT[:], in1=p1[:], op=ALU.mult), reads=vT.B + p1.B, writes=vT.B)
                S.op("dve", lambda e: e.tensor_tensor(out=vT[:], in0=vT[:], in1=p2[:], op=ALU.add), reads=vT.B + p2.B, writes=vT.B)
                self.out_proj(li, "lat", vT, w_out, tt * 256, 256)
            S.barrier()

    def ml_proj(self, les, w_in, bdb, qTd, kTd, ktmd, vtmd, P1d, P2d, GPd):
        S = self.S
        TT = 256
        sb = lambda n, sh, dt=F32: self.sb(n, sh, dt, st=les)
        uT = sb("mp_uT", [128, KC, TT + 2], BF16)
        ktm, vtm = sb("mp_ktm", [128, 2, E], BF16), sb("mp_vtm", [128, 2, E], BF16)
        wgf, wg = sb("mp_wgf", [128, 4 * 96 * 8]), sb("mp_wg", [128, 4, 96, 8], BF16)
        bd = [sb(f"mp_bd{i}", [128, 4, 128], BF16) for i in range(2)]
        cw, vec = sb("mp_cw", [128, EC, 3]), sb("mp_vec", [128, 3, EC])
        bg = sb("mp_bg", [8, 8])
        a_, xc, zs, og = sb("mp_a", [128, TT]), sb("mp_xc", [128, TT]), sb("mp_zs", [128, TT]), sb("mp_o", [128, TT])
        xmb, xcb = sb("mp_xmb", [128, TT], BF16), sb("mp_xcb", [128, TT], BF16)
        qb, kb, vb = sb("mp_qb", [128, TT], BF16), sb("mp_kb", [128, TT], BF16), sb("mp_vb", [128, TT], BF16)
        p1, p2 = sb("mp_p1", [128, TT], BF16), sb("mp_p2", [128, TT], BF16)
        gst = sb("mp_gst", [8, 4, TT])
        S.dma("sp", wgf[:], self.ml_wg[:], reads=self.ml_wg.B, writes=wgf.B)
        S.op("dve", lambda e: e.tensor_copy(out=wg[:].rearrange("p a b c -> p (a b c)"), in_=wgf[:]), reads=wgf.B, writes=wg.B)
        S.dma("sp", cw[:], self.ml_cw[:], reads=self.ml_cw.B, writes=cw.B)
        S.dma("sp", vec[:], self.ml_vec[:], reads=self.ml_vec.B, writes=vec.B)
        S.dma("sp", bg[:, 0:4], self.ml_bg[:], reads=self.ml_bg.B, writes=bg.B)
        utv = self.UT[:].rearrange("(k p) t -> p k t", p=128)
        wv = w_in[:].rearrange("(k p) n -> p k n", p=128)
        bdv = bdb[:].rearrange("(a c p) m -> c p a m", a=4, c=EC)
        psG = [self.next_ps() for _ in range(4)]
        held = tuple(psG)

        def nps():
            p = self.next_ps()
            while p in held:
                p = self.next_ps()
            return p

        for tt in range(1 + SEQ // TT):
            uc0 = UT_CTX0 - 1 if tt == 0 else UT_LAT0 + (tt - 1) * TT - 1
            tok0 = 0 if tt == 0 else NCTX + (tt - 1) * TT
            S.dma("sp", uT[:], utv[:, :, uc0:uc0 + TT + 2], reads=self.UT.B, writes=uT.B)
            for g in range(EC // 4):
                wts = []
                for part in range(2):
                    w = self.next_w()
                    wt = w[:].rearrange("p (k n) -> p k n", k=16)
                    S.dma("sp", wt, wv[:, :, part * E + g * 512: part * E + (g + 1) * 512], reads=w_in.B, writes=w.B)
                    wts.append((w, wt))
                for q in range(4):
                    ec = g * 4 + q
                    b_ = bd[ec % 2]
                    if getattr(self, "ml_stop", 9) > 0.62:
                        S.dma("sp", b_[:], bdv[ec], reads=bdb.B, writes=b_.B)
                    pss = []
                    for part in range(2):
                        ps = nps()
                        wt = wts[part][1]
                        S.op("pe", [(lambda e, k=k, ps=ps, wt=wt: e.matmul(ps[:, 0:TT + 2], lhsT=wt[:, k, q * 128:(q + 1) * 128], rhs=uT[:, k, :],
                                                                           start=(k == 0), stop=(k == KC - 1))) for k in range(KC)],
                             reads=uT.B + wts[part][0].B, writes=ps.B)
                        pss.append(ps)
                    pxm, pz = pss
                    if getattr(self, "ml_stop", 9) < 0.606:
                        continue
                    lv = getattr(self, "ml_stop", 9)
                    S.op("dve", lambda e: e.tensor_copy(out=xmb[:], in_=pxm[:, 1:TT + 1]), reads=pxm.B, writes=xmb.B)
                    if lv < 0.6075:
                        continue
                    S.op("dve", lambda e: e.tensor_scalar(out=a_[:], in0=pxm[:, 1:TT + 1], scalar1=cw[:, ec, 1:2], scalar2=vec[:, 0, ec:ec + 1],
                                                          op0=ALU.mult, op1=ALU.add), reads=pxm.B + cw.B + vec.B, writes=a_.B)
                    S.op("dve", lambda e: e.scalar_tensor_tensor(out=a_[:], in0=pxm[:, 0:TT], scalar=cw[:, ec, 0:1], in1=a_[:],
                                                                 op0=ALU.mult, op1=ALU.add), reads=pxm.B + cw.B + a_.B, writes=a_.B)
                    S.op("dve", lambda e: e.scalar_tensor_tensor(out=a_[:], in0=pxm[:, 2:TT + 2], scalar=cw[:, ec, 2:3], in1=a_[:],
                                                                 op0=ALU.mult, op1=ALU.add), reads=pxm.B + cw.B + a_.B, writes=a_.B)
                    if lv < 0.6085:
                        continue
                    S.op("act", lambda e: e.activation(out=xc[:], in_=a_[:], func=AF.Silu), reads=a_.B, writes=xc.B)
                    S.op("act", lambda e: e.mul(out=xcb[:], in_=xc[:], mul=1.0), reads=xc.B, writes=xcb.B)
                    if lv < 0.6095:
                        continue
                    S.op("act", lambda e: e.activation(out=zs[:], in_=pz[:, 1:TT + 1], func=AF.Silu), reads=pz.B, writes=zs.B)
                    lvl = getattr(self, "ml_stop", 9)
                    if lvl < 0.65:
                        continue
                    for idx, src in enumerate((xcb, xcb, xmb, xcb)):
                        pp = nps()
                        S.op("pe", lambda e: e.matmul(pp[:, 0:TT], lhsT=b_[:, idx, :], rhs=src[:], start=True, stop=True),
                             reads=b_.B + src.B, writes=pp.B)
                        if idx == 0:
                            S.op("act", lambda e: e.mul(out=qb[:], in_=pp[:, 0:TT], mul=1.0), reads=pp.B, writes=qb.B)
                        elif idx == 1:
                            S.op("dve", lambda e: e.tensor_copy(out=kb[:], in_=pp[:, 0:TT]), reads=pp.B, writes=kb.B)
                        elif idx == 2:
                            S.op("act", lambda e: e.mul(out=vb[:], in_=pp[:, 0:TT], mul=1.0), reads=pp.B, writes=vb.B)
                        else:
                            S.op("act", lambda e: e.activation(out=og[:], in_=pp[:, 0:TT], func=AF.Sigmoid, bias=vec[:, 1, ec:ec + 1]),
                                 reads=pp.B + vec.B, writes=og.B)
                    S.op("dve", lambda e: e.tensor_tensor(out=p1[:], in0=og[:], in1=zs[:], op=ALU.mult), reads=og.B + zs.B, writes=p1.B)
                    S.op("dve", lambda e: e.scalar_tensor_tensor(out=p2[:], in0=xc[:], scalar=vec[:, 2, ec:ec + 1], in1=zs[:],
                                                                 op0=ALU.mult, op1=ALU.mult), reads=xc.B + vec.B + zs.B, writes=p2.B)
                    for grp in range(4 if lvl > 0.85 else 0):
                        pg = psG[grp]
                        cs_ = slice(0, TT)
                        S.op("pe", [(lambda e, i3=i3, src=src: e.matmul(pg[0:8, cs_], lhsT=wg[:, grp, i3 * 32 + ec, :], rhs=src[:],
                                                                       start=(ec == 0 and i3 == 0), stop=(ec == EC - 1 and i3 == 2)))
                                    for i3, src in enumerate((qb, kb, vb))], reads=wg.B + qb.B + kb.B + vb.B, writes=pg.B)
                    if lvl < 0.75:
                        continue
                    rows = slice(ec * 128, (ec + 1) * 128)
                    S.dma("pool", qTd[rows, tok0:tok0 + TT], qb[:], reads=qb.B, writes=qTd.B)
                    S.dma("pool", kTd[rows, tok0:tok0 + TT], kb[:], reads=kb.B, writes=kTd.B)
                    S.dma("pool", P1d[rows, tok0:tok0 + TT], p1[:], reads=p1.B, writes=P1d.B)
                    S.dma("pool", P2d[rows, tok0:tok0 + TT], p2[:], reads=p2.B, writes=P2d.B)
                    S.op("pe", [(lambda e, q2=q2: e.transpose(self.psb[:, q2 * 128:(q2 + 1) * 128],
                                                              (kb if q2 < 2 else vb)[:, (q2 % 2) * 128:(q2 % 2 + 1) * 128], self.ident[:]))
                                for q2 in range(4)], reads=kb.B + vb.B + self.ident.B, writes=self.psb.B)
                    S.op("act", lambda e: e.copy(out=ktm[:, :, ec * 128:(ec + 1) * 128],
                                                 in_=self.psb[:, 0:256].rearrange("p (s t) -> p s t", s=2)), reads=self.psb.B, writes=ktm.B)
                    S.op("act", lambda e: e.copy(out=vtm[:, :, ec * 128:(ec + 1) * 128],
                                                 in_=self.psb[:, 256:512].rearrange("p (s t) -> p s t", s=2)), reads=self.psb.B, writes=vtm.B)
            if getattr(self, "ml_stop", 9) < 0.75:
                continue
            for sub in range(2):
                r0 = tok0 + sub * 128
                S.dma("pool", ktmd[r0:r0 + 128, :], ktm[:, sub, :], reads=ktm.B, writes=ktmd.B)
                S.dma("pool", vtmd[r0:r0 + 128, :], vtm[:, sub, :], reads=vtm.B, writes=vtmd.B)
            if getattr(self, "ml_stop", 9) < 0.85:
                continue
            for grp in range(4):
                pg = psG[grp]
                cs_ = slice(0, TT)
                S.op("dve", lambda e: e.tensor_scalar(out=gst[:, grp, :], in0=pg[0:8, cs_], scalar1=bg[:, grp:grp + 1], scalar2=None, op0=ALU.add),
                     reads=pg.B + bg.B, writes=gst.B)
            S.dma("pool", GPd[:, :, tok0:tok0 + TT].rearrange("g h t -> h g t"), gst[:], reads=gst.B, writes=GPd.B)

    def ml_gates(self, GPd, GQd, DQd):
        S = self.S
        TS = NCTX + SEQ
        NCH = TS // LCH
        LNS = math.log(DH ** -0.5)
        with contextlib.ExitStack() as st:
            sb = lambda n, sh, dt=F32: self.sb(n, sh, dt, st=st)
            li, fp, t0, t1 = sb("mg_li", [8, TS]), sb("mg_fp", [8, TS]), sb("mg_t0", [8, TS]), sb("mg_t1", [8, TS])
            ones, Bc, Gg, Mx = sb("mg_one", [8, TS]), sb("mg_B", [8, TS]), sb("mg_Gg", [8, TS]), sb("mg_Mx", [8, TS])
            res = sb("mg_res", [8, TS])
            mp_, me_, dc_ = sb("mg_mp", [8, 72]), sb("mg_me", [8, 72]), sb("mg_dc", [8, 72])
            mp, me, dc = V(mp_, NCH), V(me_, NCH), V(dc_, NCH)
            S.op("dve", lambda e: e.memset(ones[:], 1.0), writes=ones.B)
            segs = ((0, NCTX), (NCTX, SEQ))

            def rev(dst, src):
                for (o, n) in segs:
                    a = src[:, o:o + n]
                    r = AP(a.tensor, a.offset + n - 1, [list(a.ap[0]), [-1, n]])
                    S.op("dve", lambda e: e.tensor_copy(out=dst[:, o:o + n], in_=r), reads=src.B, writes=dst.B)

            def c3(t):
                return t[:].rearrange("p (c l) -> p c l", l=LCH)

            def bc3(t):
                a = t[:]
                return AP(a.tensor, a.offset, [list(a.ap[0]), list(a.ap[1]), [0, LCH]])

            for d in range(2):
                S.dma("sp", t0[:], GPd[2 * d], reads=GPd.B, writes=t0.B)
                S.dma("sp", t1[:], GPd[2 * d + 1], reads=GPd.B, writes=t1.B)
                if d == 0:
                    S.op("dve", lambda e: e.tensor_copy(out=li[:], in_=t0[:]), reads=t0.B, writes=li.B)
                    S.op("dve", lambda e: e.tensor_copy(out=fp[:], in_=t1[:]), reads=t1.B, writes=fp.B)
                else:
                    rev(li, t0)
                    rev(fp, t1)
                S.op("act", lambda e: e.activation(out=t0[:], in_=fp[:], func=AF.Exp, scale=-1.0), reads=fp.B, writes=t0.B)
                S.op("act", lambda e: e.activation(out=t1[:], in_=t0[:], func=AF.Ln, bias=1.0), reads=t0.B, writes=t1.B)
                S.op("dve", lambda e: e.tensor_scalar(out=t1[:], in0=t1[:], scalar1=-1.0, scalar2=None, op0=ALU.mult), reads=t1.B, writes=t1.B)
                S.op("dve", lambda e: e.tensor_tensor_scan(out=Bc[:], data0=ones[:], data1=t1[:], initial=0.0, op0=ALU.mult, op1=ALU.add),
                     reads=ones.B + t1.B, writes=Bc.B)
                S.op("dve", lambda e: e.tensor_tensor(out=Gg[:], in0=li[:], in1=Bc[:], op=ALU.subtract), reads=li.B + Bc.B, writes=Gg.B)
                S.op("dve", lambda e: e.tensor_tensor_scan(out=Mx[:], data0=Gg[:], data1=Gg[:], initial=-1e30, op0=ALU.max, op1=ALU.max),
                     reads=Gg.B, writes=Mx.B)
                S.op("dve", lambda e: e.tensor_copy(out=me[:], in_=c3(Mx)[:, :, LCH - 1]), reads=Mx.B, writes=me.B)
                S.op("dve", lambda e: e.memset(mp[:, 0:1], -1e30), writes=mp.B)
                S.op("dve", lambda e: e.tensor_copy(out=mp[:, 1:NCH], in_=me[:, 0:NCH - 1]), reads=me.B, writes=mp.B)
                S.op("dve", lambda e: e.tensor_tensor(out=dc[:], in0=mp[:], in1=me[:], op=ALU.subtract), reads=mp.B + me.B, writes=dc.B)
                S.op("act", lambda e: e.activation(out=dc[:], in_=dc[:], func=AF.Exp), reads=dc.B, writes=dc.B)
                S.dma("pool", DQd[d], dc[:], reads=dc.B, writes=DQd.B)

                def emit(qi, src):
                    if d == 0:
                        S.dma("pool", GQd[d, qi], src[:], reads=src.B, writes=GQd.B)
                    else:
                        rev(res, src)
                        S.dma("pool", GQd[d, qi], res[:], reads=res.B, writes=GQd.B)

                S.op("dve", lambda e: e.tensor_scalar(out=t0[:], in0=Gg[:], scalar1=LNS, scalar2=None, op0=ALU.add), reads=Gg.B, writes=t0.B)
                emit(0, t0)
                emit(1, Mx)
                S.op("dve", lambda e: e.tensor_tensor(out=c3(t0), in0=bc3(mp), in1=c3(Mx), op=ALU.subtract), reads=mp.B + Mx.B, writes=t0.B)
                S.op("act", lambda e: e.activation(out=t0[:], in_=t0[:], func=AF.Exp, bias=LNS), reads=t0.B, writes=t0.B)
                emit(2, t0)
                S.op("dve", lambda e: e.tensor_tensor(out=t1[:], in0=Bc[:], in1=Mx[:], op=ALU.add), reads=Bc.B + Mx.B, writes=t1.B)
                S.op("act", lambda e: e.activation(out=t1[:], in_=t1[:], func=AF.Exp, scale=-1.0), reads=t1.B, writes=t1.B)
                emit(3, t1)
                S.op("dve", lambda e: e.tensor_tensor(out=c3(t0), in0=c3(Gg), in1=bc3(me), op=ALU.subtract), reads=me.B + Gg.B, writes=t0.B)
                S.op("act", lambda e: e.activation(out=t0[:], in_=t0[:], func=AF.Exp), reads=t0.B, writes=t0.B)
                emit(4, t0)
            S.barrier()

    def ml_scan(self, qTd, kTd, ktmd, vtmd, Hd, HNd, GQd, DQd):
        S = self.S
        TS = NCTX + SEQ
        NCH = TS // LCH
        NG = NCH // 4
        NSL = 2
        with contextlib.ExitStack() as st:
            sb = lambda n, sh, dt=F32: self.sb(n, sh, dt, st=st)
            mask = [sb(f"ms_mask{i}", [64, 64]) for i in range(2)]
            onesb = V(sb("ms_1b", [64, 16], BF16), 1)
            S.dma("sp", mask[0][:], self.ml_mask[0], reads=self.ml_mask.B, writes=mask[0].B)
            S.dma("sp", mask[1][:], self.ml_mask[1], reads=self.ml_mask.B, writes=mask[1].B)
            S.op("dve", lambda e: e.memset(onesb[:], 1.0), writes=onesb.B)
            slots = []
            for si in range(NSL):
                o = {}
                n_ = lambda x: f"ms{si}_{x}"
                o["C"], o["Cb"] = sb(n_("C"), [128, 4, DH]), sb(n_("Cb"), [128, 4, DH], BF16)
                o["nn"], o["nb"] = V(sb(n_("n"), [128, 8]), 4), V(sb(n_("nb"), [128, 16], BF16), 4)
                o["cols"] = [V(sb(n_(f"col{i}"), [64, 72]), NCH) for i in range(4)]
                o["mxr"], o["dcr"] = sb(n_("mxr"), [64, TS]), V(sb(n_("dcr"), [128, 72]), NCH)
                o["mhg"] = sb(n_("mhg"), [64, DH])
                for nm in ("qT", "kT"):
                    o[nm] = [sb(n_(f"{nm}{i}"), [128, 4, 256], BF16) for i in range(2)]
                for nm in ("ktm", "vtm"):
                    o[nm] = [sb(n_(f"{nm}{i}"), [64, 4, DH], BF16) for i in range(2)]
                o["DT"], o["DTm"], o["STb"] = sb(n_("DT"), [64, 64]), sb(n_("DTm"), [64, 64]), sb(n_("STb"), [64, 64], BF16)
                for nm in ("t1", "num", "hc", "hf", "sq"):
                    o[nm] = sb(n_(nm), [64, DH])
                o["hnb"], o["kw"] = sb(n_("hnb"), [64, DH], BF16), sb(n_("kw"), [64, DH], BF16)
                o["sm"] = sb(n_("sm"), [64, 16])
                slots.append(o)
            for d in range(2):
                if d == 1:
                    S.barrier()
                gorder = list(range(NG)) if d == 0 else [0] + list(range(NG - 1, 0, -1))
                for hp in range(0, NH, NSL):
                    for si in range(NSL):
                        o, h = slots[si], hp + si
                        for qi in range(4):
                            src = GQd[d, (0, 2, 3, 4)[qi], h].rearrange("(c l) -> l c", l=LCH)
                            S.dma("sp", o["cols"][qi][:], src, reads=GQd.B, writes=o["cols"][qi].B, allow_slow_non_contiguous=True)
                        S.dma("sp", o["mxr"][:], GQd[d, 1, h].partition_broadcast(64), reads=GQd.B, writes=o["mxr"].B)
                        S.dma("sp", o["dcr"][:], DQd[d, h].partition_broadcast(128), reads=DQd.B, writes=o["dcr"].B)
                        if d == 1:
                            S.dma("sp", o["mhg"][:], self.ml_mhg[h * DH:(h + 1) * DH].partition_broadcast(64), reads=self.ml_mhg.B, writes=o["mhg"].B)
                        S.op("dve", lambda e: e.memset(o["C"][:], 0.0), writes=o["C"].B)
                        S.op("dve", lambda e: e.memset(o["nn"][:], 0.0), writes=o["nn"].B)
                        S.op("act", lambda e: e.mul(out=o["Cb"][:], in_=o["C"][:], mul=1.0), reads=o["C"].B, writes=o["Cb"].B)
                        S.op("act", lambda e: e.mul(out=o["nb"][:], in_=o["nn"][:], mul=1.0), reads=o["nn"].B, writes=o["nb"].B)
                    cp = 0
                    for gi_n, grp in enumerate(gorder):
                        tg0 = grp * 256
                        b_ = gi_n % 2
                        for si in range(NSL):
                            o, h = slots[si], hp + si
                            hrows = slice(h * DH, (h + 1) * DH)
                            S.dma("sp", o["qT"][b_][:], qTd[hrows, tg0:tg0 + 256].rearrange("(c p) t -> p c t", p=128), reads=qTd.B, writes=o["qT"][b_].B)
                            S.dma("sp", o["kT"][b_][:], kTd[hrows, tg0:tg0 + 256].rearrange("(c p) t -> p c t", p=128), reads=kTd.B, writes=o["kT"][b_].B)
                            S.dma("sp", o["ktm"][b_][:], ktmd[tg0:tg0 + 256, hrows].rearrange("(c l) e -> l c e", l=LCH), reads=ktmd.B, writes=o["ktm"][b_].B)
                            S.dma("sp", o["vtm"][b_][:], vtmd[tg0:tg0 + 256, hrows].rearrange("(c l) e -> l c e", l=LCH), reads=vtmd.B, writes=o["vtm"][b_].B)
                        for ci in (range(4) if d == 0 else range(3, -1, -1)):
                            gens = [self.ml_unit(slots[si], d, hp + si, grp * 4 + ci, ci, b_, cp, mask, onesb, Hd, HNd) for si in range(NSL)]
                            live = list(gens)
                            while live:
                                for g_ in list(live):
                                    try:
                                        next(g_)
                                    except StopIteration:
                                        live.remove(g_)
                            cp += 1
            S.barrier()

    def ml_unit(self, o, d, h, c, ci, b_, cp, mask, onesb, Hd, HNd):
        S = self.S
        tok0 = c * LCH
        lc = slice(ci * LCH, (ci + 1) * LCH)
        hrows = slice(h * DH, (h + 1) * DH)
        qT, kT, ktm, vtm = o["qT"][b_], o["kT"][b_], o["ktm"][b_], o["vtm"][b_]
        C, Cb, nn, nb, cols, mxr, dcr, mhg = o["C"], o["Cb"], o["nn"], o["nb"], o["cols"], o["mxr"], o["dcr"], o["mhg"]
        DT, DTm, STb, t1, num, hc, hf, sq, hnb, kw, sm = (o[k] for k in ("DT", "DTm", "STb", "t1", "num", "hc", "hf", "sq", "hnb", "kw", "sm"))
        ktc, vtc = ktm[:, ci, :], vtm[:, ci, :]
        pST, pP1, pP2, pDN = self.next_ps(), self.next_ps(), self.next_ps(), self.next_ps()
        S.op("pe", [(lambda e, k=k: e.matmul(pST[0:64, 0:64], lhsT=kT[:, k, lc], rhs=qT[:, k, lc],
                                             start=(k == 0), stop=(k == 3))) for k in range(4)],
             reads=kT.B + qT.B, writes=pST.B)
        S.op("act", lambda e: e.activation(out=DT[:], in_=mxr[:, tok0:tok0 + LCH], func=AF.Exp, scale=-1.0,
                                           bias=cols[0][:, c:c + 1]), reads=mxr.B + cols[0].B, writes=DT.B)
        S.op("pool", lambda e: e.tensor_tensor(out=DTm[:], in0=DT[:], in1=mask[d][:], op=ALU.mult),
             reads=DT.B + mask[d].B, writes=DTm.B)
        yield
        S.op("dve", lambda e: e.tensor_tensor(out=STb[:], in0=pST[0:64, 0:64], in1=DTm[:], op=ALU.mult),
             reads=pST.B + DTm.B, writes=STb.B)
        yield
        S.op("pe", [(lambda e, k=k: e.matmul(pP1[0:64, :], lhsT=qT[:, k, lc], rhs=Cb[:, k, :],
                                             start=(k == 0), stop=(k == 3))) for k in range(4)],
             reads=qT.B + Cb.B, writes=pP1.B)
        S.op("pe", lambda e: e.matmul(pP2[0:64, :], lhsT=STb[:], rhs=vtc, start=True, stop=True),
             reads=STb.B + vtm.B, writes=pP2.B)
        S.op("pe", [(lambda e, k=k: e.matmul(pDN[0:64, 0:1], lhsT=qT[:, k, lc], rhs=nb[:, k:k + 1],
                                             start=(k == 0), stop=(k == 3))) for k in range(4)]
             + [lambda e: e.matmul(pDN[0:64, 1:2], lhsT=STb[:], rhs=onesb[:], start=True, stop=True)],
             reads=qT.B + nb.B + STb.B + onesb.B, writes=pDN.B)
        yield
        acol, fcol, wcol = cols[1][:, c:c + 1], cols[2][:, c:c + 1], cols[3][:, c:c + 1]
        S.op("act", lambda e: e.mul(out=t1[:], in_=pP1[0:64, :], mul=acol), reads=pP1.B + cols[1].B, writes=t1.B)
        yield
        S.op("dve", lambda e: e.tensor_tensor(out=num[:], in0=pP2[0:64, :], in1=t1[:], op=ALU.add),
             reads=pP2.B + t1.B, writes=num.B)
        S.op("act", lambda e: e.copy(out=sm[:, 0:2], in_=pDN[0:64, 0:2]), reads=pDN.B, writes=sm.B)
        S.op("dve", lambda e: e.scalar_tensor_tensor(out=sm[:, 2:3], in0=sm[:, 0:1], scalar=acol, in1=sm[:, 1:2],
                                                     op0=ALU.mult, op1=ALU.add), reads=sm.B + cols[1].B, writes=sm.B)
        S.op("dve", lambda e: e.tensor_scalar(out=sm[:, 12:13], in0=sm[:, 2:3], scalar1=-1.0, scalar2=None, op0=ALU.mult),
             reads=sm.B, writes=sm.B)
        S.op("dve", lambda e: e.tensor_tensor(out=sm[:, 3:4], in0=sm[:, 2:3], in1=sm[:, 12:13], op=ALU.max), reads=sm.B, writes=sm.B)
        S.op("dve", lambda e: e.tensor_tensor(out=sm[:, 3:4], in0=sm[:, 3:4], in1=fcol, op=ALU.max),
             reads=sm.B + cols[2].B, writes=sm.B)
        S.op("dve", lambda e: e.reciprocal(out=sm[:, 4:5], in_=sm[:, 3:4]), reads=sm.B, writes=sm.B)
        S.op("dve", lambda e: e.tensor_scalar(out=hc[:], in0=num[:], scalar1=sm[:, 4:5], scalar2=None, op0=ALU.mult),
             reads=num.B + sm.B, writes=hc.B)
        yield
        if d == 0:
            S.dma("pool", Hd[tok0:tok0 + LCH, hrows], hc[:], reads=hc.B, writes=[Buf()])
        else:
            S.dma("sp", hf[:], Hd[tok0:tok0 + LCH, hrows], reads=Hd.B, writes=hf.B)
            S.op("dve", lambda e: e.tensor_tensor(out=hc[:], in0=hc[:], in1=hf[:], op=ALU.add), reads=hc.B + hf.B, writes=hc.B)
            S.op("dve", lambda e: e.reduce_sum(out=sm[:, 5:6], in_=hc[:], axis=AX.X), reads=hc.B, writes=sm.B)
            S.op("pool", lambda e: e.tensor_tensor(out=sq[:], in0=hc[:], in1=hc[:], op=ALU.mult), reads=hc.B, writes=sq.B)
            S.op("dve", lambda e: e.reduce_sum(out=sm[:, 6:7], in_=sq[:], axis=AX.X), reads=sq.B, writes=sm.B)
            S.op("dve", lambda e: e.tensor_scalar(out=sm[:, 5:7], in0=sm[:, 5:7], scalar1=1.0 / DH, scalar2=None, op0=ALU.mult),
                 reads=sm.B, writes=sm.B)
            S.op("dve", lambda e: e.tensor_tensor(out=sm[:, 7:8], in0=sm[:, 5:6], in1=sm[:, 5:6], op=ALU.mult), reads=sm.B, writes=sm.B)
            S.op("dve", lambda e: e.tensor_tensor(out=sm[:, 8:9], in0=sm[:, 6:7], in1=sm[:, 7:8], op=ALU.subtract), reads=sm.B, writes=sm.B)
            S.op("dve", lambda e: e.tensor_scalar(out=sm[:, 8:9], in0=sm[:, 8:9], scalar1=EPS, scalar2=None, op0=ALU.add), reads=sm.B, writes=sm.B)
            S.op("act", lambda e: e.activation(out=sm[:, 9:10], in_=sm[:, 8:9], func=AF.Sqrt), reads=sm.B, writes=sm.B)
            S.op("dve", lambda e: e.reciprocal(out=sm[:, 10:11], in_=sm[:, 9:10]), reads=sm.B, writes=sm.B)
            S.op("dve", lambda e: e.scalar_tensor_tensor(out=sm[:, 11:12], in0=sm[:, 5:6], scalar=-1.0, in1=sm[:, 10:11],
                                                         op0=ALU.mult, op1=ALU.mult), reads=sm.B, writes=sm.B)
            S.op("dve", lambda e: e.tensor_scalar(out=hc[:], in0=hc[:], scalar1=sm[:, 10:11], scalar2=sm[:, 11:12],
                                                  op0=ALU.mult, op1=ALU.add), reads=hc.B + sm.B, writes=hc.B)
            S.op("dve", lambda e: e.tensor_tensor(out=hnb[:], in0=hc[:], in1=mhg[:], op=ALU.mult), reads=hc.B + mhg.B, writes=hnb.B)
            S.dma("pool", HNd[tok0:tok0 + LCH, hrows], hnb[:], reads=hnb.B, writes=[Buf()])
        yield
        dcol = dcr[:, cp:cp + 1]
        S.op("act", lambda e: e.mul(out=kw[:], in_=ktc, mul=wcol), reads=ktm.B + cols[3].B, writes=kw.B)
        yield
        pNU = self.next_ps()
        for k in range(4):
            pU = self.next_ps()
            while pU is pNU:
                pU = self.next_ps()
            S.op("pe", lambda e: e.matmul(pU[:], lhsT=kw[:, k * 128:(k + 1) * 128], rhs=vtc, start=True, stop=True),
                 reads=kw.B + vtm.B, writes=pU.B)
            yield
            S.op("dve", lambda e: e.scalar_tensor_tensor(out=C[:, k, :], in0=C[:, k, :], scalar=dcol, in1=pU[:],
                                                         op0=ALU.mult, op1=ALU.add), reads=C.B + dcr.B + pU.B, writes=C.B)
            S.op("act", lambda e: e.mul(out=Cb[:, k, :], in_=C[:, k, :], mul=1.0), reads=C.B, writes=Cb.B)
        yield
        S.op("pe", [(lambda e, k=k: e.matmul(pNU[:, k:k + 1], lhsT=kw[:, k * 128:(k + 1) * 128], rhs=onesb[:], start=True, stop=True))
                    for k in range(4)], reads=kw.B + onesb.B, writes=pNU.B)
        yield
        S.op("dve", lambda e: e.scalar_tensor_tensor(out=nn[:], in0=nn[:], scalar=dcol, in1=pNU[:, 0:4],
                                                     op0=ALU.mult, op1=ALU.add), reads=nn.B + dcr.B + pNU.B, writes=nn.B)
        S.op("act", lambda e: e.mul(out=nb[:], in_=nn[:], mul=1.0), reads=nn.B, writes=nb.B)

    def layer_hyena(self, li, j):
        S = self.S
        w_in = self.convert_weight("hy_in", self.hy_w_in[:], D, 4 * E)
        w_out = self.convert_weight("hy_out", self.hy_w_out[:], E, D)
        ds = self.dram_scratch
        Ud, Gd = ds("hy_U", [SEQ, E], BF16), ds("hy_G", [SEQ, E], BF16)
        Ad, Bd = ds("hy_A", [SEQ, E], BF16), ds("hy_B", [SEQ, E], BF16)
        Krd, Kid = ds("hy_Kr", [SEQ, E], F32), ds("hy_Ki", [SEQ, E], F32)
        YGd = ds("hy_YG", [SEQ, E], BF16)
        S.barrier()
        self.hy_filter(Ad, Bd)
        S.barrier()
        with contextlib.ExitStack() as les:
            self.alloc_commons(les)
            self.ada_mod(li, "lat")
            self.pass1(li, "lat")
            self.hy_proj(les, w_in, Ud, Gd)
            S.barrier()
        self.hy_dft(Ud, Gd, Ad, Bd, Krd, Kid, YGd)
        S.barrier()
        with contextlib.ExitStack() as les:
            self.alloc_commons(les)
            self.ada_mod(li, "lat")
            vT = self.sb("hy_vT", [128, EC, 256], BF16, st=les)
            yg = [self.sb(f"hy_yg{i}", [128, E], BF16, st=les) for i in range(2)]
            for tt in range(SEQ // 256):
                for sub in range(2):
                    y = yg[sub]
                    r0 = tt * 256 + sub * 128
                    S.dma("sp", y[:], YGd[r0:r0 + 128, :], reads=YGd.B, writes=y.B)
                    for g in range(EC // 4):
                        S.op("pe", [(lambda e, q=q: e.transpose(self.psb[:, q * 128:(q + 1) * 128],
                                                                y[:, (g * 4 + q) * 128:(g * 4 + q + 1) * 128], self.ident[:]))
                                    for q in range(4)], reads=y.B + self.ident.B, writes=self.psb.B)
                        S.op("act", lambda e: e.copy(out=vT[:, g * 4:(g + 1) * 4, sub * 128:(sub + 1) * 128],
                                                     in_=self.psb[:, 0:512].rearrange("p (q t) -> p q t", q=4)),
                             reads=self.psb.B, writes=vT.B)
                self.out_proj(li, "lat", vT, w_out, tt * 256, 256)
            S.barrier()

    def hy_filter(self, Ad, Bd):
        S = self.S
        TWO_PI = 2.0 * math.pi
        with contextlib.ExitStack() as st:
            sb = lambda n, sh, dt=F32: self.sb(n, sh, dt, st=st)
            featT, h1, h2 = sb("hf_feat", [33, SEQ]), sb("hf_h1", [64, SEQ]), sb("hf_h2", [65, SEQ])
            w3, w1, w2 = sb("hf_w3", [65, 2 * E]), sb("hf_w1", [33, 64]), sb("hf_w2", [64, 64])
            vec = sb("hf_vec", [64, 8])
            dec, dl, ntl = sb("hf_dec", [128, E]), sb("hf_dl", [128, E]), sb("hf_ntl", [128, 32])
            arg = [sb(f"hf_arg{i}", [64, 512]) for i in range(2)]
            ki = [sb(f"hf_ki{i}", [64, 512], I32) for i in range(2)]
            kf = [sb(f"hf_kf{i}", [64, 512]) for i in range(2)]
            sB = [sb(f"hf_sB{i}", [128, 512]) for i in range(2)]
            t1 = [sb(f"hf_t1{i}", [128, 512]) for i in range(2)]
            oA = [sb(f"hf_oA{i}", [128, 512], BF16) for i in range(2)]
            oB = [sb(f"hf_oB{i}", [128, 512], BF16) for i in range(2)]
            for dst, src in ((featT[:], self.hy_featT), (w1[:], self.hy_w1), (w2[:], self.hy_w2), (vec[:, 0:4], self.hy_fvec),
                             (w3[0:64, :], self.hy_w3), (ntl[:], self.hy_ntl)):
                S.dma("sp", dst, src[:], reads=src.B, writes=[Buf()])
            S.dma("sp", w3[64:65, :], self.hy_b3[:].partition_broadcast(1), reads=self.hy_b3.B, writes=w3.B)
            S.dma("sp", dl[:], self.hy_delta[:].partition_broadcast(128), reads=self.hy_delta.B, writes=dl.B)
            S.barrier()
            S.op("dve", lambda e: e.memset(h2[64:65, :], 1.0), writes=h2.B)
            S.op("dve", lambda e: e.tensor_tensor(out=vec[:, 4:5], in0=vec[:, 0:1], in1=vec[:, 1:2], op=ALU.mult), reads=vec.B, writes=vec.B)
            S.op("dve", lambda e: e.tensor_tensor(out=vec[:, 5:6], in0=vec[:, 2:3], in1=vec[:, 3:4], op=ALU.mult), reads=vec.B, writes=vec.B)

            def sin_layer(w, src, kdim, dst, fcol, fbcol):
                for n in range(8):
                    cols = slice(n * 512, (n + 1) * 512)
                    ps = self.next_ps()
                    S.op("pe", lambda e: e.matmul(ps[0:64, :], lhsT=w[0:kdim, :], rhs=src[0:kdim, cols], start=True, stop=True),
                         reads=w.B + src.B, writes=ps.B)
                    a, k_i, k_f = arg[n % 2], ki[n % 2], kf[n % 2]
                    S.op("dve", lambda e: e.tensor_scalar(out=a[:], in0=ps[0:64, :], scalar1=vec[:, fcol:fcol + 1], scalar2=vec[:, fbcol:fbcol + 1],
                                                          op0=ALU.mult, op1=ALU.add), reads=ps.B + vec.B, writes=a.B)
                    S.op("dve", lambda e: e.tensor_scalar(out=k_f[:], in0=a[:], scalar1=1.0 / TWO_PI, scalar2=None, op0=ALU.mult), reads=a.B, writes=k_f.B)
                    S.op("dve", lambda e: e.tensor_copy(out=k_i[:], in_=k_f[:]), reads=k_f.B, writes=k_i.B)
                    S.op("dve", lambda e: e.tensor_copy(out=k_f[:], in_=k_i[:]), reads=k_i.B, writes=k_f.B)
                    S.op("dve", lambda e: e.scalar_tensor_tensor(out=a[:], in0=k_f[:], scalar=-TWO_PI, in1=a[:], op0=ALU.mult, op1=ALU.add),
                         reads=k_f.B + a.B, writes=a.B)
                    S.op("dve", lambda e: e.tensor_scalar(out=a[:], in0=a[:], scalar1=-3.141592, scalar2=3.141592, op0=ALU.max, op1=ALU.min),
                         reads=a.B, writes=a.B)
                    S.op("act", lambda e: e.activation(out=dst[0:64, cols], in_=a[:], func=AF.Sin), reads=a.B, writes=dst.B)

            sin_layer(w1, featT, 33, h1, 1, 4)
            sin_layer(w2, h1, 64, h2, 3, 5)
            for tc in range(32):
                S.op("act", lambda e: e.activation(out=dec[:], in_=dl[:], func=AF.Exp, scale=ntl[:, tc:tc + 1]), reads=dl.B + ntl.B, writes=dec.B)
                for n in range(8):
                    cols = slice(n * 512, (n + 1) * 512)
                    psF, psB = self.next_ps(), self.next_ps()
                    S.op("pe", lambda e: e.matmul(psF[:], lhsT=h2[0:65, tc * 128:(tc + 1) * 128], rhs=w3[0:65, cols], start=True, stop=True),
                         reads=h2.B + w3.B, writes=psF.B)
                    S.op("pe", lambda e: e.matmul(psB[:], lhsT=h2[0:65, tc * 128:(tc + 1) * 128], rhs=w3[0:65, E + n * 512:E + (n + 1) * 512],
                                                  start=True, stop=True), reads=h2.B + w3.B, writes=psB.B)
                    b_, t_, a_, o_ = sB[n % 2], t1[n % 2], oA[n % 2], oB[n % 2]
                    S.op("act", lambda e: e.copy(out=b_[:], in_=psB[:]), reads=psB.B, writes=b_.B)
                    if tc == 0:
                        S.op("dve", lambda e: e.memset(b_[0:1, :], 0.0), writes=b_.B)
                    S.op("dve", lambda e: e.tensor_tensor(out=t_[:], in0=psF[:], in1=b_[:], op=ALU.add), reads=psF.B + b_.B, writes=t_.B)
                    S.op("dve", lambda e: e.tensor_tensor(out=a_[:], in0=t_[:], in1=dec[:, cols], op=ALU.mult), reads=t_.B + dec.B, writes=a_.B)
                    S.op("dve", lambda e: e.tensor_tensor(out=t_[:], in0=psF[:], in1=b_[:], op=ALU.subtract), reads=psF.B + b_.B, writes=t_.B)
                    S.op("dve", lambda e: e.tensor_tensor(out=o_[:], in0=t_[:], in1=dec[:, cols], op=ALU.mult), reads=t_.B + dec.B, writes=o_.B)
                    S.dma("pool", Ad[tc * 128:(tc + 1) * 128, cols], a_[:], reads=a_.B, writes=Ad.B)
                    S.dma("pool", Bd[tc * 128:(tc + 1) * 128, cols], o_[:], reads=o_.B, writes=Bd.B)
            S.barrier()

    def hy_proj(self, les, w_in, Ud, Gd):
        S = self.S
        TT = 256
        sb = lambda n, sh, dt=F32: self.sb(n, sh, dt, st=les)
        uT = sb("hp_uT", [128, KC, TT + 2], BF16)
        Utm, Gtm = sb("hp_U", [128, 2, E], BF16), sb("hp_G", [128, 2, E], BF16)
        cva = [sb(f"hp_a{i}", [128, TT]) for i in range(3)]
        zs = sb("hp_zs", [128, TT])
        uu, gg = sb("hp_uu", [128, TT], BF16), sb("hp_gg", [128, TT], BF16)
        cw, cb = sb("hp_cw", [128, 96, 3]), sb("hp_cb", [128, 96])
        S.dma("sp", cw[:], self.hy_cw[:], reads=self.hy_cw.B, writes=cw.B)
        S.dma("sp", cb[:], self.hy_cb[:], reads=self.hy_cb.B, writes=cb.B)
        utv = self.UT[:].rearrange("(k p) t -> p k t", p=128)
        wv = w_in[:].rearrange("(k p) n -> p k n", p=128)
        for tt in range(SEQ // TT):
            t0 = tt * TT
            S.dma("sp", uT[:], utv[:, :, UT_LAT0 + t0 - 1: UT_LAT0 + t0 + TT + 1], reads=self.UT.B, writes=uT.B)
            for g in range(EC // 4):
                wts = []
                for part in range(4):
                    w = self.next_w()
                    wt = w[:].rearrange("p (k n) -> p k n", k=16)
                    S.dma("sp", wt, wv[:, :, part * E + g * 512: part * E + (g + 1) * 512], reads=w_in.B, writes=w.B)
                    wts.append((w, wt))
                for q in range(4):
                    ec = g * 4 + q
                    pss = []
                    for part in range(4):
                        ps = self.next_ps()
                        wt = wts[part][1]
                        S.op("pe", [(lambda e, k=k, ps=ps, wt=wt: e.matmul(ps[:, 0:TT + 2], lhsT=wt[:, k, q * 128:(q + 1) * 128], rhs=uT[:, k, :],
                                                                           start=(k == 0), stop=(k == KC - 1))) for k in range(KC)],
                             reads=uT.B + wts[part][0].B, writes=ps.B)
                        pss.append(ps)
                    for part in range(3):
                        a, ps, ci = cva[part], pss[part], part * 32 + ec
                        S.op("dve", lambda e: e.tensor_scalar(out=a[:], in0=ps[:, 1:TT + 1], scalar1=cw[:, ci, 1:2], scalar2=cb[:, ci:ci + 1],
                                                              op0=ALU.mult, op1=ALU.add), reads=ps.B + cw.B + cb.B, writes=a.B)
                        S.op("dve", lambda e: e.scalar_tensor_tensor(out=a[:], in0=ps[:, 0:TT], scalar=cw[:, ci, 0:1], in1=a[:],
                                                                     op0=ALU.mult, op1=ALU.add), reads=ps.B + cw.B + a.B, writes=a.B)
                        S.op("dve", lambda e: e.scalar_tensor_tensor(out=a[:], in0=ps[:, 2:TT + 2], scalar=cw[:, ci, 2:3], in1=a[:],
                                                                     op0=ALU.mult, op1=ALU.add), reads=ps.B + cw.B + a.B, writes=a.B)
                    S.op("act", lambda e: e.activation(out=zs[:], in_=pss[3][:, 1:TT + 1], func=AF.Silu), reads=pss[3].B, writes=zs.B)
                    S.op("dve", lambda e: e.tensor_tensor(out=uu[:], in0=cva[1][:], in1=cva[2][:], op=ALU.mult),
                         reads=cva[1].B + cva[2].B, writes=uu.B)
                    S.op("pool", lambda e: e.tensor_tensor(out=gg[:], in0=cva[0][:], in1=zs[:], op=ALU.mult),
                         reads=cva[0].B + zs.B, writes=gg.B)
                    S.op("pe", [(lambda e, q2=q2: e.transpose(self.psb[:, q2 * 128:(q2 + 1) * 128],
                                                              (uu if q2 < 2 else gg)[:, (q2 % 2) * 128:(q2 % 2 + 1) * 128], self.ident[:]))
                                for q2 in range(4)], reads=uu.B + gg.B + self.ident.B, writes=self.psb.B)
                    S.op("act", lambda e: e.copy(out=Utm[:, :, ec * 128:(ec + 1) * 128],
                                                 in_=self.psb[:, 0:256].rearrange("p (s t) -> p s t", s=2)), reads=self.psb.B, writes=Utm.B)
                    S.op("act", lambda e: e.copy(out=Gtm[:, :, ec * 128:(ec + 1) * 128],
                                                 in_=self.psb[:, 256:512].rearrange("p (s t) -> p s t", s=2)), reads=self.psb.B, writes=Gtm.B)
            for sub in range(2):
                r0 = t0 + sub * 128
                S.dma("pool", Ud[r0:r0 + 128, :], Utm[:, sub, :], reads=Utm.B, writes=Ud.B)
                S.dma("pool", Gd[r0:r0 + 128, :], Gtm[:, sub, :], reads=Gtm.B, writes=Gd.B)

    def hy_dft(self, Ud, Gd, Ad, Bd, Krd, Kid, YGd):
        S = self.S
        NF = 8192.0
        with contextlib.ExitStack() as st:
            sb = lambda n, sh, dt=F32: self.sb(n, sh, dt, st=st)
            Ut, Bt = sb("hd_U", [128, 32, 512], BF16), sb("hd_B", [128, 32, 512], BF16)
            Y = sb("hd_Y", [128, 64, 512], BF16)
            Cb = [sb(f"hd_C{i}", [128, 32, 128], BF16) for i in range(2)]
            Sb = [sb(f"hd_S{i}", [128, 32, 128], BF16) for i in range(2)]
            Kr = [sb(f"hd_Kr{i}", [128, 512]) for i in range(2)]
            Ki = [sb(f"hd_Ki{i}", [128, 512]) for i in range(2)]
            tm = [sb(f"hd_t{i}", [128, 512]) for i in range(4)]
            gt = [sb(f"hd_g{i}", [128, 512], BF16) for i in range(2)]
            og = [sb(f"hd_o{i}", [128, 512], BF16) for i in range(2)]
            hb = sb("hd_hb", [128, 512])
            tabC = self.hy_C[:].rearrange("(c p) k -> p c k", p=128)
            tabS = self.hy_S[:].rearrange("(c p) k -> p c k", p=128)
            tabCT = self.hy_CT[:].rearrange("(c p) k -> p c k", p=128)
            tabST = self.hy_ST[:].rearrange("(c p) k -> p c k", p=128)
            view = lambda d, n: d[:, n * 512:(n + 1) * 512].rearrange("(c p) e -> p c e", p=128)
            for n in range(8):
                cols = slice(n * 512, (n + 1) * 512)
                S.dma("sp", Ut[:], view(Ad, n), reads=Ad.B, writes=Ut.B)
                S.dma("sp", Bt[:], view(Bd, n), reads=Bd.B, writes=Bt.B)
                for j in range(32):
                    cb_, sb_ = Cb[j % 2], Sb[j % 2]
                    S.dma("sp", cb_[:], tabC[:, :, j * 128:(j + 1) * 128], reads=self.hy_C.B, writes=cb_.B)
                    S.dma("sp", sb_[:], tabS[:, :, j * 128:(j + 1) * 128], reads=self.hy_S.B, writes=sb_.B)
                    psR, psI = self.next_ps(), self.next_ps()
                    S.op("pe", [(lambda e, c=c: e.matmul(psR[:], lhsT=cb_[:, c, :], rhs=Ut[:, c, :], start=(c == 0), stop=(c == 31)))
                                for c in range(32)], reads=cb_.B + Ut.B, writes=psR.B)
                    S.op("pe", [(lambda e, c=c: e.matmul(psI[:], lhsT=sb_[:, c, :], rhs=Bt[:, c, :], start=(c == 0), stop=(c == 31)))
                                for c in range(32)], reads=sb_.B + Bt.B, writes=psI.B)
                    kr, ki_ = Kr[j % 2], Ki[j % 2]
                    S.op("act", lambda e: e.copy(out=kr[:], in_=psR[:]), reads=psR.B, writes=kr.B)
                    S.op("dve", lambda e: e.tensor_copy(out=ki_[:], in_=psI[:]), reads=psI.B, writes=ki_.B)
                    S.dma("pool", Krd[j * 128:(j + 1) * 128, cols], kr[:], reads=kr.B, writes=Krd.B)
                    S.dma("pool", Kid[j * 128:(j + 1) * 128, cols], ki_[:], reads=ki_.B, writes=Kid.B)
            S.barrier()
            for n in range(8):
                cols = slice(n * 512, (n + 1) * 512)
                S.dma("sp", Ut[:], view(Ud, n), reads=Ud.B, writes=Ut.B)
                S.dma("sp", hb[:], self.hy_hb[n * 512:(n + 1) * 512].partition_broadcast(128), reads=self.hy_hb.B, writes=hb.B)
                for j in range(32):
                    cb_, sb_ = Cb[j % 2], Sb[j % 2]
                    S.dma("sp", cb_[:], tabC[:, :, j * 128:(j + 1) * 128], reads=self.hy_C.B, writes=cb_.B)
                    S.dma("sp", sb_[:], tabS[:, :, j * 128:(j + 1) * 128], reads=self.hy_S.B, writes=sb_.B)
                    kr, ki_ = Kr[j % 2], Ki[j % 2]
                    S.dma("sp", kr[:], Krd[j * 128:(j + 1) * 128, cols], reads=Krd.B, writes=kr.B)
                    S.dma("sp", ki_[:], Kid[j * 128:(j + 1) * 128, cols], reads=Kid.B, writes=ki_.B)
                    psR, psI = self.next_ps(), self.next_ps()
                    S.op("pe", [(lambda e, c=c: e.matmul(psR[:], lhsT=cb_[:, c, :], rhs=Ut[:, c, :], start=(c == 0), stop=(c == 31)))
                                for c in range(32)], reads=cb_.B + Ut.B, writes=psR.B)
                    S.op("pe", [(lambda e, c=c: e.matmul(psI[:], lhsT=sb_[:, c, :], rhs=Ut[:, c, :], start=(c == 0), stop=(c == 31)))
                                for c in range(32)], reads=sb_.B + Ut.B, writes=psI.B)
                    S.op("dve", lambda e: e.tensor_tensor(out=tm[0][:], in0=psR[:], in1=kr[:], op=ALU.mult), reads=psR.B + kr.B, writes=tm[0].B)
                    S.op("dve", lambda e: e.tensor_tensor(out=tm[1][:], in0=psI[:], in1=ki_[:], op=ALU.mult), reads=psI.B + ki_.B, writes=tm[1].B)
                    S.op("pool", lambda e: e.tensor_tensor(out=Y[:, j, :], in0=tm[0][:], in1=tm[1][:], op=ALU.subtract),
                         reads=tm[0].B + tm[1].B, writes=Y.B)
                    S.op("dve", lambda e: e.tensor_tensor(out=tm[2][:], in0=psR[:], in1=ki_[:], op=ALU.mult), reads=psR.B + ki_.B, writes=tm[2].B)
                    S.op("dve", lambda e: e.tensor_tensor(out=tm[3][:], in0=psI[:], in1=kr[:], op=ALU.mult), reads=psI.B + kr.B, writes=tm[3].B)
                    S.op("pool", lambda e: e.tensor_tensor(out=Y[:, 32 + j, :], in0=tm[2][:], in1=tm[3][:], op=ALU.add),
                         reads=tm[2].B + tm[3].B, writes=Y.B)
                for tc in range(32):
                    cb_, sb_ = Cb[tc % 2], Sb[tc % 2]
                    S.dma("sp", cb_[:], tabCT[:, :, tc * 128:(tc + 1) * 128], reads=self.hy_CT.B, writes=cb_.B)
                    S.dma("sp", sb_[:], tabST[:, :, tc * 128:(tc + 1) * 128], reads=self.hy_ST.B, writes=sb_.B)
                    g_, o_ = gt[tc % 2], og[tc % 2]
                    S.dma("sp", g_[:], Gd[tc * 128:(tc + 1) * 128, cols], reads=Gd.B, writes=g_.B)
                    ps = self.next_ps()
                    S.op("pe", [(lambda e, c=c: e.matmul(ps[:], lhsT=(cb_ if c < 32 else sb_)[:, c % 32, :], rhs=Y[:, c, :],
                                                         start=(c == 0), stop=(c == 63))) for c in range(64)],
                         reads=cb_.B + sb_.B + Y.B, writes=ps.B)
                    ta, tb = tm[tc % 2], tm[2 + tc % 2]
                    S.op("dve", lambda e: e.tensor_tensor(out=ta[:], in0=Ut[:, tc, :], in1=hb[:], op=ALU.mult), reads=Ut.B + hb.B, writes=ta.B)
                    S.op("dve", lambda e: e.scalar_tensor_tensor(out=tb[:], in0=ps[:], scalar=2.0 / NF, in1=ta[:], op0=ALU.mult, op1=ALU.add),
                         reads=ps.B + ta.B, writes=tb.B)
                    S.op("pool", lambda e: e.tensor_tensor(out=o_[:], in0=tb[:], in1=g_[:], op=ALU.mult), reads=tb.B + g_.B, writes=o_.B)
                    S.dma("pool", YGd[tc * 128:(tc + 1) * 128, cols], o_[:], reads=o_.B, writes=YGd.B)
            S.barrier()


def _pk(v):
    return np.ascontiguousarray(v.reshape(-1, 128).T)


def make_inputs(b, inp):
    f = np.float32
    m = {
        "x": np.ascontiguousarray(inp["x"][b]), "ctx": np.ascontiguousarray(inp["ctx"][b]),
        "c_pk": _pk(inp["c"][b]), "cc_pk": _pk(inp["c_ctx"]),
        "norm_g": inp["norm_g"], "ada_w": inp["ada_w"], "ada_b": inp["ada_b"], "final_g": inp["final_g"],
        "cv_w_in": inp["cv_w_in"],
        "cv_dw": np.ascontiguousarray(inp["cv_dw_w"].reshape(2, CONVW, EC, 128).transpose(0, 3, 2, 1)),
        "cv_dwb": np.ascontiguousarray(inp["cv_dw_b"].reshape(2, EC, 128).transpose(0, 2, 1)),
        "cv_lng": np.ascontiguousarray(inp["cv_ln_g"].reshape(2, EC, 128).transpose(0, 2, 1)),
        "cv_lnb": np.ascontiguousarray(inp["cv_ln_b"].reshape(2, EC, 128).transpose(0, 2, 1)),
        "cv_w_out": inp["cv_w_out"],
        "ident": np.eye(128, dtype=f),
    }
    m.update(_hyena_consts())
    m.update(_mlstm_inputs(inp))
    m.update({
        "hy_w_in": inp["hy_w_in"][0], "hy_w_out": inp["hy_w_out"][0],
        "hy_cw": np.ascontiguousarray(inp["hy_conv_w"][0].reshape(3, 96, 128).transpose(2, 1, 0)),
        "hy_cb": np.ascontiguousarray(inp["hy_conv_b"][0].reshape(96, 128).T),
        "hy_fvec": np.ascontiguousarray(np.stack([inp["hy_f_b1"][0], inp["hy_f_freq1"][0], inp["hy_f_b2"][0], inp["hy_f_freq2"][0]], axis=1)),
        "hy_w1": inp["hy_f_w1"][0], "hy_w2": inp["hy_f_w2"][0], "hy_w3": inp["hy_f_w3"][0], "hy_b3": inp["hy_f_b3"][0],
        "hy_hb": inp["hy_h_bias"][0],
    })
    return m


def _mlstm_inputs(inp):
    f = np.float32
    bd = np.zeros((4, EC, 128, 128), f)
    for a, key in enumerate(("ml_w_q", "ml_w_k", "ml_w_v", "ml_w_o")):
        w = inp[key][0].reshape(EC, 32, 4, 4)
        for g in range(32):
            bd[a, :, 4 * g:4 * g + 4, 4 * g:4 * g + 4] = w[:, g]
    wg = inp["ml_w_gates"][0].reshape(96, 128, 2, 2, 8)
    wg = np.ascontiguousarray(wg.transpose(1, 2, 3, 0, 4)).reshape(128, 4 * 96 * 8)
    bg = np.ascontiguousarray(inp["ml_b_gates"][0].reshape(4, 8).T)
    pk = lambda v: np.ascontiguousarray(v.reshape(EC, 128).T)
    s_, t_ = np.meshgrid(np.arange(64), np.arange(64), indexing="ij")
    mask = np.stack([(s_ <= t_), (s_ >= t_)]).astype(f)
    return {
        "ml_w_in": inp["ml_w_in"][0], "ml_w_out": inp["ml_w_out"][0], "ml_bd": bd.reshape(4 * EC * 128, 128), "ml_wg": wg,
        "ml_cw": np.ascontiguousarray(inp["ml_conv_w"][0].reshape(3, EC, 128).transpose(2, 1, 0)),
        "ml_vec": np.ascontiguousarray(np.stack([pk(inp["ml_conv_b"][0]), pk(inp["ml_b_o"][0]), pk(inp["ml_skip"][0])], axis=1)),
        "ml_bg": bg, "ml_mhg": inp["ml_mh_g"][0], "ml_mask": mask,
    }


_HC = {}


def _hyena_consts():
    if _HC:
        return _HC
    f = np.float32
    L = SEQ
    t = np.linspace(0.0, 1.0, L, dtype=f)[:, None]
    bands = 16
    ang = (f(2.0 * math.pi) * np.arange(L, dtype=f)[:, None] / f(L)).astype(f)
    fr = np.linspace(1e-4, bands - 1, bands, dtype=f)[None, :]
    feat = np.concatenate([t, np.cos(fr * ang), -np.sin(fr * ang)], axis=-1).astype(f)
    lo = math.log(1e-2) / 0.3
    hi = math.log(1e-2) / 1.5
    deltas = np.abs(np.linspace(lo, hi, E, dtype=f)).astype(f)
    tl = t[:, 0]
    _HC["hy_featT"] = np.ascontiguousarray(feat.T)
    _HC["hy_delta"] = deltas
    _HC["hy_ntl"] = np.ascontiguousarray((-tl).reshape(32, 128).T)
    n = np.arange(SEQ, dtype=np.int64)
    ph = (np.outer(n, 2 * n + 1) % 16384).astype(np.float64) * (2.0 * math.pi / 16384.0)
    C = np.cos(ph).astype(ml_dtypes.bfloat16)
    Sn = np.sin(ph).astype(ml_dtypes.bfloat16)
    _HC["hy_C"] = C
    _HC["hy_S"] = Sn
    _HC["hy_CT"] = np.ascontiguousarray(C.T)
    _HC["hy_ST"] = np.ascontiguousarray(Sn.T)
    return _HC


_PROG = {}
ML_STOP = 9


def run(inputs, layers=(0, 1, 2, 3), final=True, cores=8):
    inp = {k: np.asarray(v, dtype=np.float32) for k, v in inputs.items()}
    key = (tuple(layers), final)
    p = Prog(layers, final)
    p.ml_stop = ML_STOP
    nc = p.build()
    in_maps = []
    for b in range(cores):
        m = make_inputs(b, inp)
        in_maps.append({k: m[k] for k in p.in_names})
    res = run_bass_kernel_spmd(nc, in_maps, core_ids=list(range(cores)))
    return np.stack([res.results[b]["out"] for b in range(cores)], axis=0)


def kernel(**inputs):
    return run(inputs).astype(np.float32)
```
